# Optimizing a Trainium2 kernel written in Bass

```python
import math
import jax
import jax.numpy as jnp
from jax import lax
import numpy as np

D_MODEL = 1024
BATCH = 32
SEQ = 2048
DEPTH = 4

GRID_W = 64
CTX_LEN = 256
N_MIXERS = 3
EPS = 1e-6
S5_GROUP = 16
S5_GROUPS = D_MODEL // S5_GROUP
S5_STATE = 64
ML_HEADS = 4
ML_HEAD_DIM = D_MODEL // ML_HEADS
ML_CHUNK = 64
ML_CONV = 3
NA_HEADS = 16
NA_HEAD_DIM = D_MODEL // NA_HEADS
NA_WIN_ROWS = 8
NA_WIN_COLS = 16
ROPE_BASE = 10000.0
D_FF = 128 * ((8 * D_MODEL // 3 + 127) // 128)
FFN_CONV = 3
F32 = jnp.float32

kernel_name = 'hybrid_s5_mlstm_natten_flow_block'


def _rmsnorm(x, g):
    xf = x.astype(F32)
    y = xf * lax.rsqrt(jnp.mean(xf * xf, axis=-1, keepdims=True) + EPS)
    return (y * g.astype(F32)).astype(x.dtype)


def _modulation(cvec, w, b):
    return jnp.split(jax.nn.silu(cvec) @ w + b, 6, axis=-1)


def _dwconv(x, w):
    k = w.shape[0]
    return lax.conv_general_dilated(
        x, w[:, None, :].astype(x.dtype), window_strides=(1,),
        padding=[(k // 2, k // 2)], dimension_numbers=('NWC', 'WIO', 'NWC'),
        feature_group_count=x.shape[-1])


def _conv_ffn(h, w_in, conv_w, w_out):
    u = _dwconv(h @ w_in, conv_w)
    a, g = jnp.split(u, 2, axis=-1)
    return (a * jax.nn.silu(g)) @ w_out


def _rope_axis(x, pos):
    nf = x.shape[-1] // 2
    inv = ROPE_BASE ** (-jnp.arange(nf, dtype=F32) / nf)
    ang = pos.astype(F32)[:, None] * inv[None, :]
    cos, sin = jnp.cos(ang), jnp.sin(ang)
    x1, x2 = x[..., :nf], x[..., nf:]
    return jnp.concatenate([x1 * cos - x2 * sin, x1 * sin + x2 * cos], axis=-1)


def _rope_2d(x):
    t = jnp.arange(x.shape[2], dtype=jnp.int32)
    half = x.shape[-1] // 2
    return jnp.concatenate([_rope_axis(x[..., :half], t // GRID_W),
                            _rope_axis(x[..., half:], t % GRID_W)], axis=-1)


def _s5_discretise(lam_re, lam_im, log_dt, b_re, b_im):
    lam_re, lam_im = lam_re.astype(F32), lam_im.astype(F32)
    b_re, b_im = b_re.astype(F32), b_im.astype(F32)
    dt = jnp.exp(log_dt.astype(F32))[:, None]
    lr = jnp.minimum(lam_re, -1e-4)
    er = jnp.exp(lr * dt)
    ang = lam_im * dt
    a_re, a_im = er * jnp.cos(ang), er * jnp.sin(ang)
    den = lr * lr + lam_im * lam_im
    nr = a_re - 1.0
    q_re = (nr * lr + a_im * lam_im) / den
    q_im = (a_im * lr - nr * lam_im) / den
    bb_re = q_re[..., None] * b_re - q_im[..., None] * b_im
    bb_im = q_re[..., None] * b_im + q_im[..., None] * b_re
    return a_re, a_im, bb_re, bb_im


def _cmul_combine(e1, e2):
    a1r, a1i, b1r, b1i = e1
    a2r, a2i, b2r, b2i = e2
    return (a2r * a1r - a2i * a1i, a2r * a1i + a2i * a1r,
            a2r * b1r - a2i * b1i + b2r, a2r * b1i + a2i * b1r + b2i)


def _s5_states(u, a_re, a_im, bb_re, bb_im, h0):
    bu_re = jnp.einsum('lbgh,gph->lbgp', u, bb_re)
    bu_im = jnp.einsum('lbgh,gph->lbgp', u, bb_im)
    if h0 is not None:
        h_re, h_im = h0
        bu_re = bu_re.at[0].add(a_re * h_re - a_im * h_im)
        bu_im = bu_im.at[0].add(a_re * h_im + a_im * h_re)
    shape = (u.shape[0], 1) + a_re.shape
    ar, ai = jnp.broadcast_to(a_re, shape), jnp.broadcast_to(a_im, shape)
    _, _, x_re, x_im = lax.associative_scan(_cmul_combine, (ar, ai, bu_re, bu_im), axis=0)
    return x_re, x_im


def _s5_readout(x_re, x_im, c_re, c_im):
    return (jnp.einsum('lbgp,ghp->lbgh', x_re, c_re.astype(F32))
            - jnp.einsum('lbgp,ghp->lbgh', x_im, c_im.astype(F32)))


def _s5_mixer(h_ctx, h_lat, lam_re, lam_im, log_dt, b_re, b_im, c_re, c_im,
              d_skip, glu_w, glu_b, need_ctx):
    def to_groups(h):
        return jnp.swapaxes(h.astype(F32), 0, 1).reshape(h.shape[1], h.shape[0], S5_GROUPS, S5_GROUP)

    def finish(y, u, dtype):
        y = y + d_skip.astype(F32).reshape(S5_GROUPS, S5_GROUP) * u
        y = jax.nn.gelu(jnp.swapaxes(y.reshape(u.shape[0], u.shape[1], D_MODEL), 0, 1))
        a, g = jnp.split(y @ glu_w.astype(F32) + glu_b.astype(F32), 2, axis=-1)
        return (a * jax.nn.sigmoid(g)).astype(dtype)

    u_ctx, u_lat = to_groups(h_ctx), to_groups(h_lat)
    y_lat, y_ctx = 0.0, 0.0
    for dirn in range(2):
        a_re, a_im, bb_re, bb_im = _s5_discretise(lam_re[dirn], lam_im[dirn], log_dt[dirn],
                                                  b_re[dirn], b_im[dirn])
        rev = (lambda t: t) if dirn == 0 else (lambda t: t[::-1])
        xc_re, xc_im = _s5_states(rev(u_ctx), a_re, a_im, bb_re, bb_im, None)
        xl_re, xl_im = _s5_states(rev(u_lat), a_re, a_im, bb_re, bb_im, (xc_re[-1], xc_im[-1]))
        y_lat = y_lat + rev(_s5_readout(xl_re, xl_im, c_re[dirn], c_im[dirn]))
        if need_ctx:
            y_ctx = y_ctx + rev(_s5_readout(xc_re, xc_im, c_re[dirn], c_im[dirn]))
    out_lat = finish(y_lat, u_lat, h_lat.dtype)
    out_ctx = finish(y_ctx, u_ctx, h_ctx.dtype) if need_ctx else None
    return out_ctx, out_lat


def _mlstm_chunk_scan(q, k, v, i_pre, logf, state):
    bsz, nh, seq, dh = q.shape
    nc = seq // ML_CHUNK

    def chunks(t):
        return jnp.moveaxis(t.reshape((bsz, nh, nc, ML_CHUNK) + t.shape[3:]), 2, 0)

    tril = jnp.tril(jnp.ones((ML_CHUNK, ML_CHUNK), dtype=bool))

    def step(carry, inp):
        c_mat, n_vec, m = carry
        qc, kc, vc, ic, fc = inp
        b = jnp.cumsum(fc, axis=-1)
        log_d = jnp.where(tril, b[..., :, None] - b[..., None, :] + ic[..., None, :], -jnp.inf)
        log_inter = b + m[..., None]
        m_t = jnp.maximum(log_inter, jnp.max(log_d, axis=-1))
        w_intra = jnp.einsum('bhtd,bhsd->bhts', qc, kc) * jnp.exp(log_d - m_t[..., None])
        w_inter = jnp.exp(log_inter - m_t)
        num = (jnp.einsum('bhts,bhsv->bhtv', w_intra, vc)
               + w_inter[..., None] * jnp.einsum('bhvk,bhtk->bhtv', c_mat, qc))
        den = jnp.sum(w_intra, axis=-1) + w_inter * jnp.einsum('bhk,bhtk->bht', n_vec, qc)
        h = num / jnp.maximum(jnp.abs(den), jnp.exp(-m_t))[..., None]
        b_end = b[..., -1]
        log_w = b_end[..., None] - b + ic
        m_new = jnp.maximum(b_end + m, jnp.max(log_w, axis=-1))
        w = jnp.exp(log_w - m_new[..., None])
        decay = jnp.exp(b_end + m - m_new)
        c_new = decay[..., None, None] * c_mat + jnp.einsum('bhs,bhsv,bhsk->bhvk', w, vc, kc)
        n_new = decay[..., None] * n_vec + jnp.einsum('bhs,bhsk->bhk', w, kc)
        return (c_new, n_new, m_new), h

    state, hs = lax.scan(step, state, (chunks(q), chunks(k), chunks(v), chunks(i_pre), chunks(logf)))
    return jnp.moveaxis(hs, 0, 2).reshape(bsz, nh, seq, dh), state


def _mlstm_mixer(h_ctx, h_lat, w_in, conv_w, w_if, b_if, norm_g, w_out, need_ctx):
    D, H, dh = D_MODEL, ML_HEADS, ML_HEAD_DIM

    def heads(t):
        return t.reshape(t.shape[0], t.shape[1], H, dh).transpose(0, 2, 1, 3)

    def prep(h, rotate):
        hf = h.astype(F32)
        proj = hf @ w_in.astype(F32)
        qk = jax.nn.silu(_dwconv(proj[..., :2 * D], conv_w.astype(F32)))
        q, k = heads(qk[..., :D]), heads(qk[..., D:])
        if rotate:
            q, k = _rope_2d(q), _rope_2d(k)
        return hf, q * dh ** -0.5, k, heads(proj[..., 2 * D:3 * D]), jax.nn.sigmoid(proj[..., 3 * D:])

    def finish(h_sum, o, dtype):
        hn = h_sum * lax.rsqrt(jnp.mean(h_sum * h_sum, axis=-1, keepdims=True) + EPS)
        hn = hn.transpose(0, 2, 1, 3).reshape(o.shape) * norm_g.astype(F32)
        return ((o * hn) @ w_out.astype(F32)).astype(dtype)

    hc, qc, kc, vc, oc = prep(h_ctx, False)
    hl, ql, kl, vl, ol = prep(h_lat, True)
    bsz = h_lat.shape[0]
    h_lat_sum, h_ctx_sum = 0.0, 0.0
    for dirn in range(2):
        wg, bg = w_if[dirn].astype(F32), b_if[dirn].astype(F32)

        def gates(hf):
            g = hf @ wg + bg
            return (jnp.swapaxes(g[..., :H], 1, 2),
                    jnp.swapaxes(jax.nn.log_sigmoid(g[..., H:]), 1, 2))

        rev = (lambda t: t) if dirn == 0 else (lambda t: jnp.flip(t, axis=2))
        ic, fc = gates(hc)
        il, fl = gates(hl)
        state0 = (jnp.zeros((bsz, H, dh, dh), F32), jnp.zeros((bsz, H, dh), F32),
                  jnp.zeros((bsz, H), F32))
        out_c, state_c = _mlstm_chunk_scan(rev(qc), rev(kc), rev(vc), rev(ic), rev(fc), state0)
        out_l, _ = _mlstm_chunk_scan(rev(ql), rev(kl), rev(vl), rev(il), rev(fl), state_c)
        h_lat_sum = h_lat_sum + rev(out_l)
        if need_ctx:
            h_ctx_sum = h_ctx_sum + rev(out_c)
    out_lat = finish(h_lat_sum, ol, h_lat.dtype)
    out_ctx = finish(h_ctx_sum, oc, h_ctx.dtype) if need_ctx else None
    return out_ctx, out_lat


def _na_mixer(h_ctx, h_lat, w_qkv, rpb, w_out, need_ctx):
    H, dh = NA_HEADS, NA_HEAD_DIM
    bsz, seq, _ = h_lat.shape
    rows = seq // GRID_W
    kr = min(NA_WIN_ROWS, rows)
    n_loc = kr * GRID_W
    scale = dh ** -0.5
    rpb = rpb.astype(F32)

    def heads(h):
        qkv = (h.astype(F32) @ w_qkv.astype(F32)).reshape(h.shape[0], h.shape[1], 3, H, dh)
        q, k, v = [qkv[:, :, j].transpose(0, 2, 1, 3) for j in range(3)]
        return q * scale, k, v

    q_c, k_c, v_c = heads(h_ctx)
    q_l, k_l, v_l = heads(h_lat)
    k_grid = k_l.reshape(bsz, H, rows, GRID_W, dh)
    v_grid = v_l.reshape(bsz, H, rows, GRID_W, dh)
    q_rows = jnp.moveaxis(q_l.reshape(bsz, H, rows, GRID_W, dh), 2, 0)
    cols = jnp.arange(GRID_W, dtype=jnp.int32)
    c0 = jnp.clip(cols - NA_WIN_COLS // 2, 0, GRID_W - NA_WIN_COLS)
    col_mask = (cols[None, :] >= c0[:, None]) & (cols[None, :] < c0[:, None] + NA_WIN_COLS)
    dc_idx = jnp.clip(cols[None, :] - cols[:, None], 1 - NA_WIN_COLS, NA_WIN_COLS - 1) + NA_WIN_COLS - 1

    def row_block(args):
        r, q_r = args
        r0 = jnp.clip(r - kr // 2, 0, rows - kr)
        k_b = lax.dynamic_slice_in_dim(k_grid, r0, kr, axis=2)
        v_b = lax.dynamic_slice_in_dim(v_grid, r0, kr, axis=2)
        dr_idx = r0 + jnp.arange(kr, dtype=jnp.int32) - r + NA_WIN_ROWS - 1
        bias = rpb[:, dr_idx[None, :, None], dc_idx[:, None, :]]
        s_loc = jnp.einsum('bhqd,bhrkd->bhqrk', q_r, k_b) + bias
        s_loc = jnp.where(col_mask[:, None, :], s_loc, -jnp.inf).reshape(bsz, H, GRID_W, n_loc)
        s_ctx = jnp.einsum('bhqd,bhcd->bhqc', q_r, k_c)
        p = jax.nn.softmax(jnp.concatenate([s_loc, s_ctx], axis=-1), axis=-1)
        return (jnp.einsum('bhqn,bhnd->bhqd', p[..., :n_loc], v_b.reshape(bsz, H, n_loc, dh))
                + jnp.einsum('bhqc,bhcd->bhqd', p[..., n_loc:], v_c))

    o_rows = lax.map(row_block, (jnp.arange(rows, dtype=jnp.int32), q_rows))
    o_lat = o_rows.transpose(1, 0, 3, 2, 4).reshape(bsz, seq, D_MODEL)
    out_lat = (o_lat @ w_out.astype(F32)).astype(h_lat.dtype)
    out_ctx = None
    if need_ctx:
        p_c = jax.nn.softmax(jnp.einsum('bhqd,bhkd->bhqk', q_c, k_c), axis=-1)
        o_ctx = jnp.einsum('bhqk,bhkd->bhqd', p_c, v_c).transpose(0, 2, 1, 3)
        o_ctx = o_ctx.reshape(h_ctx.shape[0], h_ctx.shape[1], D_MODEL)
        out_ctx = (o_ctx @ w_out.astype(F32)).astype(h_ctx.dtype)
    return out_ctx, out_lat


def setup_inputs(seed: int = 0) -> dict:
    key = jax.random.key(seed)
    ks = iter(jax.random.split(key, 40))

    def nrm(shape, std):
        return std * jax.random.normal(next(ks), shape, F32)

    D, F = D_MODEL, D_FF
    G, P, GC = S5_GROUPS, S5_STATE, S5_GROUP
    H = ML_HEADS
    n_a = len(range(0, DEPTH, N_MIXERS))
    n_b = len(range(1, DEPTH, N_MIXERS))
    n_c = len(range(2, DEPTH, N_MIXERS))
    inp = {}
    inp['x'] = nrm((BATCH, SEQ, D), 1.0)
    inp['c'] = nrm((BATCH, D), 1.0)
    inp['ctx'] = nrm((BATCH, CTX_LEN, D), 1.0)
    inp['c_ctx'] = nrm((D,), 1.0)
    inp['ada_w'] = nrm((DEPTH, D, 6 * D), 0.5 * D ** -0.5)
    inp['ada_b'] = nrm((DEPTH, 6 * D), 0.02)
    inp['norm1_g'] = 1.0 + nrm((DEPTH, D), 0.02)
    inp['norm2_g'] = 1.0 + nrm((DEPTH, D), 0.02)
    inp['ffn_w_in'] = nrm((DEPTH, D, 2 * F), D ** -0.5)
    inp['ffn_conv'] = nrm((DEPTH, FFN_CONV, 2 * F), FFN_CONV ** -0.5)
    inp['ffn_w_out'] = nrm((DEPTH, F, D), F ** -0.5)
    inp['s5_lam_re'] = -0.5 + nrm((n_a, 2, G, P), 0.01)
    inp['s5_lam_im'] = math.pi * jnp.arange(P, dtype=F32) + nrm((n_a, 2, G, P), 0.01)
    inp['s5_log_dt'] = jax.random.uniform(next(ks), (n_a, 2, G), F32, math.log(1e-3), math.log(1e-1))
    inp['s5_b_re'] = nrm((n_a, 2, G, P, GC), (2 * GC) ** -0.5)
    inp['s5_b_im'] = nrm((n_a, 2, G, P, GC), (2 * GC) ** -0.5)
    inp['s5_c_re'] = nrm((n_a, 2, G, GC, P), 0.5)
    inp['s5_c_im'] = nrm((n_a, 2, G, GC, P), 0.5)
    inp['s5_d'] = nrm((n_a, D), 0.5)
    inp['s5_glu_w'] = nrm((n_a, D, 2 * D), D ** -0.5)
    inp['s5_glu_b'] = nrm((n_a, 2 * D), 0.02)
    inp['ml_w_in'] = nrm((n_b, D, 4 * D), D ** -0.5)
    inp['ml_conv'] = nrm((n_b, ML_CONV, 2 * D), ML_CONV ** -0.5)
    inp['ml_w_if'] = nrm((n_b, 2, D, 2 * H), 0.1 * D ** -0.5)
    inp['ml_b_if'] = jnp.concatenate(
        [nrm((n_b, 2, H), 0.1), jnp.linspace(3.0, 6.0, H, dtype=F32) + nrm((n_b, 2, H), 0.1)], axis=-1)
    inp['ml_norm_g'] = 1.0 + nrm((n_b, D), 0.02)
    inp['ml_w_out'] = nrm((n_b, D, D), D ** -0.5)
    inp['na_w_qkv'] = nrm((n_c, D, 3 * D), D ** -0.5)
    inp['na_rpb'] = nrm((n_c, NA_HEADS, 2 * NA_WIN_ROWS - 1, 2 * NA_WIN_COLS - 1), 0.1)
    inp['na_w_out'] = nrm((n_c, D, D), D ** -0.5)
    inp['final_g'] = 1.0 + nrm((D,), 0.02)
    return inp


def reference(x, c, ctx, c_ctx, ada_w, ada_b, norm1_g, norm2_g, ffn_w_in, ffn_conv, ffn_w_out,
              s5_lam_re, s5_lam_im, s5_log_dt, s5_b_re, s5_b_im, s5_c_re, s5_c_im, s5_d,
              s5_glu_w, s5_glu_b, ml_w_in, ml_conv, ml_w_if, ml_b_if, ml_norm_g, ml_w_out,
              na_w_qkv, na_rpb, na_w_out, final_g):
    for i in range(DEPTH):
        need_ctx = i < DEPTH - 1
        sh1, sc1, g1, sh2, sc2, g2 = [m[:, None, :] for m in _modulation(c, ada_w[i], ada_b[i])]
        csh1, csc1, cg1, csh2, csc2, cg2 = _modulation(c_ctx, ada_w[i], ada_b[i])
        h_lat = _rmsnorm(x, norm1_g[i]) * (1 + sc1) + sh1
        h_ctx = _rmsnorm(ctx, norm1_g[i]) * (1 + csc1) + csh1
        kind, j = i % N_MIXERS, i // N_MIXERS
        if kind == 0:
            y_ctx, y_lat = _s5_mixer(h_ctx, h_lat, s5_lam_re[j], s5_lam_im[j], s5_log_dt[j],
                                     s5_b_re[j], s5_b_im[j], s5_c_re[j], s5_c_im[j], s5_d[j],
                                     s5_glu_w[j], s5_glu_b[j], need_ctx)
        elif kind == 1:
            y_ctx, y_lat = _mlstm_mixer(h_ctx, h_lat, ml_w_in[j], ml_conv[j], ml_w_if[j], ml_b_if[j],
                                        ml_norm_g[j], ml_w_out[j], need_ctx)
        else:
            y_ctx, y_lat = _na_mixer(h_ctx, h_lat, na_w_qkv[j], na_rpb[j], na_w_out[j], need_ctx)
        x = x + g1 * y_lat
        x = x + g2 * _conv_ffn(_rmsnorm(x, norm2_g[i]) * (1 + sc2) + sh2,
                               ffn_w_in[i], ffn_conv[i], ffn_w_out[i])
        if need_ctx:
            ctx = ctx + cg1 * y_ctx
            ctx = ctx + cg2 * _conv_ffn(_rmsnorm(ctx, norm2_g[i]) * (1 + csc2) + csh2,
                                        ffn_w_in[i], ffn_conv[i], ffn_w_out[i])
    return _rmsnorm(x, final_g)
```

```python
import contextlib
import numpy as np
import concourse.bass as bass
import concourse.mybir as mybir
from concourse.bass_utils import run_bass_kernel_spmd

F32 = mybir.dt.float32
BF16 = mybir.dt.bfloat16
I32 = mybir.dt.int32
AF = mybir.ActivationFunctionType
ALU = mybir.AluOpType
ENGS = ("pe", "act", "dve", "pool", "sp")
NDMASEM = 12

D = 1024
NCH = 8
LCTX = 256
LLAT = 2048
L = LCTX + LLAT
LP = L + 3
DEPTH = 4
DFF = 2816
NJ = DFF // 128
EPS = 1e-6
NCORES = 8


class Prog:
    def __init__(self, nc, es):
        self.nc = nc
        self.es = es
        self.ops = {e: [] for e in ENGS}
        self.cnt = {e: 0 for e in ENGS}
        self.sems = {}
        for e in ENGS:
            self.sems[("e", e)] = es.enter_context(nc.semaphore("sem_" + e))
        self.dman = {e: 0 for e in ENGS}
        for e in ("sp", "pool", "act"):
            for i in range(NDMASEM):
                self.sems[("d", e, i)] = es.enter_context(nc.semaphore(f"dsem_{e}_{i}"))
        self.seen = {e: {} for e in ENGS}
        self.bw = {}
        self.br = {}

    def _deps(self, eng, reads, writes, extra=()):
        d = {}

        def add(ev):
            if ev is None:
                return
            sk, v = ev
            if d.get(sk, 0) < v:
                d[sk] = v
        for r in reads:
            add(self.bw.get(r))
        for w in writes:
            add(self.bw.get(w))
            for sk, v in self.br.get(w, {}).items():
                add((sk, v))
        for ev in extra:
            add(ev)
        out = []
        seen = self.seen[eng]
        for sk, v in d.items():
            if eng == "pe" and sk == ("e", "pe"):
                continue
            if seen.get(sk, 0) >= v:
                continue
            seen[sk] = v
            out.append((sk, v))
        return out

    def _commit(self, ev, reads, writes):
        sk, v = ev
        for r in reads:
            self.br.setdefault(r, {})[sk] = v
        for w in writes:
            self.bw[w] = ev
            self.br[w] = {}

    def op(self, eng, fn, reads=(), writes=()):
        reads = list(reads)
        writes = list(writes)
        waits = self._deps(eng, reads, writes)
        self.cnt[eng] += 1
        ev = (("e", eng), self.cnt[eng])
        self.ops[eng].append((waits, fn, (("e", eng), 1)))
        self._commit(ev, reads, writes)
        return ev

    def dma(self, eng, out, in_, reads=(), writes=(), **kw):
        reads = list(reads)
        writes = list(writes)
        n = self.dman[eng]
        self.dman[eng] += 1
        slot = n % NDMASEM
        use = n // NDMASEM + 1
        sk = ("d", eng, slot)
        extra = [(sk, 16 * (use - 1))] if use > 1 else []
        waits = self._deps(eng, reads, writes, extra)
        ev = (sk, 16 * use)
        self.ops[eng].append((waits, (lambda e: e.dma_start(out=out, in_=in_, **kw)), (sk, 16)))
        self._commit(ev, reads, writes)
        return ev

    def barrier(self):
        evs = [(("e", e), self.cnt[e]) for e in ENGS if self.cnt[e] > 0]
        for e in ("sp", "pool", "act"):
            n = self.dman[e]
            for slot in range(min(n, NDMASEM)):
                uses = (n - 1 - slot) // NDMASEM + 1
                evs.append((("d", e, slot), 16 * uses))
        for eng in ENGS:
            waits = []
            for sk, v in evs:
                if sk == ("e", eng) and eng == "pe":
                    continue
                if self.seen[eng].get(sk, 0) >= v:
                    continue
                self.seen[eng][sk] = v
                waits.append((sk, v))
            if waits:
                self.ops[eng].append((waits, None, None))
        self.bw = {}
        self.br = {}

    def wait_all(self, eng, keys):
        waits = self._deps(eng, keys, [])
        self.ops[eng].append((waits, None, None))

    def emit(self):
        nc = self.nc
        with nc.Block() as block:
            def run(engobj, name):
                for waits, fn, inc in self.ops[name]:
                    for sk, v in waits:
                        engobj.wait_ge(self.sems[sk], v)
                    if fn is None:
                        continue
                    ins = fn(engobj)
                    if inc is not None:
                        ins.then_inc(self.sems[inc[0]], inc[1])

            @block.sync
            def _(e):
                run(e, "sp")

            @block.scalar
            def _(e):
                run(e, "act")

            @block.vector
            def _(e):
                run(e, "dve")

            @block.gpsimd
            def _(e):
                run(e, "pool")

            @block.tensor
            def _(e):
                run(e, "pe")

    def mm(self, out, lhsT, rhs, start, stop, reads, writes):
        return self.op("pe", lambda e: e.matmul(out, lhsT=lhsT, rhs=rhs, start=start, stop=stop), reads, writes)

    def tr(self, out, in_, ident, reads, writes):
        return self.op("pe", lambda e: e.transpose(out, in_, ident), reads, writes)

    def act(self, out, in_, func, reads, writes, scale=None, bias=None):
        kw = {}
        if scale is not None:
            kw["scale"] = scale
        if bias is not None:
            kw["bias"] = bias
        return self.op("act", lambda e: e.activation(out=out, in_=in_, func=func, **kw), reads, writes)

    def ts(self, eng, out, in0, s1, s2, op0, op1, reads, writes):
        if op1 is None:
            return self.op(eng, lambda e: e.tensor_scalar(out=out, in0=in0, scalar1=s1, scalar2=None, op0=op0), reads, writes)
        return self.op(eng, lambda e: e.tensor_scalar(out=out, in0=in0, scalar1=s1, scalar2=s2, op0=op0, op1=op1), reads, writes)

    def stt(self, out, in0, scalar, in1, op0, op1, reads, writes):
        return self.op("dve", lambda e: e.scalar_tensor_tensor(out=out, in0=in0, scalar=scalar, in1=in1, op0=op0, op1=op1), reads, writes)

    def tt(self, eng, out, in0, in1, op, reads, writes):
        return self.op(eng, lambda e: e.tensor_tensor(out=out, in0=in0, in1=in1, op=op), reads, writes)

    def copy(self, eng, out, in_, reads, writes):
        if eng == "act":
            return self.op("act", lambda e: e.copy(out=out, in_=in_), reads, writes)
        return self.op(eng, lambda e: e.tensor_copy(out=out, in_=in_), reads, writes)

    def memset(self, eng, ap, val, writes):
        return self.op(eng, lambda e: e.memset(ap, val), (), writes)

    def recip(self, out, in_, reads, writes):
        return self.op("dve", lambda e: e.reciprocal(out=out, in_=in_), reads, writes)

    def scan(self, out, d0, d1, init, reads, writes):
        return self.op("dve", lambda e: e.tensor_tensor_scan(out=out, data0=d0, data1=d1, initial=init, op0=ALU.mult, op1=ALU.add), reads, writes)


class Arena:
    def __init__(self, t, nbytes):
        self.t = t
        self.n = nbytes
        self.off = 0
        self.marks = []

    def mark(self):
        self.marks.append(self.off)

    def release(self):
        self.off = self.marks.pop()

    def alloc(self, shape, dtype):
        esz = 2 if dtype == BF16 else 4
        n = int(np.prod(shape)) * esz
        n4 = (n + 63) // 64 * 64
        assert self.off + n4 <= self.n, f"arena overflow {self.off}+{n4}>{self.n}"
        a = self.t[:, self.off // 4:(self.off + n4) // 4]
        self.off += n4
        if dtype != F32:
            a = a.bitcast(dtype)
        a = a[:, 0:int(np.prod(shape))]
        if len(shape) == 2:
            return a.rearrange("p (a b) -> p a b", a=shape[0])
        if len(shape) == 3:
            return a.rearrange("p (a b c) -> p a b c", a=shape[0], b=shape[1])
        return a


class Ctx:
    pass


def ffn_blocks():
    blks = [(0, LCTX + 2)]
    sizes = [410, 410, 410, 410, 408]
    s = 0
    for sz in sizes:
        blks.append((LCTX + 1 + s, sz + 2))
        s += sz
    return blks


def tok_blocks():
    out = [(0, LCTX, 1)]
    for i in range(4):
        out.append((LCTX + 512 * i, 512, LCTX + 2 + 512 * i))
    return out


def emit_mods(P, K):
    nc = P.nc
    ar = K.arena
    ar.mark()
    cs = ar.alloc([NCH, 5], F32)
    P.dma("sp", cs, K.d["cT"][:, :, :], writes=["cs"])
    P.act(cs, cs, AF.Silu, ["cs"], ["cs"])
    wt = [ar.alloc([NCH, 512], F32) for _ in range(3)]
    n = 0
    for l in range(DEPTH):
        wv = K.d["ada_w"][l].rearrange("(kc p) n -> p kc n", p=128)
        psb = K.ps[l % 2]
        for jg in range(12):
            buf = n % 3
            n += 1
            P.dma("sp", wt[buf], wv[:, :, jg * 512:(jg + 1) * 512], writes=[("adaw", buf)])
            for jj in range(4):
                j = jg * 4 + jj
                for kc in range(NCH):
                    P.mm(psb[:, j * 5:(j + 1) * 5], wt[buf][:, kc, jj * 128:(jj + 1) * 128], cs[:, kc, :],
                         kc == 0, kc == NCH - 1, [("adaw", buf), "cs"], [("ps", l % 2)])
        P.tt("dve", K.MOD[:, l, :, :], psb[:, 0:240].rearrange("p (j b) -> p j b", b=5),
             K.adab[:, l, :].unsqueeze(2).to_broadcast([128, 48, 5]), ALU.add,
             [("ps", l % 2), "consts"], [("MOD", l)])
        for (dst, g, s) in ((K.A1, K.n1g, 1), (K.A2, K.n2g, 4)):
            P.ts("dve", dst[:, l, :, :], K.MOD[:, l, s * 8:(s + 1) * 8, :], 1.0, None, ALU.add, None,
                 [("MOD", l)], [("A", l)])
            P.tt("dve", dst[:, l, :, :], dst[:, l, :, :], g[:, l, :].unsqueeze(2).to_broadcast([128, NCH, 5]),
                 ALU.mult, [("A", l), "consts"], [("A", l)])
    ar.release()
    P.barrier()


def emit_rstd(P, K, tagp):
    K.RSTD = K.arena.alloc([1, L], F32)[:, 0, :]
    for bi, (c0, n, _) in enumerate(tok_blocks()):
        pb = K.ps[7]
        for c in range(NCH):
            sq = K.sq[c % 2]
            P.act(sq[:, 0:n], K.X[:, c, c0:c0 + n], AF.Square, [("X", c)], [("sq", c % 2)])
            P.mm(pb[:, 0:n], K.ones, sq[:, 0:n], c == 0, c == NCH - 1, [("sq", c % 2), "consts"], [("ps", 7)])
        P.act(K.RSTD[:, c0:c0 + n], pb[:, 0:n], AF.Sqrt, [("ps", 7), "consts"], [("rstd", bi)], bias=K.epsc[:, 0:1])
        P.recip(K.RSTD[:, c0:c0 + n], K.RSTD[:, c0:c0 + n], [("rstd", bi)], [("rstd", bi)])


RSTD_KEYS = [("rstd", i) for i in range(5)]


def emit_norm_mod(P, K, A, SH, l, b, H, keep_rstd=False):
    K.arena.mark()
    emit_rstd(P, K, "n")
    i = 0
    for bi, (c0, n, p0) in enumerate(tok_blocks()):
        bcol = 4 if bi == 0 else b
        for c in range(NCH):
            tmp = K.tmpB[i % 3]
            tk = ("tmpB", i % 3)
            i += 1
            P.tt("dve", tmp[:, 0:n], K.X[:, c, c0:c0 + n], K.RSTD[:, c0:c0 + n], ALU.mult, [("X", c), ("rstd", bi)], [tk])
            P.act(H[:, c, p0:p0 + n], tmp[:, 0:n], AF.Identity, [tk, ("MOD", l), ("A", l)], [("H", c)],
                  scale=A[:, l, c, bcol:bcol + 1], bias=SH[:, c, bcol:bcol + 1])
    if keep_rstd:
        K.arena.marks.pop()
    else:
        K.arena.release()
        P.barrier()


def zero_pads(P, H, nchunks, keyname):
    for col in (0, LCTX + 1, LP - 1):
        P.memset("pool", H[:, :, col:col + 1], 0.0, [(keyname, c) for c in range(nchunks)])


def emit_ffn(P, K, l, b):
    ar = K.arena
    ar.mark()
    H = ar.alloc([NCH, LP], BF16)
    zero_pads(P, H, NCH, "H")
    emit_norm_mod(P, K, K.A2, K.MOD[:, l, 24:32, :], l, b, H)
    G2 = K.MOD[:, l, 40:48, :]
    groups = [(0, 4), (4, 4), (8, 4), (12, 4), (16, 3), (19, 3)]
    GM = 4
    M = ar.alloc([GM, LP], BF16)
    win = [ar.alloc([2, NCH, 128], BF16) for _ in range(3)]
    wout = [ar.alloc([GM, D], BF16) for _ in range(2)]
    acc = [[ar.alloc([1, 412], F32) for _ in range(3)] for _ in range(2)]
    wiv = K.d["ffn_w_in"][l].rearrange("(kc p) n -> p kc n", p=128)
    wov = K.d["ffn_w_out"][l].rearrange("(j p) n -> p j n", p=128)
    fb = ffn_blocks()
    nw = 0
    nacc = 0
    nps = 0
    for gi, (j0, nj) in enumerate(groups):
        wo = wout[gi % 2]
        P.dma("pool", wo[:, 0:nj, :], wov[:, j0:j0 + nj, :], writes=[("wout", gi % 2)])
        for jl in range(nj):
            j = j0 + jl
            wb = nw % 3
            nw += 1
            P.dma("pool", win[wb][:, 0, :, :], wiv[:, :, j * 128:(j + 1) * 128], writes=[("win", wb, 0)])
            P.dma("pool", win[wb][:, 1, :, :], wiv[:, :, (NJ + j) * 128:(NJ + j + 1) * 128], writes=[("win", wb, 1)])
            for (c0, n) in fb:
                pa = nps % 2
                nps += 1
                ab = nacc % 2
                nacc += 1
                for half in range(2):
                    pst = K.ps[2 * half + pa]
                    for kc in range(NCH):
                        P.mm(pst[:, 0:n], win[wb][:, half, kc, :], H[:, kc, c0:c0 + n], kc == 0, kc == NCH - 1,
                             [("win", wb, half), ("H", kc)], [("ps", 2 * half + pa)])
                no = n - 2
                accs = acc[ab]
                for half in range(2):
                    pst = K.ps[2 * half + pa]
                    ch = j if half == 0 else NJ + j
                    a = accs[half][:, 0, 0:no]
                    kr = [("ps", 2 * half + pa), "consts"]
                    kw = [("acc", ab, half)]
                    P.act(a, pst[:, 1:1 + no], AF.Identity, kr, kw, scale=K.cw[:, l, ch, 1:2])
                    P.stt(a, pst[:, 0:no], K.cw[:, l, ch, 0:1], a, ALU.mult, ALU.add, kr + kw, kw)
                    P.stt(a, pst[:, 2:2 + no], K.cw[:, l, ch, 2:3], a, ALU.mult, ALU.add, kr + kw, kw)
                sg = accs[2][:, 0, 0:no]
                P.act(sg, accs[1][:, 0, 0:no], AF.Silu, [("acc", ab, 1)], [("acc", ab, 2)])
                P.tt("pool", M[:, jl, c0 + 1:c0 + 1 + no], accs[0][:, 0, 0:no], sg, ALU.mult,
                     [("acc", ab, 0), ("acc", ab, 2)], [("M", jl)])
        for oc in range(NCH):
            for (x0, n, p0) in tok_blocks():
                pi = 4 + (nps % 2)
                nps += 1
                for jl in range(nj):
                    P.mm(K.ps[pi][:, 0:n], wo[:, jl, oc * 128:(oc + 1) * 128], M[:, jl, p0:p0 + n], jl == 0, jl == nj - 1,
                         [("wout", gi % 2), ("M", jl)], [("ps", pi)])
                bcol = 4 if x0 == 0 else b
                P.stt(K.X[:, oc, x0:x0 + n], K.ps[pi][:, 0:n], G2[:, oc, bcol:bcol + 1], K.X[:, oc, x0:x0 + n],
                      ALU.mult, ALU.add, [("ps", pi), ("MOD", l), ("X", oc)], [("X", oc)])
    ar.release()
    P.barrier()


NA_H = 16
GRID_W = 64
NA_NT = 21


def na_qtile_info(i):
    if i == 0:
        return list(range(0, 4)), 5
    if i == 1:
        return list(range(0, 4)), 9
    if i == 14:
        return list(range(12, 16)), 13
    if i == 15:
        return list(range(12, 16)), 17
    return list(range(i - 2, i + 3)), 0


def make_na_bias(rpb):
    rpb = np.asarray(rpb, np.float32)
    tiles = [(5, j) for j in range(3, 8)]
    for i in (0, 1):
        tiles += [(i, j) for j in range(0, 4)]
    for i in (14, 15):
        tiles += [(i, j) for j in range(12, 16)]
    assert len(tiles) == NA_NT
    out = np.empty((NA_H, 128, NA_NT, 128), np.float32)
    a = np.arange(2)[:, None]
    col = np.arange(64)[None, :]
    for ti, (i, j) in enumerate(tiles):
        kr = (2 * j + a + 0 * col).reshape(128)
        kc = (0 * a + col).reshape(128)
        qr = (2 * i + a + 0 * col).reshape(128)
        qc = kc.copy()
        r0 = np.clip(qr - 4, 0, 24)
        c0 = np.clip(qc - 8, 0, 48)
        valid = ((kr[:, None] >= r0[None, :]) & (kr[:, None] <= r0[None, :] + 7)
                 & (kc[:, None] >= c0[None, :]) & (kc[:, None] < c0[None, :] + 16))
        dr = np.clip(kr[:, None] - qr[None, :] + 7, 0, 14)
        dc = np.clip(kc[:, None] - qc[None, :], -15, 15) + 15
        vals = rpb[:, dr, dc]
        out[:, :, ti, :] = np.where(valid[None], vals, np.float32(-30000.0))
    return np.ascontiguousarray(out.reshape(NA_H, 128, NA_NT * 128))


def emit_down_proj(P, K, wo, OT, G, b, l, nk, wkey, okey):
    n_ = 0
    for oc in range(NCH):
        for (x0, n, p0) in tok_blocks():
            pi = 4 + (n_ % 2)
            n_ += 1
            for kc in range(nk):
                P.mm(K.ps[pi][:, 0:n], wo[:, kc, oc * 128:(oc + 1) * 128], OT[:, kc, x0:x0 + n], kc == 0, kc == nk - 1,
                     [wkey, (okey, kc)], [("ps", pi)])
            bcol = 4 if x0 == 0 else b
            P.stt(K.X[:, oc, x0:x0 + n], K.ps[pi][:, 0:n], G[:, oc, bcol:bcol + 1], K.X[:, oc, x0:x0 + n],
                  ALU.mult, ALU.add, [("ps", pi), ("MOD", l), ("X", oc)], [("X", oc)])


def emit_na(P, K, l, b):
    jn = l // 3
    ar = K.arena
    ar.mark()
    H = ar.alloc([NCH, LP], BF16)
    emit_norm_mod(P, K, K.A1, K.MOD[:, l, 0:8, :], l, b, H)
    G1 = K.MOD[:, l, 16:24, :]
    OTs = [ar.alloc([1, L], BF16) for _ in range(2)]
    wos = [ar.alloc([1, D], BF16) for _ in range(2)]
    wqkv = [ar.alloc([3, NCH, 128], BF16) for _ in range(2)]
    Qt = ar.alloc([1, L], BF16)[:, 0, :]
    Kt = ar.alloc([1, L], BF16)[:, 0, :]
    V = ar.alloc([18, 128], BF16)
    BI1 = ar.alloc([NA_NT, 128], F32)
    BI = [BI1, BI1]
    Tb = [ar.alloc([1, 640], F32)[:, 0, :] for _ in range(2)]
    PT = [ar.alloc([1, 896], BF16)[:, 0, :] for _ in range(2)]
    rden = [ar.alloc([1, 128], F32)[:, 0, :] for _ in range(2)]
    onesb = ar.alloc([1, 128], BF16)[:, 0, :]
    P.memset("pool", onesb, 1.0, ["onesb"])
    wv_ = K.d["na_w_qkv"][jn].rearrange("(kc p) n -> p kc n", p=128)
    nu = 0
    npj = 0
    wov_ = K.d["na_w_out"][jn].rearrange("(kc p) n -> p kc n", p=128)
    for hp in range(NCH):
        wb = hp % 2
        OT = OTs[wb]
        for t3 in range(3):
            P.dma("pool", wqkv[wb][:, t3, :, :], wv_[:, :, t3 * D + hp * 128:t3 * D + (hp + 1) * 128], writes=[("wqkv", wb, t3)])
        P.dma("pool", wos[wb], wov_[:, hp:hp + 1, :], writes=[("wos", wb)])
        for (x0, n, p0) in tok_blocks():
            for t3, dst, dk_ in ((0, Qt, "Qt"), (1, Kt, "Kt")):
                pi = 6 + (npj % 2)
                npj += 1
                for kc in range(NCH):
                    P.mm(K.ps[pi][:, 0:n], wqkv[wb][:, t3, kc, :], H[:, kc, p0:p0 + n], kc == 0, kc == NCH - 1,
                         [("wqkv", wb, t3), ("H", kc)], [("ps", pi)])
                if t3 == 0:
                    P.act(dst[:, x0:x0 + n], K.ps[pi][:, 0:n], AF.Identity, [("ps", pi)], [dk_], scale=0.125)
                else:
                    P.copy("dve", dst[:, x0:x0 + n], K.ps[pi][:, 0:n], [("ps", pi)], [dk_])
        for tt in range(18):
            pc = 1 + 128 * tt if tt < 2 else LCTX + 2 + 128 * (tt - 2)
            pi = 6 + (npj % 2)
            npj += 1
            for kc in range(NCH):
                P.mm(K.ps[pi][:, 0:128], H[:, kc, pc:pc + 128], wqkv[wb][:, 2, kc, :], kc == 0, kc == NCH - 1,
                     [("wqkv", wb, 2), ("H", kc)], [("ps", pi)])
            P.copy("act", V[:, tt, :], K.ps[pi][:, 0:128], [("ps", pi)], ["V"])
        for hh in range(2):
            h = 2 * hp + hh
            hb = 64 * hh
            P.dma("sp", BI[hh], K.d["na_bias"][jn, h].rearrange("p (t q) -> p t q", q=128), writes=[("BI", 0)])
            for qt in range(18):
                u = nu % 2
                nu += 1
                if qt < 2:
                    lat_k, base = [], 0
                else:
                    lat_k, base = na_qtile_info(qt - 2)
                nk = len(lat_k)
                ktiles = [2 + j for j in lat_k] + [0, 1]
                qs = slice(qt * 128, (qt + 1) * 128)

                def sreg(i0, i1):
                    assert i0 // 4 == (i1 - 1) // 4
                    bk = 2 * u + i0 // 4
                    return K.ps[bk][:, (i0 % 4) * 128:(i0 % 4) * 128 + (i1 - i0) * 128], ("ps", bk)
                for idx, kt in enumerate(ktiles):
                    reg, rk = sreg(idx, idx + 1)
                    P.mm(reg, Kt[hb:hb + 64, kt * 128:(kt + 1) * 128], Qt[hb:hb + 64, qs],
                         True, True, ["Qt", "Kt"], [rk])
                if nk:
                    for (i0, i1) in ((0, min(nk, 4)), (4, nk)):
                        if i1 <= i0:
                            continue
                        reg, rk = sreg(i0, i1)
                        P.tt("dve", Tb[u][:, i0 * 128:i1 * 128], reg,
                             BI[hh][:, base + i0:base + i1, :].rearrange("p t q -> p (t q)"), ALU.add,
                             [rk, ("BI", 0)], [("Tb", u)])
                    P.act(PT[u][:, 0:nk * 128], Tb[u][:, 0:nk * 128], AF.Exp, [("Tb", u)], [("PT", u)])
                reg, rk = sreg(nk, nk + 2)
                P.act(PT[u][:, nk * 128:(nk + 2) * 128], reg, AF.Exp, [rk], [("PT", u)])
                Ops = K.ps[4 + u]
                nkt = len(ktiles)
                for idx, kt in enumerate(ktiles):
                    P.mm(Ops[:, 0:128], V[:, kt, :], PT[u][:, idx * 128:(idx + 1) * 128], idx == 0, idx == nkt - 1,
                         ["V", ("PT", u)], [("ps", 4 + u)])
                for idx, kt in enumerate(ktiles):
                    P.mm(Ops[:, 128:256], onesb, PT[u][:, idx * 128:(idx + 1) * 128], idx == 0, idx == nkt - 1,
                         ["onesb", ("PT", u)], [("ps", 4 + u)])
                P.recip(rden[u][hb:hb + 64, :], Ops[hb:hb + 64, 128:256], [("ps", 4 + u)], [("rden", u)])
                P.tt("dve", OT[hb:hb + 64, 0, qs], Ops[hb:hb + 64, 0:128], rden[u][hb:hb + 64, :], ALU.mult,
                     [("ps", 4 + u), ("rden", u)], [(("OT", wb), 0)])
        emit_down_proj(P, K, wos[wb], OT, G1, b, l, 1, ("wos", wb), ("OT", wb))
    ar.release()
    P.barrier()


ML_H = 4
ML_DH = 256
NEGM = -30000.0


def ml_blocks():
    blks = [(0, LCTX + 2, None, 0)]
    s = 0
    for sz in (448, 448, 448, 448, 256):
        blks.append((LCTX + 1 + s, sz + 2, s // 64, sz // 64))
        s += sz
    return blks


def tile_pcol(tt):
    return 1 + 128 * tt if tt < 2 else LCTX + 2 + 128 * (tt - 2)


def emit_mlstm(P, K, l, b):
    jn = l // 3
    ar = K.arena
    d = K.d
    G1 = K.MOD[:, l, 16:24, :]
    ar.mark()
    GT = ar.alloc([18, 16], F32)
    LFt = ar.alloc([18, 8], F32)
    IMB = ar.alloc([18, 8], F32)
    identb = ar.alloc([1, 128], BF16)[:, 0, :]
    onesb = ar.alloc([1, 128], BF16)[:, 0, :]
    P.memset("pool", onesb, 1.0, ["onesb"])
    P.copy("dve", identb, K.ident, ["consts"], ["identb"])
    ar.mark()
    H = ar.alloc([NCH, LP], BF16)
    zero_pads(P, H, NCH, "H")
    emit_norm_mod(P, K, K.A1, K.MOD[:, l, 0:8, :], l, b, H)
    wif = ar.alloc([NCH, 16], BF16)
    P.dma("pool", wif, d["ml_wif"][:, :, :], writes=["wif"])
    for tt in range(18):
        pc = tile_pcol(tt)
        for kc in range(NCH):
            P.mm(K.ps[7][:, tt * 16:(tt + 1) * 16], H[:, kc, pc:pc + 128], wif[:, kc, :], kc == 0, kc == NCH - 1,
                 ["wif", ("H", kc)], [("ps", 7)])
    P.tt("dve", GT, K.ps[7][:, 0:288].rearrange("p (t g) -> p t g", g=16),
         K.mlbif.unsqueeze(1).to_broadcast([128, 18, 16]), ALU.add, [("ps", 7), "consts"], ["GT"])
    GTv = GT.rearrange("p t (d g) -> p t d g", d=2)
    LFv = LFt.rearrange("p t (d h) -> p t d h", d=2)
    IMv = IMB.rearrange("p t (d h) -> p t d h", d=2)
    P.act(LFv, GTv[:, :, :, 4:8], AF.Exp, ["GT"], ["LFt"], scale=-1.0)
    P.act(LFv, LFv, AF.Ln, ["LFt", "consts"], ["LFt"], bias=K.onec[:, 0:1])
    P.ts("dve", LFt, LFt, -1.0, None, ALU.mult, None, ["LFt"], ["LFt"])
    for tt in range(18):
        for dr in range(2):
            P.mm(K.ps[6][:, tt * 8 + dr * 4:tt * 8 + dr * 4 + 4], K.tri[:, dr, :], LFt[:, tt, dr * 4:(dr + 1) * 4], True, True,
                 ["LFt", "consts"], [("ps", 6)])
    P.tt("dve", IMv, GTv[:, :, :, 0:4], K.ps[6][:, 0:144].rearrange("p (t d h) -> p t d h", d=2, h=4), ALU.subtract,
         ["GT", ("ps", 6)], ["IMB"])
    wqk = ar.alloc([4, NCH, 128], BF16)
    wvo = ar.alloc([2, NCH, 256], BF16)
    QK = ar.alloc([4, L], BF16)
    Vst = ar.alloc([18, 256], BF16)
    Og = ar.alloc([2, L], BF16)
    accs = [[ar.alloc([1, 450], F32)[:, 0, :] for _ in range(2)] for _ in range(2)]
    rt = [ar.alloc([1, 448], F32)[:, 0, :] for _ in range(4)]
    wv_ = d["ml_w_in"][jn].rearrange("(kc p) n -> p kc n", p=128)
    npj = 0
    for hd in range(ML_H):
        for qk in range(2):
            base = qk * D + hd * ML_DH
            for ab in range(2):
                for hf in range(2):
                    c0 = base + 128 * hf + 64 * ab
                    P.dma("pool", wqk[:, 2 * qk + ab, :, 64 * hf:64 * hf + 64], wv_[:, :, c0:c0 + 64], writes=[("wqk", 2 * qk + ab)])
        for vo in range(2):
            c0 = (2 + vo) * D + hd * ML_DH
            P.dma("pool", wvo[:, vo, :, :], wv_[:, :, c0:c0 + 256], writes=[("wvo", vo)])
        for qk in range(2):
            for (c0, n, r0, nr) in ml_blocks():
                no = n - 2
                for ab in range(2):
                    pi = 2 * ab + (npj % 2)
                    for kc in range(NCH):
                        P.mm(K.ps[pi][:, 0:n], wqk[:, 2 * qk + ab, kc, :], H[:, kc, c0:c0 + n], kc == 0, kc == NCH - 1,
                             [("wqk", 2 * qk + ab), ("H", kc)], [("ps", pi)])
                    a = accs[ab][npj % 2][:, 0:no]
                    ak = ("macc", ab, npj % 2)
                    cwi = (qk * ML_H + hd) * 2 + ab
                    P.act(a, K.ps[pi][:, 1:1 + no], AF.Identity, [("ps", pi), "consts"], [ak], scale=K.mlcw[:, cwi, 1:2])
                    P.stt(a, K.ps[pi][:, 0:no], K.mlcw[:, cwi, 0:1], a, ALU.mult, ALU.add, [("ps", pi), "consts", ak], [ak])
                    P.stt(a, K.ps[pi][:, 2:2 + no], K.mlcw[:, cwi, 2:3], a, ALU.mult, ALU.add, [("ps", pi), "consts", ak], [ak])
                    P.act(a, a, AF.Silu, [ak], [ak])
                A_ = accs[0][npj % 2][:, 0:no]
                B_ = accs[1][npj % 2][:, 0:no]
                kA = ("macc", 0, npj % 2)
                kB = ("macc", 1, npj % 2)
                npj += 1
                x0 = c0
                if r0 is None:
                    for ab, src, sk in ((0, A_, kA), (1, B_, kB)):
                        if qk == 0:
                            P.act(QK[:, ab, 0:LCTX], src, AF.Identity, [sk], [("QK", ab)], scale=1.0 / 16.0)
                        else:
                            P.copy("dve", QK[:, 2 + ab, 0:LCTX], src, [sk], [("QK", 2 + ab)])
                    continue
                xs = c0 - 1
                tb = K.ropeq if qk == 0 else K.ropek
                for hfp in range(2):
                    ps_ = slice(64 * hfp, 64 * hfp + 64)
                    if hfp == 0:
                        cosv = tb[ps_, 0, r0:r0 + nr].unsqueeze(2).to_broadcast([64, nr, 64])
                        sinv = tb[ps_, 1, r0:r0 + nr].unsqueeze(2).to_broadcast([64, nr, 64])
                    else:
                        cosv = tb[ps_, 0, :].unsqueeze(1).to_broadcast([64, nr, 64])
                        sinv = tb[ps_, 1, :].unsqueeze(1).to_broadcast([64, nr, 64])

                    def v3(t_):
                        return t_[ps_, 0:no].rearrange("p (r c) -> p r c", c=64)
                    rk = [("rt", i, hfp) for i in range(4)]
                    P.tt("pool", v3(rt[0]), v3(A_), cosv, ALU.mult, [kA, "consts"], [rk[0]])
                    P.tt("pool", v3(rt[1]), v3(B_), sinv, ALU.mult, [kB, "consts"], [rk[1]])
                    P.tt("pool", v3(rt[2]), v3(A_), sinv, ALU.mult, [kA, "consts"], [rk[2]])
                    P.tt("pool", v3(rt[3]), v3(B_), cosv, ALU.mult, [kB, "consts"], [rk[3]])
                    P.tt("dve", QK[ps_, 2 * qk + 0, xs:xs + no], rt[0][ps_, 0:no], rt[1][ps_, 0:no], ALU.subtract,
                         [rk[0], rk[1]], [("QK", 2 * qk)])
                    P.tt("dve", QK[ps_, 2 * qk + 1, xs:xs + no], rt[2][ps_, 0:no], rt[3][ps_, 0:no], ALU.add,
                         [rk[2], rk[3]], [("QK", 2 * qk + 1)])
        for tt in range(18):
            pc = tile_pcol(tt)
            pi = 4 + (tt % 2)
            for kc in range(NCH):
                P.mm(K.ps[pi][:, 0:256], H[:, kc, pc:pc + 128], wvo[:, 0, kc, :], kc == 0, kc == NCH - 1,
                     [("wvo", 0), ("H", kc)], [("ps", pi)])
            P.copy("act", Vst[:, tt, :], K.ps[pi][:, 0:256], [("ps", pi)], ["Vst"])
        n_ = 0
        for mc in range(2):
            for (x0, n, p0) in tok_blocks():
                pi = 4 + (n_ % 2)
                n_ += 1
                for kc in range(NCH):
                    P.mm(K.ps[pi][:, 0:n], wvo[:, 1, kc, mc * 128:(mc + 1) * 128], H[:, kc, p0:p0 + n], kc == 0, kc == NCH - 1,
                         [("wvo", 1), ("H", kc)], [("ps", pi)])
                P.act(Og[:, mc, x0:x0 + n], K.ps[pi][:, 0:n], AF.Sigmoid, [("ps", pi)], [("Og", mc)])
        P.dma("sp", d["ml_sq"][hd], QK, reads=[("QK", i) for i in range(4)], writes=[("ml_sq", hd)])
        P.dma("sp", d["ml_sv"][hd], Vst, reads=["Vst"], writes=[("ml_sv", hd)])
        P.dma("sp", d["ml_so"][hd], Og, reads=[("Og", 0), ("Og", 1)], writes=[("ml_so", hd)])
    ar.release()
    P.barrier()
    ar.mark()
    QK = ar.alloc([4, L], BF16)
    Vst = ar.alloc([18, 256], BF16)
    Og = ar.alloc([2, L], BF16)
    HS = ar.alloc([2, L], F32)
    wo = ar.alloc([2, D], BF16)
    Cst = [ar.alloc([2, 384], F32) for _ in range(2)]
    Cb = [ar.alloc([2, 384], BF16) for _ in range(2)]
    LFr = [ar.alloc([1, 128], F32)[:, 0, :] for _ in range(2)]
    Targ = [ar.alloc([1, 128], F32)[:, 0, :] for _ in range(2)]
    Dm = [ar.alloc([1, 128], F32)[:, 0, :] for _ in range(2)]
    WT = [ar.alloc([1, 128], BF16)[:, 0, :] for _ in range(2)]
    Ebc = [ar.alloc([1, 128], F32)[:, 0, :] for _ in range(2)]
    Qtl = [ar.alloc([2, 128], BF16) for _ in range(2)]
    ktl = [ar.alloc([1, 256], BF16)[:, 0, :] for _ in range(2)]
    rr = [ar.alloc([1, 128], F32)[:, 0, :] for _ in range(2)]
    tmo = [ar.alloc([1, 128], F32)[:, 0, :] for _ in range(2)]
    smallc = [ar.alloc([1, 4], F32)[:, 0, :] for _ in range(2)]
    psT = K.ps[6].bitcast(BF16)
    wov = d["ml_w_out"][jn].rearrange("(kc p) n -> p kc n", p=128)
    for hd in range(ML_H):
        P.dma("sp", QK, d["ml_sq"][hd], writes=[("QK", i) for i in range(4)])
        P.dma("sp", Vst, d["ml_sv"][hd], writes=["Vst"])
        P.dma("sp", Og, d["ml_so"][hd], writes=[("Og", 0), ("Og", 1)])
        P.dma("pool", wo, wov[:, 2 * hd:2 * hd + 2, :], writes=["wo"])
        it = 0
        for dr in range(2):
            order = [0, 1] + list(range(2, 18)) if dr == 0 else [1, 0] + list(range(17, 1, -1))
            te = 127 if dr == 0 else 0
            C_ = Cst[dr]
            Cb_ = Cb[dr]
            P.memset("pool", C_, 0.0, [("C", dr)])
            for oi, tt in enumerate(order):
                u = it % 2
                it += 1
                first = oi == 0
                last = oi == len(order) - 1
                cs = slice(tt * 128, (tt + 1) * 128)
                g = dr * 4 + hd
                pb = K.ps[u]
                pn = K.ps[2 + u]
                P.copy("pool", LFr[u], LFt[:, tt, g:g + 1].to_broadcast([128, 128]), ["LFt"], [("LFr", u)])
                P.mm(pb[:, 0:128], LFr[u], K.tri[:, dr, :], True, True, [("LFr", u), "consts"], [("ps", u)])
                P.mm(pb[:, 128:256], QK[:, 2, cs], QK[:, 0, cs], True, False, [("QK", 2), ("QK", 0)], [("ps", u)])
                P.mm(pb[:, 128:256], QK[:, 3, cs], QK[:, 1, cs], False, True, [("QK", 3), ("QK", 1)], [("ps", u)])
                P.stt(Targ[u], pb[:, 0:128], IMB[:, tt, g:g + 1], K.negm[:, dr, :], ALU.add, ALU.add,
                      [("ps", u), "IMB", "consts"], [("Targ", u)])
                P.act(Dm[u], Targ[u], AF.Exp, [("Targ", u)], [("Dm", u)])
                P.tt("dve", WT[u], pb[:, 128:256], Dm[u], ALU.mult, [("ps", u), ("Dm", u)], [("WT", u)])
                if not first or not last:
                    P.act(Ebc[u], pb[:, 0:128], AF.Exp, [("ps", u)], [("Ebc", u)])
                if not first:
                    P.tt("pool", Qtl[u][:, 0, :], QK[:, 0, cs], Ebc[u], ALU.mult, [("QK", 0), ("Ebc", u)], [("Qtl", u)])
                    P.tt("pool", Qtl[u][:, 1, :], QK[:, 1, cs], Ebc[u], ALU.mult, [("QK", 1), ("Ebc", u)], [("Qtl", u)])
                for mc in range(3):
                    lh = Vst[:, tt, mc * 128:(mc + 1) * 128] if mc < 2 else onesb
                    P.mm(pn[:, mc * 128:(mc + 1) * 128], lh, WT[u], True, first, ["Vst", "onesb", ("WT", u)], [("ps", 2 + u)])
                    if not first:
                        for kc in range(2):
                            P.mm(pn[:, mc * 128:(mc + 1) * 128], Cb_[:, kc, mc * 128:(mc + 1) * 128], Qtl[u][:, kc, :], False, kc == 1,
                                 [("Cb", dr), ("Qtl", u)], [("ps", 2 + u)])
                P.act(rr[u], pn[:, 256:384], AF.Abs, [("ps", 2 + u)], [("rr", u)])
                P.ts("dve", rr[u], rr[u], 1.0, None, ALU.max, None, [("rr", u)], [("rr", u)])
                P.recip(rr[u], rr[u], [("rr", u)], [("rr", u)])
                for mc in range(2):
                    if dr == 0:
                        P.tt("dve", HS[:, mc, cs], pn[:, mc * 128:(mc + 1) * 128], rr[u], ALU.mult,
                             [("ps", 2 + u), ("rr", u)], [("HS", mc)])
                    else:
                        P.tt("dve", tmo[mc], pn[:, mc * 128:(mc + 1) * 128], rr[u], ALU.mult,
                             [("ps", 2 + u), ("rr", u)], [("tmo", mc)])
                        P.tt("pool", HS[:, mc, cs], HS[:, mc, cs], tmo[mc], ALU.add, [("HS", mc), ("tmo", mc)], [("HS", mc)])
                if last:
                    continue
                sc_ = smallc[u]
                P.copy("dve", sc_[:, 0:1], pb[:, te:te + 1], [("ps", u)], [("smallc", u)])
                P.act(sc_[:, 1:2], IMB[:, tt, g:g + 1], AF.Exp, ["IMB", ("smallc", u)], [("smallc", u)], bias=sc_[:, 0:1])
                P.tr(psT[:, 0:128], QK[:, 2, cs], identb, [("QK", 2), "identb"], [("ps", 6)])
                P.tr(psT[:, 128:256], QK[:, 3, cs], identb, [("QK", 3), "identb"], [("ps", 6)])
                P.ts("dve", ktl[u], psT[:, 0:256], sc_[:, 1:2], None, ALU.mult, None, [("ps", 6), ("smallc", u)], [("ktl", u)])
                for kc in range(2):
                    pc_ = K.ps[4 + kc]
                    P.mm(pc_[:, 0:256], ktl[u][:, kc * 128:(kc + 1) * 128], Vst[:, tt, :], True, True, [("ktl", u), "Vst"], [("ps", 4 + kc)])
                    P.mm(pc_[:, 256:384], ktl[u][:, kc * 128:(kc + 1) * 128], onesb, True, True, [("ktl", u), "onesb"], [("ps", 4 + kc)])
                    P.stt(C_[:, kc, :], C_[:, kc, :], Ebc[u][:, te:te + 1], pc_[:, 0:384], ALU.mult, ALU.add,
                          [("C", dr), ("Ebc", u), ("ps", 4 + kc)], [("C", dr)])
                P.copy("act", Cb_, C_, [("C", dr)], [("Cb", dr)])
        HN = QK[:, 0:2, :]
        for bi, (x0, n, p0) in enumerate(tok_blocks()):
            for mc in range(2):
                sq = K.sq[mc]
                P.act(sq[:, 0:n], HS[:, mc, x0:x0 + n], AF.Square, [("HS", mc)], [("sq", mc)])
                P.mm(K.ps[7][:, 0:n], K.ones, sq[:, 0:n], mc == 0, mc == 1, [("sq", mc), "consts"], [("ps", 7)])
            rs = K.tmpB[bi % 3]
            rk = ("tmpB", bi % 3)
            P.act(rs[:, 0:n], K.ps[7][:, 0:n], AF.Sqrt, [("ps", 7), "consts"], [rk], bias=K.epsc[:, 0:1], scale=4.0)
            P.recip(rs[:, 0:n], rs[:, 0:n], [rk], [rk])
            for mc in range(2):
                P.tt("dve", HS[:, mc, x0:x0 + n], HS[:, mc, x0:x0 + n], rs[:, 0:n], ALU.mult, [("HS", mc), rk], [("HS", mc)])
                P.stt(HN[:, mc, x0:x0 + n], HS[:, mc, x0:x0 + n], K.mlng[:, 2 * hd + mc:2 * hd + mc + 1], Og[:, mc, x0:x0 + n],
                      ALU.mult, ALU.mult, [("HS", mc), ("Og", mc), "consts"], [("QK", mc)])
        emit_down_proj(P, K, wo, HN, G1, b, l, 2, "wo", "QK")
    ar.release()
    ar.release()
    P.barrier()


S5_G2 = 32
GELU_C = 1.5957691216057308


def s5_scalars(P, K, ar, W, LR, LI, LDT, tag, want_q):
    def T():
        return ar.alloc([1, W], F32)[:, 0, :]

    def k(n):
        return (tag, n)
    lr, er, c, s_ = [T() for _ in range(4)]
    ar.mark()
    dt, ang, t1, t2 = [T() for _ in range(4)]
    P.act(dt, LDT, AF.Exp, [k("in")], [k("dt")])
    P.ts("dve", lr, LR, -1e-4, None, ALU.min, None, [k("in")], [k("lr")])
    P.tt("dve", t1, lr, dt, ALU.mult, [k("lr"), k("dt")], [k("t1")])
    P.act(er, t1, AF.Exp, [k("t1")], [k("er")])
    P.tt("dve", ang, LI, dt, ALU.mult, [k("in"), k("dt")], [k("ang")])
    ki = ar.alloc([1, W], I32)[:, 0, :]
    kr, ph, x2 = T(), T(), T()
    P.ts("dve", t1, ang, 1.0 / (2.0 * np.pi), 0.25, ALU.mult, ALU.add, [k("ang")], [k("t1")])
    P.copy("dve", ki, t1, [k("t1")], [k("ki")])
    P.copy("dve", kr, ki, [k("ki")], [k("kr")])
    P.stt(ph, kr, -6.28125, ang, ALU.mult, ALU.add, [k("kr"), k("ang")], [k("ph")])
    P.stt(ph, kr, -1.9353071795864769e-3, ph, ALU.mult, ALU.add, [k("kr"), k("ph")], [k("ph")])
    P.ts("dve", ph, ph, 0.25, None, ALU.mult, None, [k("ph")], [k("ph")])
    P.tt("dve", x2, ph, ph, ALU.mult, [k("ph")], [k("x2")])
    import math
    sco = [(-1.0) ** i / math.factorial(2 * i + 1) for i in range(1, 7)]
    cco = [(-1.0) ** i / math.factorial(2 * i) for i in range(1, 7)]
    P.ts("dve", s_, x2, sco[5], None, ALU.mult, None, [k("x2")], [k("s")])
    for i in range(4, -1, -1):
        P.stt(s_, s_, sco[i], x2, ALU.add, ALU.mult, [k("s"), k("x2")], [k("s")])
    P.stt(s_, s_, 1.0, ph, ALU.add, ALU.mult, [k("s"), k("ph")], [k("s")])
    P.ts("dve", c, x2, cco[5], None, ALU.mult, None, [k("x2")], [k("c")])
    for i in range(4, -1, -1):
        P.stt(c, c, cco[i], x2, ALU.add, ALU.mult, [k("c"), k("x2")], [k("c")])
    P.ts("dve", c, c, 1.0, None, ALU.add, None, [k("c")], [k("c")])
    for it in range(2):
        P.tt("dve", t1, c, c, ALU.mult, [k("c")], [k("t1")])
        P.tt("dve", t2, s_, s_, ALU.mult, [k("s")], [k("t2")])
        P.stt(s_, c, 2.0, s_, ALU.mult, ALU.mult, [k("c"), k("s")], [k("s")])
        P.tt("dve", c, t1, t2, ALU.subtract, [k("t1"), k("t2")], [k("c")])
    out = {"er": er, "c": c, "s": s_}
    ar.release()
    P.barrier()
    if want_q:
        are, aim, den, nr, qre, qim, t1, t2 = [T() for _ in range(8)]
        P.tt("dve", are, er, c, ALU.mult, [k("er"), k("c")], [k("are")])
        P.tt("dve", aim, er, s_, ALU.mult, [k("er"), k("s")], [k("aim")])
        P.tt("dve", t1, lr, lr, ALU.mult, [k("lr")], [k("t1")])
        P.tt("dve", den, LI, LI, ALU.mult, [k("in")], [k("den")])
        P.tt("dve", den, den, t1, ALU.add, [k("den"), k("t1")], [k("den")])
        P.recip(den, den, [k("den")], [k("den")])
        P.ts("dve", nr, are, -1.0, None, ALU.add, None, [k("are")], [k("nr")])
        P.tt("dve", t1, nr, lr, ALU.mult, [k("nr"), k("lr")], [k("t1")])
        P.tt("dve", t2, aim, LI, ALU.mult, [k("aim"), k("in")], [k("t2")])
        P.tt("dve", qre, t1, t2, ALU.add, [k("t1"), k("t2")], [k("qre")])
        P.tt("dve", qre, qre, den, ALU.mult, [k("qre"), k("den")], [k("qre")])
        P.tt("dve", t1, aim, lr, ALU.mult, [k("aim"), k("lr")], [k("t1")])
        P.tt("dve", t2, nr, LI, ALU.mult, [k("nr"), k("in")], [k("t2")])
        P.tt("dve", qim, t1, t2, ALU.subtract, [k("t1"), k("t2")], [k("qim")])
        P.tt("dve", qim, qim, den, ALU.mult, [k("qim"), k("den")], [k("qim")])
        out["qre"] = qre
        out["qim"] = qim
    return out


def emit_s5_gen(P, K, js):
    ar = K.arena
    d = K.d
    ar.mark()
    p1 = ar.alloc([3, 64], F32)
    P.dma("sp", p1, d["s5p1"][js], writes=[("g1", "in")])
    sc = s5_scalars(P, K, ar, 64, p1[:, 0, :], p1[:, 1, :], p1[:, 2, :], "g1", False)
    P.copy("dve", K.S5R[:, js, 0, :], sc["er"], [("g1", "er")], [("S5R", js)])
    for dh in range(2):
        ar.mark()
        COS = ar.alloc([32, 128], F32)
        SIN = ar.alloc([32, 128], F32)
        t1 = ar.alloc([32, 64], F32)
        t2 = ar.alloc([32, 64], F32)
        P.copy("dve", COS[:, :, 0:1], sc["c"][:, dh * 32:(dh + 1) * 32].unsqueeze(2), [("g1", "c")], ["COS"])
        P.copy("dve", SIN[:, :, 0:1], sc["s"][:, dh * 32:(dh + 1) * 32].unsqueeze(2), [("g1", "s")], ["SIN"])
        for kk in range(7):
            ln = 2 ** kk
            pr = COS[:, :, ln - 1:ln].to_broadcast([128, 32, ln])
            pi_ = SIN[:, :, ln - 1:ln].to_broadcast([128, 32, ln])
            a1 = t1[:, :, 0:ln]
            a2 = t2[:, :, 0:ln]
            P.tt("dve", a1, COS[:, :, 0:ln], pr, ALU.mult, ["COS"], ["t1"])
            P.tt("dve", a2, SIN[:, :, 0:ln], pi_, ALU.mult, ["SIN"], ["t2"])
            P.tt("dve", COS[:, :, ln:2 * ln], a1, a2, ALU.subtract, ["t1", "t2", "COS"], ["COSn"])
            P.tt("dve", a1, COS[:, :, 0:ln], pi_, ALU.mult, ["COS", "SIN", "COSn"], ["t1"])
            P.tt("dve", a2, SIN[:, :, 0:ln], pr, ALU.mult, ["SIN", "COS", "COSn"], ["t2"])
            P.tt("dve", SIN[:, :, ln:2 * ln], a1, a2, ALU.add, ["t1", "t2", "SIN"], ["SIN"])
            P.copy("dve", COS[:, :, 0:1], COS[:, :, 0:1], ["COSn", "COS"], ["COS"])
        P.copy("dve", K.S5R[:, js, 1, dh * 32:(dh + 1) * 32].unsqueeze(2), COS[:, :, 127:128], ["COS"], [("S5R", js)])
        P.copy("dve", K.S5R[:, js, 2, dh * 32:(dh + 1) * 32].unsqueeze(2), SIN[:, :, 127:128], ["SIN"], [("S5R", js)])
        P.dma("sp", d["s5t"][js, :, 0, dh * 32:(dh + 1) * 32, :], COS, reads=["COS"], writes=[("s5t", js, 0, dh)])
        P.dma("sp", d["s5t"][js, :, 1, dh * 32:(dh + 1) * 32, :], SIN, reads=["SIN"], writes=[("s5t", js, 1, dh)])
        ar.release()
        P.barrier()
    ar.release()
    P.barrier()
    ar.mark()
    CC = ar.alloc([2, 1024], F32)
    P.dma("sp", CC, d["s5c"][js], writes=["CC"])
    for var in range(2):
        WC = ar.alloc([2, S5_G2, 128], BF16)
        P.memset("pool", WC, 0.0, [("WC", var)])
        WCv = WC.rearrange("p d (c k) m -> p d c k m", k=4)
        CCv = CC[:, var, :].rearrange("p (d c k h) -> p d c k h", d=2, c=8, k=4)
        for gl in range(2):
            for k4 in range(4):
                o_ = WCv[64 * gl:64 * gl + 64, :, :, k4, 32 * k4 + 16 * gl:32 * k4 + 16 * gl + 16]
                i_ = CCv[64 * gl:64 * gl + 64, :, :, k4, :]
                if var == 0:
                    P.copy("dve", o_, i_, ["CC"], [("WC", var)])
                else:
                    P.ts("dve", o_, i_, -1.0, None, ALU.mult, None, ["CC"], [("WC", var)])
        P.dma("sp", d["s5wc"][js, :, var], WC, reads=[("WC", var)], writes=[("s5wc", js, var)])
    ar.release()
    P.barrier()
    ar.mark()
    p2 = ar.alloc([5, 1024], F32)
    P.dma("sp", p2, d["s5p2"][js], writes=[("g2", "in"), "BB"])
    sc = s5_scalars(P, K, ar, 1024, p2[:, 0, :], p2[:, 1, :], p2[:, 2, :], "g2", True)
    t1 = p2[:, 0, :]
    t2 = p2[:, 1, :]
    Bb = ar.alloc([1, 1024], F32)[:, 0, :]
    WB = ar.alloc([2, S5_G2, 128], BF16)
    WBv = WB.rearrange("p d (c k) (g m) -> p d c k g m", k=4, g=2)
    for var in range(2):
        x1, x2 = (p2[:, 3, :], p2[:, 4, :]) if var == 0 else (p2[:, 4, :], p2[:, 3, :])
        P.tt("dve", t1, sc["qre"], x1, ALU.mult, [("g2", "qre"), "BB", ("g2", "in"), ("g2", "lr")], ["bt1"])
        P.tt("dve", t2, sc["qim"], x2, ALU.mult, [("g2", "qim"), "BB", ("g2", "in"), ("g2", "ang"), ("g2", "den")], ["bt2"])
        P.tt("dve", Bb, t1, t2, ALU.subtract if var == 0 else ALU.add, ["bt1", "bt2"], ["Bb"])
        P.memset("pool", WB, 0.0, ["WB"])
        Bv = Bb.rearrange("p (d c m) -> p d c m", d=2, c=8)
        for k4 in range(4):
            for gl in range(2):
                P.ts("dve", WBv[:, :, :, k4, gl, :], Bv, K.mk8[:, 2 * k4 + gl:2 * k4 + gl + 1], None, ALU.mult, None,
                     ["Bb", "consts"], ["WB"])
        P.dma("sp", d["s5wb"][js, :, var], WB, reads=["WB"], writes=[("s5wb", js, var)])
    ar.release()
    P.barrier()


def emit_s5(P, K, l, b):
    js = l // 3
    ar = K.arena
    d = K.d
    G1 = K.MOD[:, l, 16:24, :]
    ar.mark()
    H = ar.alloc([NCH, LP], BF16)
    emit_norm_mod(P, K, K.A1, K.MOD[:, l, 0:8, :], l, b, H, keep_rstd=True)
    SH1 = K.MOD[:, l, 0:8, :]
    RSTD = K.RSTD
    Wc = [ar.alloc([2, 2, 2, 4, 128], BF16) if False else ar.alloc([4, 8, 128], BF16) for _ in range(2)]
    tabs = [ar.alloc([2, 128], F32) for _ in range(4)]
    NB2 = 4
    tq = [[ar.alloc([1, 128], F32)[:, 0, :] for _ in range(4)] for _ in range(NB2)]
    bt = [[ar.alloc([1, 128], F32)[:, 0, :] for _ in range(2)] for _ in range(NB2)]
    zz = [[ar.alloc([1, 128], F32)[:, 0, :] for _ in range(2)] for _ in range(NB2)]
    PP = [ar.alloc([4, 128], BF16) for _ in range(NB2)]
    car = [[ar.alloc([1, 4], F32)[:, 0, :] for _ in range(2)] for _ in range(2)]
    segs = [(0, 0, 256, 4)] + [(0, 256, 256, b)] + [(bk, 0, 512, b) for bk in (1, 2, 3)] + [(4, 0, 256, b)]
    Rt = K.S5R[:, js, 0, :]
    CAc = K.S5R[:, js, 1, :]
    CAs = K.S5R[:, js, 2, :]
    it = 0
    ntab = 0
    for c in range(NCH):
        wb = c % 2
        bank_started = set()
        for v in range(2):
            P.dma("sp", Wc[wb][:, v, :, :].rearrange("p (d k) m -> p d k m", d=2),
                  d["s5wb"][js, :, v].rearrange("p d (c k) m -> p d c k m", k=4)[:, :, c, :, :], writes=[("Wc", wb, v)])
            P.dma("sp", Wc[wb][:, 2 + v, :, :].rearrange("p (d k) m -> p d k m", d=2),
                  d["s5wc"][js, :, v].rearrange("p d (c k) m -> p d c k m", k=4)[:, :, c, :, :], writes=[("Wc", wb, 2 + v)])
        for k4 in range(4):
            streams = []
            for dr in range(2):
                cg = dr * 32 + 4 * c + k4
                tb = tabs[ntab % 4]
                tk_ = ("tabs", ntab % 4)
                ntab += 1
                P.dma("sp", tb, d["s5t"][js, :, :, cg, :], writes=[tk_])
                order = list(range(18)) if dr == 0 else [1, 0] + list(range(17, 1, -1))
                streams.append((dr, cg, tb, tk_, order))
            for oi in range(18):
                for (dr, cg, tb, tk_, order) in streams:
                    tt_ = order[oi]
                    u = it % NB2
                    it += 1
                    pc = tile_pcol(tt_)
                    pv = K.ps[5 + (it % 2)]
                    pvk = ("ps", 5 + (it % 2))
                    wi = dr * 4 + k4
                    P.mm(pv[:, 0:128], Wc[wb][:, 0, wi, :], H[:, c, pc:pc + 128], True, True, [("Wc", wb, 0), ("H", c)], [pvk])
                    P.mm(pv[:, 128:256], Wc[wb][:, 1, wi, :], H[:, c, pc:pc + 128], True, True, [("Wc", wb, 1), ("H", c)], [pvk])
                    if dr == 0:
                        vr = pv[:, 0:128]
                        vi = pv[:, 128:256]
                    else:
                        vr = pv[:, 127::-1]
                        vi = pv[:, 255:127:-1]
                    cosT = tb[:, 0, :]
                    sinT = tb[:, 1, :]
                    q = tq[u]
                    qk = [("tq", u, i) for i in range(4)]
                    P.tt("dve", q[0], vr, cosT, ALU.mult, [pvk, tk_], [qk[0]])
                    P.tt("dve", q[1], vi, sinT, ALU.mult, [pvk, tk_], [qk[1]])
                    P.tt("dve", q[2], vi, cosT, ALU.mult, [pvk, tk_], [qk[2]])
                    P.tt("dve", q[3], vr, sinT, ALU.mult, [pvk, tk_], [qk[3]])
                    P.tt("pool", bt[u][0], q[0], q[1], ALU.add, [qk[0], qk[1]], [("bt", u, 0)])
                    P.tt("pool", bt[u][1], q[2], q[3], ALU.subtract, [qk[2], qk[3]], [("bt", u, 1)])
                    Rb = Rt[:, cg:cg + 1].to_broadcast([128, 128])
                    cr = car[dr]
                    for ri in range(2):
                        init = 0.0 if oi == 0 else cr[ri][:, 0:1]
                        P.scan(zz[u][ri], Rb, bt[u][ri], init, [("bt", u, ri), ("S5R", js), ("car", dr, ri)], [("zz", u, ri)])
                    zr = zz[u][0]
                    zi = zz[u][1]
                    if oi < 17:
                        P.tt("pool", cr[0][:, 1:2], zi[:, 127:128], CAs[:, cg:cg + 1], ALU.mult, [("zz", u, 1), ("S5R", js)], [("cart", dr, 0)])
                        P.tt("pool", cr[1][:, 1:2], zi[:, 127:128], CAc[:, cg:cg + 1], ALU.mult, [("zz", u, 1), ("S5R", js)], [("cart", dr, 1)])
                        P.stt(cr[0][:, 0:1], zr[:, 127:128], CAc[:, cg:cg + 1], cr[0][:, 1:2], ALU.mult, ALU.subtract,
                              [("zz", u, 0), ("cart", dr, 0), ("S5R", js)], [("car", dr, 0)])
                        P.stt(cr[1][:, 0:1], zr[:, 127:128], CAs[:, cg:cg + 1], cr[1][:, 1:2], ALU.mult, ALU.add,
                              [("zz", u, 0), ("cart", dr, 1), ("S5R", js)], [("car", dr, 1)])
                    Pu = PP[u]

                    def po(i):
                        return Pu[:, i, :] if dr == 0 else Pu[:, i, 127::-1]
                    pk = [("PP", u, i) for i in range(4)]
                    P.tt("dve", po(0), zr, cosT, ALU.mult, [("zz", u, 0), tk_], [pk[0]])
                    P.stt(po(1), zi, -1.0, sinT, ALU.mult, ALU.mult, [("zz", u, 1), tk_], [pk[1]])
                    P.tt("pool", po(2), zr, sinT, ALU.mult, [("zz", u, 0), tk_], [pk[2]])
                    P.tt("pool", po(3), zi, cosT, ALU.mult, [("zz", u, 1), tk_], [pk[3]])
                    yb = K.ps[tt_ // 4]
                    ys = yb[:, (tt_ % 4) * 128:(tt_ % 4) * 128 + 128]
                    lastw = (k4 == 3 and dr == 1)
                    for i in range(4):
                        st = (tt_ // 4) not in bank_started
                        bank_started.add(tt_ // 4)
                        P.mm(ys, Wc[wb][:, 2 + (i // 2), wi, :], Pu[:, i, :], st, lastw and i == 3,
                             [("Wc", wb, 2 + i // 2), pk[i]], [("ps", tt_ // 4)])
        for si, (bk, o0, n, bcol) in enumerate(segs):
            x0 = bk * 512 + o0
            p0 = 1 + x0 if x0 < LCTX else 2 + x0
            f = [K.tmpB[i][:, 0:n] for i in range(3)]
            fk = [("tmpB", i) for i in range(3)]
            P.tt("dve", f[0], K.X[:, c, x0:x0 + n], RSTD[:, x0:x0 + n], ALU.mult, [("X", c)] + RSTD_KEYS, [fk[0]])
            P.act(f[0], f[0], AF.Identity, [fk[0], ("MOD", l), ("A", l)], [fk[0]],
                  scale=K.A1[:, l, c, bcol:bcol + 1], bias=SH1[:, c, bcol:bcol + 1])
            P.stt(f[1], f[0], K.s5d[:, js, c:c + 1], K.ps[bk][:, o0:o0 + n], ALU.mult, ALU.add,
                  [fk[0], ("ps", bk), "consts"], [fk[1]])
            P.tt("pool", f[2], f[1], f[1], ALU.mult, [fk[1]], [fk[2]])
            P.ts("pool", f[2], f[2], 0.044715, 1.0, ALU.mult, ALU.add, [fk[2]], [fk[2]])
            P.tt("pool", f[2], f[2], f[1], ALU.mult, [fk[2], fk[1]], [fk[2]])
            P.act(f[2], f[2], AF.Sigmoid, [fk[2]], [fk[2]], scale=GELU_C)
            P.tt("dve", H[:, c, p0:p0 + n], f[1], f[2], ALU.mult, [fk[1], fk[2]], [("H", c)])
    gw = [ar.alloc([2, NCH, 128], BF16) for _ in range(2)]
    sg = [ar.alloc([1, 512], F32)[:, 0, :] for _ in range(2)]
    gv = d["s5_glu_w"][js].rearrange("(kc p) n -> p kc n", p=128)
    n_ = 0
    for oc in range(NCH):
        wb = oc % 2
        P.dma("pool", gw[wb][:, 0, :, :], gv[:, :, oc * 128:(oc + 1) * 128], writes=[("gw", wb, 0)])
        P.dma("pool", gw[wb][:, 1, :, :], gv[:, :, D + oc * 128:D + (oc + 1) * 128], writes=[("gw", wb, 1)])
        for (x0, n, p0) in tok_blocks():
            u = n_ % 2
            n_ += 1
            for half in range(2):
                pst = K.ps[2 * half + u]
                for kc in range(NCH):
                    P.mm(pst[:, 0:n], gw[wb][:, half, kc, :], H[:, kc, p0:p0 + n], kc == 0, kc == NCH - 1,
                         [("gw", wb, half), ("H", kc)], [("ps", 2 * half + u)])
            P.act(sg[u][:, 0:n], K.ps[2 + u][:, 0:n], AF.Sigmoid, [("ps", 2 + u), "consts"], [("sg", u)],
                  bias=K.s5gb[:, js, 8 + oc:9 + oc])
            P.stt(sg[u][:, 0:n], K.ps[u][:, 0:n], K.s5gb[:, js, oc:oc + 1], sg[u][:, 0:n], ALU.add, ALU.mult,
                  [("ps", u), ("sg", u), "consts"], [("sg", u)])
            bcol = 4 if x0 == 0 else b
            P.stt(K.X[:, oc, x0:x0 + n], sg[u][:, 0:n], G1[:, oc, bcol:bcol + 1], K.X[:, oc, x0:x0 + n], ALU.mult, ALU.add,
                  [("sg", u), ("MOD", l), ("X", oc)], [("X", oc)])
    ar.release()
    P.barrier()


def emit_final(P, K, b):
    K.arena.mark()
    emit_rstd(P, K, "f")
    i = 0
    for bi, (c0, n, p0) in enumerate(tok_blocks()):
        if bi == 0:
            continue
        for c in range(NCH):
            tmp = K.tmpB[i % 3]
            tk = ("tmpB", i % 3)
            i += 1
            P.tt("dve", tmp[:, 0:n], K.X[:, c, c0:c0 + n], K.RSTD[:, c0:c0 + n], ALU.mult, [("X", c), ("rstd", bi)], [tk])
            P.act(tmp[:, 0:n], tmp[:, 0:n], AF.Identity, [tk, "consts"], [tk], scale=K.fg[:, c:c + 1])
            P.dma("sp", K.d["out"][b, c, :, c0 - LCTX:c0 - LCTX + n], tmp[:, 0:n], reads=[tk], writes=[("out", b, c, bi)])
    K.arena.release()
    P.barrier()


def build_program(NB=4, layers=(0, 1, 2, 3), mixers=True, final=True):
    nc = bass.Bass("TRN2", target_bir_lowering=False)
    d = {}

    def din(name, shape, dt=F32):
        d[name] = nc.dram_tensor(name, list(shape), dt, kind="ExternalInput").ap()

    din("xin", [NB, NCH, 128, L])
    din("cT", [128, NCH, 5])
    din("ada_w", [DEPTH, D, 6 * D])
    din("consts", [128, K_CONST_COLS])
    din("ffn_w_in", [DEPTH, D, 2 * DFF])
    din("ffn_w_out", [DEPTH, DFF, D])
    s5_js = sorted(set(l // 3 for l in layers if l % 3 == 0)) if mixers else []
    if s5_js:
        din("s5p1", [2, 128, 3, 64])
        din("s5p2", [2, 128, 5, 1024])
        din("s5c", [2, 128, 2, 1024])
        din("s5_glu_w", [2, D, 2 * D])
        d["s5t"] = nc.dram_tensor("s5t", [2, 128, 2, 64, 128], F32, kind="Internal").ap()
        d["s5wb"] = nc.dram_tensor("s5wb", [2, 128, 2, 2, S5_G2, 128], BF16, kind="Internal").ap()
        d["s5wc"] = nc.dram_tensor("s5wc", [2, 128, 2, 2, S5_G2, 128], BF16, kind="Internal").ap()
    if mixers and any(l % 3 == 1 for l in layers):
        din("ml_w_in", [1, D, 4 * D])
        din("ml_w_out", [1, D, D])
        din("ml_wif", [128, NCH, 16])
        for nm, shp in (("ml_sq", [ML_H, 128, 4, L]), ("ml_sv", [ML_H, 128, 18, 256]), ("ml_so", [ML_H, 128, 2, L])):
            d[nm] = nc.dram_tensor(nm, shp, BF16, kind="Internal").ap()
    if mixers and any(l % 3 == 2 for l in layers):
        din("na_w_qkv", [1, D, 3 * D])
        din("na_w_out", [1, D, D])
        din("na_bias", [1, NA_H, 128, NA_NT * 128])
    d["out"] = nc.dram_tensor("out", [NB, NCH, 128, LLAT], F32, kind="ExternalOutput").ap()

    with contextlib.ExitStack() as es:
        P = Prog(nc, es)
        K = Ctx()
        K.d = d
        K.NB = NB
        NBYTES = 204 * 1024
        big = es.enter_context(nc.sbuf_tensor("arena", [128, NBYTES // 4], F32))
        K.arena = Arena(big, NBYTES)
        ar = K.arena
        K.ps = [es.enter_context(nc.psum_tensor(f"ps{i}", [128, 512], F32)) for i in range(8)]
        K.X = ar.alloc([NCH, L], F32)
        cst = ar.alloc([1, K_CONST_COLS], F32)[:, 0, :]
        P.dma("sp", cst, d["consts"][:, :], writes=["consts"])
        o = 0

        def take(n, shape=None):
            nonlocal o
            v = cst[:, o:o + n]
            o += n
            return v
        K.ones = take(128)
        K.epsc = take(1)
        K.adab = take(DEPTH * 48).rearrange("p (l j) -> p l j", l=DEPTH)
        K.n1g = take(DEPTH * NCH).rearrange("p (l c) -> p l c", l=DEPTH)
        K.n2g = take(DEPTH * NCH).rearrange("p (l c) -> p l c", l=DEPTH)
        K.fg = take(NCH)
        K.cw = take(DEPTH * 44 * 3).rearrange("p (l c k) -> p l c k", l=DEPTH, c=44)
        K.onec = take(1)
        K.ident = take(128)
        K.tri = take(256).rearrange("p (d t) -> p d t", d=2)
        K.negm = take(256).rearrange("p (d t) -> p d t", d=2)
        K.mlbif = take(16)
        K.mlcw = take(48).rearrange("p (c k) -> p c k", k=3)
        K.mlng = take(NCH)
        K.ropeq = take(128).rearrange("p (t c) -> p t c", t=2)
        K.ropek = take(128).rearrange("p (t c) -> p t c", t=2)
        K.halfpi = take(1)
        K.mk8 = take(8)
        K.s5d = take(2 * NCH).rearrange("p (j c) -> p j c", j=2)
        K.s5gb = take(2 * 16).rearrange("p (j c) -> p j c", j=2)
        assert o == K_CONST_COLS, (o, K_CONST_COLS)
        K.MOD = ar.alloc([DEPTH, 48, 5], F32)
        K.A1 = ar.alloc([DEPTH, NCH, 5], F32)
        K.A2 = ar.alloc([DEPTH, NCH, 5], F32)
        K.tmpB = [ar.alloc([1, 512], F32)[:, 0, :] for _ in range(3)]
        K.sq = [ar.alloc([1, 512], F32)[:, 0, :] for _ in range(2)]

        K.S5R = ar.alloc([2, 3, 64], F32)
        emit_mods(P, K)
        for js in s5_js:
            emit_s5_gen(P, K, js)
        for b in range(NB):
            for c in range(NCH):
                P.dma("sp", K.X[:, c, :], d["xin"][b, c], writes=[("X", c)])
            for l in layers:
                if mixers:
                    if l % 3 == 2:
                        emit_na(P, K, l, b)
                    if l % 3 == 1:
                        emit_mlstm(P, K, l, b)
                    if l % 3 == 0:
                        emit_s5(P, K, l, b)
                emit_ffn(P, K, l, b)
            if final:
                emit_final(P, K, b)
            else:
                for c in range(NCH):
                    P.dma("sp", d["out"][b, c], K.X[:, c, LCTX:L], reads=[("X", c)], writes=[("out", b, c)])
                P.barrier()
        P.emit()
    nc._used_inputs = set(d.keys()) - {"out", "ml_sq", "ml_sv", "ml_so", "s5t", "s5wb", "s5wc"}
    return nc


K_CONST_COLS = 128 + 1 + DEPTH * 48 + DEPTH * NCH * 2 + NCH + DEPTH * 44 * 3 + 1 + 128 + 256 + 256 + 16 + 48 + NCH + 128 + 128 + 1 + 8 + 16 + 32


def make_consts(inp):
    cols = []
    cols.append(np.full((128, 128), 1.0 / D, np.float32))
    cols.append(np.full((128, 1), EPS, np.float32))
    ab = np.asarray(inp["ada_b"], np.float32).reshape(DEPTH, 48, 128).transpose(2, 0, 1).reshape(128, -1)
    cols.append(ab)
    for nm in ("norm1_g", "norm2_g"):
        g = np.asarray(inp[nm], np.float32).reshape(DEPTH, NCH, 128).transpose(2, 0, 1).reshape(128, -1)
        cols.append(g)
    cols.append(np.asarray(inp["final_g"], np.float32).reshape(NCH, 128).T)
    cw = np.asarray(inp["ffn_conv"], np.float32).reshape(DEPTH, 3, 44, 128).transpose(3, 0, 2, 1).reshape(128, -1)
    cols.append(cw)
    cols.append(np.ones((128, 1), np.float32))
    cols.append(np.eye(128, dtype=np.float32))
    si = np.arange(128)[:, None]
    ti = np.arange(128)[None, :]
    trif = (si <= ti).astype(np.float32)
    trib = (si >= ti).astype(np.float32)
    cols.append(np.concatenate([trif, trib], axis=1))
    cols.append(np.concatenate([(1 - trif) * NEGM, (1 - trib) * NEGM], axis=1).astype(np.float32))
    bif = np.asarray(inp["ml_b_if"], np.float32)[0].reshape(1, 16)
    cols.append(np.repeat(bif, 128, axis=0))
    mc_ = np.asarray(inp["ml_conv"], np.float32)[0]
    cwm = np.empty((128, 16, 3), np.float32)
    for qk in range(2):
        for hd in range(ML_H):
            base = qk * D + hd * ML_DH
            for ab in range(2):
                idx = np.concatenate([base + 64 * ab + np.arange(64), base + 128 + 64 * ab + np.arange(64)])
                cwm[:, (qk * ML_H + hd) * 2 + ab, :] = mc_[:, idx].T
    cols.append(cwm.reshape(128, 48))
    cols.append(np.asarray(inp["ml_norm_g"], np.float32)[0].reshape(NCH, 128).T)
    inv = (10000.0 ** (-np.arange(64, dtype=np.float32) / 64.0)).astype(np.float32)
    pos = np.arange(64, dtype=np.float32)
    ang = (pos[None, :] * inv[:, None]).astype(np.float32)
    tab = np.stack([np.cos(ang), np.sin(ang)], axis=1).astype(np.float32)
    tab = np.concatenate([tab, tab], axis=0)
    cols.append((tab / np.float32(16.0)).reshape(128, 128).astype(np.float32))
    cols.append(tab.reshape(128, 128))
    cols.append(np.full((128, 1), np.pi / 2, np.float32))
    cols.append((np.arange(128)[:, None] // 16 == np.arange(8)[None, :]).astype(np.float32))
    cols.append(np.asarray(inp["s5_d"], np.float32).reshape(2, NCH, 128).transpose(2, 0, 1).reshape(128, 16))
    cols.append(np.asarray(inp["s5_glu_b"], np.float32).reshape(2, 16, 128).transpose(2, 0, 1).reshape(128, 32))
    out = np.ascontiguousarray(np.concatenate(cols, axis=1), dtype=np.float32)
    assert out.shape[1] == K_CONST_COLS
    return out


def make_s5_params(inp):
    f = lambda k: np.asarray(inp[k], np.float32)
    lre, lim, ldt = f("s5_lam_re"), f("s5_lam_im"), f("s5_log_dt")
    bre, bim, cre, cim = f("s5_b_re"), f("s5_b_im"), f("s5_c_re"), f("s5_c_im")

    def lay1(a):
        a = a.reshape(2, 2, 32, 2, 64).transpose(0, 3, 4, 1, 2)
        return a.reshape(2, 128, 64)

    def lay2(a):
        a = a.reshape(2, 2, 8, 8, 64).transpose(0, 3, 1, 2, 4)
        a = np.broadcast_to(a[:, :, None], (2, 8, 16, 2, 8, 64))
        return a.reshape(2, 128, 1024)
    ldt4 = np.broadcast_to(ldt[..., None], (2, 2, 64, 64))
    p1 = np.stack([lay1(lre), lay1(lim), lay1(ldt4)], axis=2)
    b2 = []
    for bb in (bre, bim):
        a = bb.reshape(2, 2, 8, 8, 64, 16).transpose(0, 3, 5, 1, 2, 4)
        b2.append(a.reshape(2, 128, 1024))
    p2 = np.stack([lay2(lre), lay2(lim), lay2(ldt4), b2[0], b2[1]], axis=2)
    cc = []
    for c_ in (cre, cim):
        a = c_.reshape(2, 2, 32, 2, 16, 64).transpose(0, 3, 5, 1, 2, 4)
        cc.append(a.reshape(2, 128, 1024))
    s5c = np.stack(cc, axis=2)
    return (np.ascontiguousarray(p1, dtype=np.float32), np.ascontiguousarray(p2, dtype=np.float32),
            np.ascontiguousarray(s5c, dtype=np.float32))


def make_in_maps(inp, NB, ncores):
    x = np.asarray(inp["x"], np.float32)
    ctx = np.asarray(inp["ctx"], np.float32)
    c = np.asarray(inp["c"], np.float32)
    c_ctx = np.asarray(inp["c_ctx"], np.float32)
    consts = make_consts(inp)
    s5p1, s5p2, s5c = make_s5_params(inp)
    shared = {
        "s5p1": s5p1, "s5p2": s5p2, "s5c": s5c,
        "s5_glu_w": np.ascontiguousarray(inp["s5_glu_w"], dtype=np.float32),
        "ada_w": np.ascontiguousarray(inp["ada_w"], dtype=np.float32),
        "consts": consts,
        "ffn_w_in": np.ascontiguousarray(inp["ffn_w_in"], dtype=np.float32),
        "ffn_w_out": np.ascontiguousarray(inp["ffn_w_out"], dtype=np.float32),
        "ml_w_in": np.ascontiguousarray(inp["ml_w_in"], dtype=np.float32),
        "ml_w_out": np.ascontiguousarray(inp["ml_w_out"], dtype=np.float32),
        "ml_wif": np.ascontiguousarray(np.asarray(inp["ml_w_if"], np.float32)[0].transpose(1, 0, 2).reshape(NCH, 128, 16).transpose(1, 0, 2)),
        "na_w_qkv": np.ascontiguousarray(inp["na_w_qkv"], dtype=np.float32),
        "na_w_out": np.ascontiguousarray(inp["na_w_out"], dtype=np.float32),
        "na_bias": make_na_bias(np.asarray(inp["na_rpb"])[0])[None],
    }
    maps = []
    for core in range(ncores):
        b0 = core * NB
        xin = np.empty((NB, D, L), np.float32)
        for i in range(NB):
            xin[i, :, :LCTX] = ctx[b0 + i].T
            xin[i, :, LCTX:] = x[b0 + i].T
        cc = np.concatenate([c[b0:b0 + NB], np.zeros((4 - NB, D), np.float32), c_ctx[None]], axis=0)
        cT = np.ascontiguousarray(cc.reshape(5, NCH, 128).transpose(2, 1, 0))
        m = dict(shared)
        m["xin"] = xin.reshape(NB, NCH, 128, L)
        m["cT"] = cT
        maps.append(m)
    return maps


def gather_out(res, NB, ncores):
    outs = []
    for core in range(ncores):
        o = np.asarray(res.results[core]["out"]).reshape(NB, D, LLAT)
        outs.append(np.ascontiguousarray(o.transpose(0, 2, 1)))
    return np.concatenate(outs, axis=0)


def nc_inputs(nc):
    return getattr(nc, "_used_inputs")


def kernel(**inputs):
    NB = 1
    nc = build_program(NB=NB)
    outs = []
    x = np.asarray(inputs["x"]); ctx = np.asarray(inputs["ctx"]); c = np.asarray(inputs["c"])
    per = x.shape[0] // NCORES
    res_all = np.empty((x.shape[0], LLAT, D), np.float32)
    for g in range(per):
        idx = [core * per + g for core in range(NCORES)]
        sub = dict(inputs)
        sub["x"] = x[idx]
        sub["ctx"] = ctx[idx]
        sub["c"] = c[idx]
        maps = make_in_maps(sub, NB, NCORES)
        maps = [{k: v for k, v in m.items() if k in nc_inputs(nc)} for m in maps]
        res = run_bass_kernel_spmd(nc, maps, core_ids=list(range(NCORES)))
        o = gather_out(res, NB, NCORES)
        for i, bi in enumerate(idx):
            res_all[bi] = o[i]
    return res_all.astype(np.float32)
```

```python
import contextlib
import numpy as np
import concourse.bass as bass
import concourse.mybir as mybir
from concourse.bass_utils import run_bass_kernel_spmd

F32 = mybir.dt.float32
BF16 = mybir.dt.bfloat16
I32 = mybir.dt.int32
AF = mybir.ActivationFunctionType
ALU = mybir.AluOpType
ENGS = ("pe", "act", "dve", "pool", "sp")
NDMASEM = 12

D = 1024
NCH = 8
LCTX = 256
LLAT = 2048
L = LCTX + LLAT
LP = L + 3
DEPTH = 4
DFF = 2816
NJ = DFF // 128
EPS = 1e-6
NCORES = 8


class Prog:
    def __init__(self, nc, es):
        self.nc = nc
        self.es = es
        self.ops = {e: [] for e in ENGS}
        self.cnt = {e: 0 for e in ENGS}
        self.sems = {}
        for e in ENGS:
            self.sems[("e", e)] = es.enter_context(nc.semaphore("sem_" + e))
        self.dman = {e: 0 for e in ENGS}
        for e in ("sp", "pool", "act"):
            for i in range(NDMASEM):
                self.sems[("d", e, i)] = es.enter_context(nc.semaphore(f"dsem_{e}_{i}"))
        self.seen = {e: {} for e in ENGS}
        self.bw = {}
        self.br = {}

    def _deps(self, eng, reads, writes, extra=()):
        d = {}

        def add(ev):
            if ev is None:
                return
            sk, v = ev
            if d.get(sk, 0) < v:
                d[sk] = v
        for r in reads:
            add(self.bw.get(r))
        for w in writes:
            add(self.bw.get(w))
            for sk, v in self.br.get(w, {}).items():
                add((sk, v))
        for ev in extra:
            add(ev)
        out = []
        seen = self.seen[eng]
        for sk, v in d.items():
            if eng == "pe" and sk == ("e", "pe"):
                continue
            if seen.get(sk, 0) >= v:
                continue
            seen[sk] = v
            out.append((sk, v))
        return out

    def _commit(self, ev, reads, writes):
        sk, v = ev
        for r in reads:
            self.br.setdefault(r, {})[sk] = v
        for w in writes:
            self.bw[w] = ev
            self.br[w] = {}

    def op(self, eng, fn, reads=(), writes=()):
        reads = list(reads)
        writes = list(writes)
        waits = self._deps(eng, reads, writes)
        self.cnt[eng] += 1
        ev = (("e", eng), self.cnt[eng])
        self.ops[eng].append((waits, fn, (("e", eng), 1)))
        self._commit(ev, reads, writes)
        return ev

    def dma(self, eng, out, in_, reads=(), writes=(), **kw):
        reads = list(reads)
        writes = list(writes)
        n = self.dman[eng]
        self.dman[eng] += 1
        slot = n % NDMASEM
        use = n // NDMASEM + 1
        sk = ("d", eng, slot)
        extra = [(sk, 16 * (use - 1))] if use > 1 else []
        waits = self._deps(eng, reads, writes, extra)
        ev = (sk, 16 * use)
        self.ops[eng].append((waits, (lambda e: e.dma_start(out=out, in_=in_, **kw)), (sk, 16)))
        self._commit(ev, reads, writes)
        return ev

    def barrier(self):
        evs = [(("e", e), self.cnt[e]) for e in ENGS if self.cnt[e] > 0]
        for e in ("sp", "pool", "act"):
            n = self.dman[e]
            for slot in range(min(n, NDMASEM)):
                uses = (n - 1 - slot) // NDMASEM + 1
                evs.append((("d", e, slot), 16 * uses))
        for eng in ENGS:
            waits = []
            for sk, v in evs:
                if sk == ("e", eng) and eng == "pe":
                    continue
                if self.seen[eng].get(sk, 0) >= v:
                    continue
                self.seen[eng][sk] = v
                waits.append((sk, v))
            if waits:
                self.ops[eng].append((waits, None, None))
        self.bw = {}
        self.br = {}

    def wait_all(self, eng, keys):
        waits = self._deps(eng, keys, [])
        self.ops[eng].append((waits, None, None))

    def emit(self):
        nc = self.nc
        with nc.Block() as block:
            def run(engobj, name):
                for waits, fn, inc in self.ops[name]:
                    attach = None
                    if fn is not None and name != "pe" and inc is not None and inc[0][0] == "e" and waits:
                        attach = waits[-1]
                        waits = waits[:-1]
                    for sk, v in waits:
                        engobj.wait_ge(self.sems[sk], v)
                    if fn is None:
                        continue
                    ins = fn(engobj)
                    if attach is not None:
                        ins._wait_ge(self.sems[attach[0]], attach[1])
                    if inc is not None:
                        ins.then_inc(self.sems[inc[0]], inc[1])

            @block.sync
            def _(e):
                run(e, "sp")

            @block.scalar
            def _(e):
                run(e, "act")

            @block.vector
            def _(e):
                run(e, "dve")

            @block.gpsimd
            def _(e):
                run(e, "pool")

            @block.tensor
            def _(e):
                run(e, "pe")

    def mm(self, out, lhsT, rhs, start, stop, reads, writes):
        return self.op("pe", lambda e: e.matmul(out, lhsT=lhsT, rhs=rhs, start=start, stop=stop), reads, writes)

    def tr(self, out, in_, ident, reads, writes):
        return self.op("pe", lambda e: e.transpose(out, in_, ident), reads, writes)

    def act(self, out, in_, func, reads, writes, scale=None, bias=None):
        kw = {}
        if scale is not None:
            kw["scale"] = scale
        if bias is not None:
            kw["bias"] = bias
        return self.op("act", lambda e: e.activation(out=out, in_=in_, func=func, **kw), reads, writes)

    def ts(self, eng, out, in0, s1, s2, op0, op1, reads, writes):
        if op1 is None:
            return self.op(eng, lambda e: e.tensor_scalar(out=out, in0=in0, scalar1=s1, scalar2=None, op0=op0), reads, writes)
        return self.op(eng, lambda e: e.tensor_scalar(out=out, in0=in0, scalar1=s1, scalar2=s2, op0=op0, op1=op1), reads, writes)

    def stt(self, out, in0, scalar, in1, op0, op1, reads, writes):
        return self.op("dve", lambda e: e.scalar_tensor_tensor(out=out, in0=in0, scalar=scalar, in1=in1, op0=op0, op1=op1), reads, writes)

    def tt(self, eng, out, in0, in1, op, reads, writes):
        return self.op(eng, lambda e: e.tensor_tensor(out=out, in0=in0, in1=in1, op=op), reads, writes)

    def copy(self, eng, out, in_, reads, writes):
        if eng == "act":
            return self.op("act", lambda e: e.copy(out=out, in_=in_), reads, writes)
        return self.op(eng, lambda e: e.tensor_copy(out=out, in_=in_), reads, writes)

    def memset(self, eng, ap, val, writes):
        return self.op(eng, lambda e: e.memset(ap, val), (), writes)

    def recip(self, out, in_, reads, writes):
        return self.op("dve", lambda e: e.reciprocal(out=out, in_=in_), reads, writes)

    def scan(self, out, d0, d1, init, reads, writes):
        return self.op("dve", lambda e: e.tensor_tensor_scan(out=out, data0=d0, data1=d1, initial=init, op0=ALU.mult, op1=ALU.add), reads, writes)


class Arena:
    def __init__(self, t, nbytes):
        self.t = t
        self.n = nbytes
        self.off = 0
        self.marks = []

    def mark(self):
        self.marks.append(self.off)

    def release(self):
        self.off = self.marks.pop()

    def alloc(self, shape, dtype):
        esz = 2 if dtype == BF16 else 4
        n = int(np.prod(shape)) * esz
        n4 = (n + 63) // 64 * 64
        assert self.off + n4 <= self.n, f"arena overflow {self.off}+{n4}>{self.n}"
        a = self.t[:, self.off // 4:(self.off + n4) // 4]
        self.off += n4
        if dtype != F32:
            a = a.bitcast(dtype)
        a = a[:, 0:int(np.prod(shape))]
        if len(shape) == 2:
            return a.rearrange("p (a b) -> p a b", a=shape[0])
        if len(shape) == 3:
            return a.rearrange("p (a b c) -> p a b c", a=shape[0], b=shape[1])
        return a


class Ctx:
    pass


def ffn_blocks():
    blks = [(0, LCTX + 2)]
    sizes = [410, 410, 410, 410, 408]
    s = 0
    for sz in sizes:
        blks.append((LCTX + 1 + s, sz + 2))
        s += sz
    return blks


def tok_blocks():
    out = [(0, LCTX, 1)]
    for i in range(4):
        out.append((LCTX + 512 * i, 512, LCTX + 2 + 512 * i))
    return out


def emit_mods(P, K):
    nc = P.nc
    ar = K.arena
    ar.mark()
    cs = ar.alloc([NCH, 5], F32)
    P.dma("sp", cs, K.d["cT"][:, :, :], writes=["cs"])
    P.act(cs, cs, AF.Silu, ["cs"], ["cs"])
    wt = [ar.alloc([NCH, 512], F32) for _ in range(3)]
    n = 0
    for l in range(DEPTH):
        wv = K.d["ada_w"][l].rearrange("(kc p) n -> p kc n", p=128)
        psb = K.ps[l % 2]
        for jg in range(12):
            buf = n % 3
            n += 1
            P.dma("sp", wt[buf], wv[:, :, jg * 512:(jg + 1) * 512], writes=[("adaw", buf)])
            for jj in range(4):
                j = jg * 4 + jj
                for kc in range(NCH):
                    P.mm(psb[:, j * 5:(j + 1) * 5], wt[buf][:, kc, jj * 128:(jj + 1) * 128], cs[:, kc, :],
                         kc == 0, kc == NCH - 1, [("adaw", buf), "cs"], [("ps", l % 2)])
        P.tt("dve", K.MOD[:, l, :, :], psb[:, 0:240].rearrange("p (j b) -> p j b", b=5),
             K.adab[:, l, :].unsqueeze(2).to_broadcast([128, 48, 5]), ALU.add,
             [("ps", l % 2), "consts"], [("MOD", l)])
        for (dst, g, s) in ((K.A1, K.n1g, 1), (K.A2, K.n2g, 4)):
            P.ts("dve", dst[:, l, :, :], K.MOD[:, l, s * 8:(s + 1) * 8, :], 1.0, None, ALU.add, None,
                 [("MOD", l)], [("A", l)])
            P.tt("dve", dst[:, l, :, :], dst[:, l, :, :], g[:, l, :].unsqueeze(2).to_broadcast([128, NCH, 5]),
                 ALU.mult, [("A", l), "consts"], [("A", l)])
    ar.release()
    P.barrier()


def emit_rstd(P, K, tagp):
    K.RSTD = K.arena.alloc([1, L], F32)[:, 0, :]
    for bi, (c0, n, _) in enumerate(tok_blocks()):
        pb = K.ps[7]
        for c in range(NCH):
            sq = K.sq[c % 2]
            P.act(sq[:, 0:n], K.X[:, c, c0:c0 + n], AF.Square, [("X", c)], [("sq", c % 2)])
            P.mm(pb[:, 0:n], K.ones, sq[:, 0:n], c == 0, c == NCH - 1, [("sq", c % 2), "consts"], [("ps", 7)])
        P.act(K.RSTD[:, c0:c0 + n], pb[:, 0:n], AF.Sqrt, [("ps", 7), "consts"], [("rstd", bi)], bias=K.epsc[:, 0:1])
        P.recip(K.RSTD[:, c0:c0 + n], K.RSTD[:, c0:c0 + n], [("rstd", bi)], [("rstd", bi)])


RSTD_KEYS = [("rstd", i) for i in range(5)]


def emit_norm_mod(P, K, A, SH, l, b, H, keep_rstd=False):
    K.arena.mark()
    emit_rstd(P, K, "n")
    i = 0
    for bi, (c0, n, p0) in enumerate(tok_blocks()):
        bcol = 4 if bi == 0 else b
        for c in range(NCH):
            tmp = K.tmpB[i % 3]
            tk = ("tmpB", i % 3)
            i += 1
            P.tt("dve", tmp[:, 0:n], K.X[:, c, c0:c0 + n], K.RSTD[:, c0:c0 + n], ALU.mult, [("X", c), ("rstd", bi)], [tk])
            P.act(H[:, c, p0:p0 + n], tmp[:, 0:n], AF.Identity, [tk, ("MOD", l), ("A", l)], [("H", c)],
                  scale=A[:, l, c, bcol:bcol + 1], bias=SH[:, c, bcol:bcol + 1])
    if keep_rstd:
        K.arena.marks.pop()
    else:
        K.arena.release()
        P.barrier()


def zero_pads(P, H, nchunks, keyname):
    for col in (0, LCTX + 1, LP - 1):
        P.memset("pool", H[:, :, col:col + 1], 0.0, [(keyname, c) for c in range(nchunks)])


def emit_ffn(P, K, l, b):
    ar = K.arena
    ar.mark()
    H = ar.alloc([NCH, LP], BF16)
    zero_pads(P, H, NCH, "H")
    emit_norm_mod(P, K, K.A2, K.MOD[:, l, 24:32, :], l, b, H)
    G2 = K.MOD[:, l, 40:48, :]
    groups = [(0, 4), (4, 4), (8, 4), (12, 4), (16, 3), (19, 3)]
    GM = 4
    M = ar.alloc([GM, LP], BF16)
    win = [ar.alloc([2, NCH, 128], BF16) for _ in range(3)]
    wout = [ar.alloc([GM, D], BF16) for _ in range(2)]
    acc = [[ar.alloc([1, 412], F32) for _ in range(3)] for _ in range(2)]
    wiv = K.d["ffn_w_in"][l].rearrange("(kc p) n -> p kc n", p=128)
    wov = K.d["ffn_w_out"][l].rearrange("(j p) n -> p j n", p=128)
    fb = ffn_blocks()
    nw = 0
    nacc = 0
    nps = 0
    for gi, (j0, nj) in enumerate(groups):
        wo = wout[gi % 2]
        P.dma("pool", wo[:, 0:nj, :], wov[:, j0:j0 + nj, :], writes=[("wout", gi % 2)])
        for jl in range(nj):
            j = j0 + jl
            wb = nw % 3
            nw += 1
            P.dma("pool", win[wb][:, 0, :, :], wiv[:, :, j * 128:(j + 1) * 128], writes=[("win", wb, 0)])
            P.dma("pool", win[wb][:, 1, :, :], wiv[:, :, (NJ + j) * 128:(NJ + j + 1) * 128], writes=[("win", wb, 1)])
            for (c0, n) in fb:
                pa = nps % 2
                nps += 1
                ab = nacc % 2
                nacc += 1
                for half in range(2):
                    pst = K.ps[2 * half + pa]
                    for kc in range(NCH):
                        P.mm(pst[:, 0:n], win[wb][:, half, kc, :], H[:, kc, c0:c0 + n], kc == 0, kc == NCH - 1,
                             [("win", wb, half), ("H", kc)], [("ps", 2 * half + pa)])
                no = n - 2
                accs = acc[ab]
                for half in range(2):
                    pst = K.ps[2 * half + pa]
                    ch = j if half == 0 else NJ + j
                    a = accs[half][:, 0, 0:no]
                    kr = [("ps", 2 * half + pa), "consts"]
                    kw = [("acc", ab, half)]
                    P.act(a, pst[:, 1:1 + no], AF.Identity, kr, kw, scale=K.cw[:, l, ch, 1:2])
                    P.stt(a, pst[:, 0:no], K.cw[:, l, ch, 0:1], a, ALU.mult, ALU.add, kr + kw, kw)
                    P.stt(a, pst[:, 2:2 + no], K.cw[:, l, ch, 2:3], a, ALU.mult, ALU.add, kr + kw, kw)
                sg = accs[2][:, 0, 0:no]
                P.act(sg, accs[1][:, 0, 0:no], AF.Silu, [("acc", ab, 1)], [("acc", ab, 2)])
                P.tt("pool", M[:, jl, c0 + 1:c0 + 1 + no], accs[0][:, 0, 0:no], sg, ALU.mult,
                     [("acc", ab, 0), ("acc", ab, 2)], [("M", jl)])
        for oc in range(NCH):
            for (x0, n, p0) in tok_blocks():
                pi = 4 + (nps % 2)
                nps += 1
                for jl in range(nj):
                    P.mm(K.ps[pi][:, 0:n], wo[:, jl, oc * 128:(oc + 1) * 128], M[:, jl, p0:p0 + n], jl == 0, jl == nj - 1,
                         [("wout", gi % 2), ("M", jl)], [("ps", pi)])
                bcol = 4 if x0 == 0 else b
                P.stt(K.X[:, oc, x0:x0 + n], K.ps[pi][:, 0:n], G2[:, oc, bcol:bcol + 1], K.X[:, oc, x0:x0 + n],
                      ALU.mult, ALU.add, [("ps", pi), ("MOD", l), ("X", oc)], [("X", oc)])
    ar.release()
    P.barrier()


NA_H = 16
GRID_W = 64
NA_NT = 21


def na_qtile_info(i):
    if i == 0:
        return list(range(0, 4)), 5
    if i == 1:
        return list(range(0, 4)), 9
    if i == 14:
        return list(range(12, 16)), 13
    if i == 15:
        return list(range(12, 16)), 17
    return list(range(i - 2, i + 3)), 0


def make_na_bias(rpb):
    rpb = np.asarray(rpb, np.float32)
    tiles = [(5, j) for j in range(3, 8)]
    for i in (0, 1):
        tiles += [(i, j) for j in range(0, 4)]
    for i in (14, 15):
        tiles += [(i, j) for j in range(12, 16)]
    assert len(tiles) == NA_NT
    out = np.empty((NA_H, 128, NA_NT, 128), np.float32)
    a = np.arange(2)[:, None]
    col = np.arange(64)[None, :]
    for ti, (i, j) in enumerate(tiles):
        kr = (2 * j + a + 0 * col).reshape(128)
        kc = (0 * a + col).reshape(128)
        qr = (2 * i + a + 0 * col).reshape(128)
        qc = kc.copy()
        r0 = np.clip(qr - 4, 0, 24)
        c0 = np.clip(qc - 8, 0, 48)
        valid = ((kr[:, None] >= r0[None, :]) & (kr[:, None] <= r0[None, :] + 7)
                 & (kc[:, None] >= c0[None, :]) & (kc[:, None] < c0[None, :] + 16))
        dr = np.clip(kr[:, None] - qr[None, :] + 7, 0, 14)
        dc = np.clip(kc[:, None] - qc[None, :], -15, 15) + 15
        vals = rpb[:, dr, dc]
        out[:, :, ti, :] = np.where(valid[None], vals, np.float32(-30000.0))
    return np.ascontiguousarray(out.reshape(NA_H, 128, NA_NT * 128))


def emit_down_proj(P, K, wo, OT, G, b, l, nk, wkey, okey):
    n_ = 0
    for oc in range(NCH):
        for (x0, n, p0) in tok_blocks():
            pi = 4 + (n_ % 2)
            n_ += 1
            for kc in range(nk):
                P.mm(K.ps[pi][:, 0:n], wo[:, kc, oc * 128:(oc + 1) * 128], OT[:, kc, x0:x0 + n], kc == 0, kc == nk - 1,
                     [wkey, (okey, kc)], [("ps", pi)])
            bcol = 4 if x0 == 0 else b
            P.stt(K.X[:, oc, x0:x0 + n], K.ps[pi][:, 0:n], G[:, oc, bcol:bcol + 1], K.X[:, oc, x0:x0 + n],
                  ALU.mult, ALU.add, [("ps", pi), ("MOD", l), ("X", oc)], [("X", oc)])


def emit_na(P, K, l, b):
    jn = l // 3
    ar = K.arena
    ar.mark()
    H = ar.alloc([NCH, LP], BF16)
    emit_norm_mod(P, K, K.A1, K.MOD[:, l, 0:8, :], l, b, H)
    G1 = K.MOD[:, l, 16:24, :]
    OTs = [ar.alloc([1, L], BF16) for _ in range(2)]
    wos = [ar.alloc([1, D], BF16) for _ in range(2)]
    wqkv = [ar.alloc([3, NCH, 128], BF16) for _ in range(2)]
    Qt = ar.alloc([1, L], BF16)[:, 0, :]
    Kt = ar.alloc([1, L], BF16)[:, 0, :]
    V = ar.alloc([18, 128], BF16)
    BI1 = ar.alloc([NA_NT, 128], F32)
    BI = [BI1, BI1]
    Tb = [ar.alloc([1, 640], F32)[:, 0, :] for _ in range(2)]
    PT = [ar.alloc([1, 896], BF16)[:, 0, :] for _ in range(2)]
    rden = [ar.alloc([1, 128], F32)[:, 0, :] for _ in range(2)]
    onesb = ar.alloc([1, 128], BF16)[:, 0, :]
    P.memset("pool", onesb, 1.0, ["onesb"])
    wv_ = K.d["na_w_qkv"][jn].rearrange("(kc p) n -> p kc n", p=128)
    nu = 0
    npj = 0
    wov_ = K.d["na_w_out"][jn].rearrange("(kc p) n -> p kc n", p=128)
    for hp in range(NCH):
        wb = hp % 2
        OT = OTs[wb]
        for t3 in range(3):
            P.dma("pool", wqkv[wb][:, t3, :, :], wv_[:, :, t3 * D + hp * 128:t3 * D + (hp + 1) * 128], writes=[("wqkv", wb, t3)])
        P.dma("pool", wos[wb], wov_[:, hp:hp + 1, :], writes=[("wos", wb)])
        for (x0, n, p0) in tok_blocks():
            for t3, dst, dk_ in ((0, Qt, "Qt"), (1, Kt, "Kt")):
                pi = 6 + (npj % 2)
                npj += 1
                for kc in range(NCH):
                    P.mm(K.ps[pi][:, 0:n], wqkv[wb][:, t3, kc, :], H[:, kc, p0:p0 + n], kc == 0, kc == NCH - 1,
                         [("wqkv", wb, t3), ("H", kc)], [("ps", pi)])
                if t3 == 0:
                    P.act(dst[:, x0:x0 + n], K.ps[pi][:, 0:n], AF.Identity, [("ps", pi)], [dk_], scale=0.125)
                else:
                    P.copy("dve", dst[:, x0:x0 + n], K.ps[pi][:, 0:n], [("ps", pi)], [dk_])
        for tt in range(18):
            pc = 1 + 128 * tt if tt < 2 else LCTX + 2 + 128 * (tt - 2)
            pi = 6 + (npj % 2)
            npj += 1
            for kc in range(NCH):
                P.mm(K.ps[pi][:, 0:128], H[:, kc, pc:pc + 128], wqkv[wb][:, 2, kc, :], kc == 0, kc == NCH - 1,
                     [("wqkv", wb, 2), ("H", kc)], [("ps", pi)])
            P.copy("act", V[:, tt, :], K.ps[pi][:, 0:128], [("ps", pi)], ["V"])
        for hh in range(2):
            h = 2 * hp + hh
            hb = 64 * hh
            P.dma("sp", BI[hh], K.d["na_bias"][jn, h].rearrange("p (t q) -> p t q", q=128), writes=[("BI", 0)])
            for qt in range(18):
                u = nu % 2
                nu += 1
                if qt < 2:
                    lat_k, base = [], 0
                else:
                    lat_k, base = na_qtile_info(qt - 2)
                nk = len(lat_k)
                ktiles = [2 + j for j in lat_k] + [0, 1]
                qs = slice(qt * 128, (qt + 1) * 128)

                def sreg(i0, i1):
                    assert i0 // 4 == (i1 - 1) // 4
                    bk = 2 * u + i0 // 4
                    return K.ps[bk][:, (i0 % 4) * 128:(i0 % 4) * 128 + (i1 - i0) * 128], ("ps", bk)
                for idx, kt in enumerate(ktiles):
                    reg, rk = sreg(idx, idx + 1)
                    P.mm(reg, Kt[hb:hb + 64, kt * 128:(kt + 1) * 128], Qt[hb:hb + 64, qs],
                         True, True, ["Qt", "Kt"], [rk])
                if nk:
                    for (i0, i1) in ((0, min(nk, 4)), (4, nk)):
                        if i1 <= i0:
                            continue
                        reg, rk = sreg(i0, i1)
                        P.tt("dve", Tb[u][:, i0 * 128:i1 * 128], reg,
                             BI[hh][:, base + i0:base + i1, :].rearrange("p t q -> p (t q)"), ALU.add,
                             [rk, ("BI", 0)], [("Tb", u)])
                    P.act(PT[u][:, 0:nk * 128], Tb[u][:, 0:nk * 128], AF.Exp, [("Tb", u)], [("PT", u)])
                reg, rk = sreg(nk, nk + 2)
                P.act(PT[u][:, nk * 128:(nk + 2) * 128], reg, AF.Exp, [rk], [("PT", u)])
                Ops = K.ps[4 + u]
                nkt = len(ktiles)
                for idx, kt in enumerate(ktiles):
                    P.mm(Ops[:, 0:128], V[:, kt, :], PT[u][:, idx * 128:(idx + 1) * 128], idx == 0, idx == nkt - 1,
                         ["V", ("PT", u)], [("ps", 4 + u)])
                for idx, kt in enumerate(ktiles):
                    P.mm(Ops[:, 128:256], onesb, PT[u][:, idx * 128:(idx + 1) * 128], idx == 0, idx == nkt - 1,
                         ["onesb", ("PT", u)], [("ps", 4 + u)])
                P.recip(rden[u][hb:hb + 64, :], Ops[hb:hb + 64, 128:256], [("ps", 4 + u)], [("rden", u)])
                P.tt("dve", OT[hb:hb + 64, 0, qs], Ops[hb:hb + 64, 0:128], rden[u][hb:hb + 64, :], ALU.mult,
                     [("ps", 4 + u), ("rden", u)], [(("OT", wb), 0)])
        emit_down_proj(P, K, wos[wb], OT, G1, b, l, 1, ("wos", wb), ("OT", wb))
    ar.release()
    P.barrier()


ML_H = 4
ML_DH = 256
NEGM = -30000.0


def ml_blocks():
    blks = [(0, LCTX + 2, None, 0)]
    s = 0
    for sz in (448, 448, 448, 448, 256):
        blks.append((LCTX + 1 + s, sz + 2, s // 64, sz // 64))
        s += sz
    return blks


def tile_pcol(tt):
    return 1 + 128 * tt if tt < 2 else LCTX + 2 + 128 * (tt - 2)


def emit_mlstm(P, K, l, b):
    jn = l // 3
    ar = K.arena
    d = K.d
    G1 = K.MOD[:, l, 16:24, :]
    ar.mark()
    GT = ar.alloc([18, 16], F32)
    LFt = ar.alloc([18, 8], F32)
    IMB = ar.alloc([18, 8], F32)
    identb = ar.alloc([1, 128], BF16)[:, 0, :]
    onesb = ar.alloc([1, 128], BF16)[:, 0, :]
    P.memset("pool", onesb, 1.0, ["onesb"])
    P.copy("dve", identb, K.ident, ["consts"], ["identb"])
    ar.mark()
    H = ar.alloc([NCH, LP], BF16)
    zero_pads(P, H, NCH, "H")
    emit_norm_mod(P, K, K.A1, K.MOD[:, l, 0:8, :], l, b, H)
    wif = ar.alloc([NCH, 16], BF16)
    P.dma("pool", wif, d["ml_wif"][:, :, :], writes=["wif"])
    for tt in range(18):
        pc = tile_pcol(tt)
        for kc in range(NCH):
            P.mm(K.ps[7][:, tt * 16:(tt + 1) * 16], H[:, kc, pc:pc + 128], wif[:, kc, :], kc == 0, kc == NCH - 1,
                 ["wif", ("H", kc)], [("ps", 7)])
    P.tt("dve", GT, K.ps[7][:, 0:288].rearrange("p (t g) -> p t g", g=16),
         K.mlbif.unsqueeze(1).to_broadcast([128, 18, 16]), ALU.add, [("ps", 7), "consts"], ["GT"])
    GTv = GT.rearrange("p t (d g) -> p t d g", d=2)
    LFv = LFt.rearrange("p t (d h) -> p t d h", d=2)
    IMv = IMB.rearrange("p t (d h) -> p t d h", d=2)
    P.act(LFv, GTv[:, :, :, 4:8], AF.Exp, ["GT"], ["LFt"], scale=-1.0)
    P.act(LFv, LFv, AF.Ln, ["LFt", "consts"], ["LFt"], bias=K.onec[:, 0:1])
    P.ts("dve", LFt, LFt, -1.0, None, ALU.mult, None, ["LFt"], ["LFt"])
    for tt in range(18):
        for dr in range(2):
            P.mm(K.ps[6][:, tt * 8 + dr * 4:tt * 8 + dr * 4 + 4], K.tri[:, dr, :], LFt[:, tt, dr * 4:(dr + 1) * 4], True, True,
                 ["LFt", "consts"], [("ps", 6)])
    P.tt("dve", IMv, GTv[:, :, :, 0:4], K.ps[6][:, 0:144].rearrange("p (t d h) -> p t d h", d=2, h=4), ALU.subtract,
         ["GT", ("ps", 6)], ["IMB"])
    wqk = ar.alloc([4, NCH, 128], BF16)
    wvo = ar.alloc([2, NCH, 256], BF16)
    QK = ar.alloc([4, L], BF16)
    Vst = ar.alloc([18, 256], BF16)
    Og = ar.alloc([2, L], BF16)
    accs = [[ar.alloc([1, 450], F32)[:, 0, :] for _ in range(2)] for _ in range(2)]
    rt = [ar.alloc([1, 448], F32)[:, 0, :] for _ in range(4)]
    wv_ = d["ml_w_in"][jn].rearrange("(kc p) n -> p kc n", p=128)
    npj = 0
    for hd in range(ML_H):
        for qk in range(2):
            base = qk * D + hd * ML_DH
            for ab in range(2):
                for hf in range(2):
                    c0 = base + 128 * hf + 64 * ab
                    P.dma("pool", wqk[:, 2 * qk + ab, :, 64 * hf:64 * hf + 64], wv_[:, :, c0:c0 + 64], writes=[("wqk", 2 * qk + ab)])
        for vo in range(2):
            c0 = (2 + vo) * D + hd * ML_DH
            P.dma("pool", wvo[:, vo, :, :], wv_[:, :, c0:c0 + 256], writes=[("wvo", vo)])
        for qk in range(2):
            for (c0, n, r0, nr) in ml_blocks():
                no = n - 2
                for ab in range(2):
                    pi = 2 * ab + (npj % 2)
                    for kc in range(NCH):
                        P.mm(K.ps[pi][:, 0:n], wqk[:, 2 * qk + ab, kc, :], H[:, kc, c0:c0 + n], kc == 0, kc == NCH - 1,
                             [("wqk", 2 * qk + ab), ("H", kc)], [("ps", pi)])
                    a = accs[ab][npj % 2][:, 0:no]
                    ak = ("macc", ab, npj % 2)
                    cwi = (qk * ML_H + hd) * 2 + ab
                    P.act(a, K.ps[pi][:, 1:1 + no], AF.Identity, [("ps", pi), "consts"], [ak], scale=K.mlcw[:, cwi, 1:2])
                    P.stt(a, K.ps[pi][:, 0:no], K.mlcw[:, cwi, 0:1], a, ALU.mult, ALU.add, [("ps", pi), "consts", ak], [ak])
                    P.stt(a, K.ps[pi][:, 2:2 + no], K.mlcw[:, cwi, 2:3], a, ALU.mult, ALU.add, [("ps", pi), "consts", ak], [ak])
                    P.act(a, a, AF.Silu, [ak], [ak])
                A_ = accs[0][npj % 2][:, 0:no]
                B_ = accs[1][npj % 2][:, 0:no]
                kA = ("macc", 0, npj % 2)
                kB = ("macc", 1, npj % 2)
                npj += 1
                x0 = c0
                if r0 is None:
                    for ab, src, sk in ((0, A_, kA), (1, B_, kB)):
                        if qk == 0:
                            P.act(QK[:, ab, 0:LCTX], src, AF.Identity, [sk], [("QK", ab)], scale=1.0 / 16.0)
                        else:
                            P.copy("dve", QK[:, 2 + ab, 0:LCTX], src, [sk], [("QK", 2 + ab)])
                    continue
                xs = c0 - 1
                tb = K.ropeq if qk == 0 else K.ropek
                for hfp in range(2):
                    ps_ = slice(64 * hfp, 64 * hfp + 64)
                    if hfp == 0:
                        cosv = tb[ps_, 0, r0:r0 + nr].unsqueeze(2).to_broadcast([64, nr, 64])
                        sinv = tb[ps_, 1, r0:r0 + nr].unsqueeze(2).to_broadcast([64, nr, 64])
                    else:
                        cosv = tb[ps_, 0, :].unsqueeze(1).to_broadcast([64, nr, 64])
                        sinv = tb[ps_, 1, :].unsqueeze(1).to_broadcast([64, nr, 64])

                    def v3(t_):
                        return t_[ps_, 0:no].rearrange("p (r c) -> p r c", c=64)
                    rk = [("rt", i, hfp) for i in range(4)]
                    P.tt("pool", v3(rt[0]), v3(A_), cosv, ALU.mult, [kA, "consts"], [rk[0]])
                    P.tt("pool", v3(rt[1]), v3(B_), sinv, ALU.mult, [kB, "consts"], [rk[1]])
                    P.tt("pool", v3(rt[2]), v3(A_), sinv, ALU.mult, [kA, "consts"], [rk[2]])
                    P.tt("pool", v3(rt[3]), v3(B_), cosv, ALU.mult, [kB, "consts"], [rk[3]])
                    P.tt("dve", QK[ps_, 2 * qk + 0, xs:xs + no], rt[0][ps_, 0:no], rt[1][ps_, 0:no], ALU.subtract,
                         [rk[0], rk[1]], [("QK", 2 * qk)])
                    P.tt("dve", QK[ps_, 2 * qk + 1, xs:xs + no], rt[2][ps_, 0:no], rt[3][ps_, 0:no], ALU.add,
                         [rk[2], rk[3]], [("QK", 2 * qk + 1)])
        for tt in range(18):
            pc = tile_pcol(tt)
            pi = 4 + (tt % 2)
            for kc in range(NCH):
                P.mm(K.ps[pi][:, 0:256], H[:, kc, pc:pc + 128], wvo[:, 0, kc, :], kc == 0, kc == NCH - 1,
                     [("wvo", 0), ("H", kc)], [("ps", pi)])
            P.copy("act", Vst[:, tt, :], K.ps[pi][:, 0:256], [("ps", pi)], ["Vst"])
        n_ = 0
        for mc in range(2):
            for (x0, n, p0) in tok_blocks():
                pi = 4 + (n_ % 2)
                n_ += 1
                for kc in range(NCH):
                    P.mm(K.ps[pi][:, 0:n], wvo[:, 1, kc, mc * 128:(mc + 1) * 128], H[:, kc, p0:p0 + n], kc == 0, kc == NCH - 1,
                         [("wvo", 1), ("H", kc)], [("ps", pi)])
                P.act(Og[:, mc, x0:x0 + n], K.ps[pi][:, 0:n], AF.Sigmoid, [("ps", pi)], [("Og", mc)])
        P.dma("sp", d["ml_sq"][hd], QK, reads=[("QK", i) for i in range(4)], writes=[("ml_sq", hd)])
        P.dma("sp", d["ml_sv"][hd], Vst, reads=["Vst"], writes=[("ml_sv", hd)])
        P.dma("sp", d["ml_so"][hd], Og, reads=[("Og", 0), ("Og", 1)], writes=[("ml_so", hd)])
    ar.release()
    P.barrier()
    ar.mark()
    QK = ar.alloc([4, L], BF16)
    Vst = ar.alloc([18, 256], BF16)
    Og = ar.alloc([2, L], BF16)
    HS = ar.alloc([2, L], F32)
    wo = ar.alloc([2, D], BF16)
    Cst = [ar.alloc([2, 384], F32) for _ in range(2)]
    Cb = [ar.alloc([2, 384], BF16) for _ in range(2)]
    LFr = [ar.alloc([1, 128], F32)[:, 0, :] for _ in range(2)]
    Targ = [ar.alloc([1, 128], F32)[:, 0, :] for _ in range(2)]
    Dm = [ar.alloc([1, 128], F32)[:, 0, :] for _ in range(2)]
    WT = [ar.alloc([1, 128], BF16)[:, 0, :] for _ in range(2)]
    Ebc = [ar.alloc([1, 128], F32)[:, 0, :] for _ in range(2)]
    Qtl = [ar.alloc([2, 128], BF16) for _ in range(2)]
    ktl = [ar.alloc([1, 256], BF16)[:, 0, :] for _ in range(2)]
    rr = [ar.alloc([1, 128], F32)[:, 0, :] for _ in range(2)]
    tmo = [ar.alloc([1, 128], F32)[:, 0, :] for _ in range(2)]
    smallc = [ar.alloc([1, 4], F32)[:, 0, :] for _ in range(2)]
    psT = K.ps[6].bitcast(BF16)
    wov = d["ml_w_out"][jn].rearrange("(kc p) n -> p kc n", p=128)
    for hd in range(ML_H):
        P.dma("sp", QK, d["ml_sq"][hd], writes=[("QK", i) for i in range(4)])
        P.dma("sp", Vst, d["ml_sv"][hd], writes=["Vst"])
        P.dma("sp", Og, d["ml_so"][hd], writes=[("Og", 0), ("Og", 1)])
        P.dma("pool", wo, wov[:, 2 * hd:2 * hd + 2, :], writes=["wo"])
        it = 0
        for dr in range(2):
            order = [0, 1] + list(range(2, 18)) if dr == 0 else [1, 0] + list(range(17, 1, -1))
            te = 127 if dr == 0 else 0
            C_ = Cst[dr]
            Cb_ = Cb[dr]
            P.memset("pool", C_, 0.0, [("C", dr)])
            for oi, tt in enumerate(order):
                u = it % 2
                it += 1
                first = oi == 0
                last = oi == len(order) - 1
                cs = slice(tt * 128, (tt + 1) * 128)
                g = dr * 4 + hd
                pb = K.ps[u]
                pn = K.ps[2 + u]
                P.copy("pool", LFr[u], LFt[:, tt, g:g + 1].to_broadcast([128, 128]), ["LFt"], [("LFr", u)])
                P.mm(pb[:, 0:128], LFr[u], K.tri[:, dr, :], True, True, [("LFr", u), "consts"], [("ps", u)])
                P.mm(pb[:, 128:256], QK[:, 2, cs], QK[:, 0, cs], True, False, [("QK", 2), ("QK", 0)], [("ps", u)])
                P.mm(pb[:, 128:256], QK[:, 3, cs], QK[:, 1, cs], False, True, [("QK", 3), ("QK", 1)], [("ps", u)])
                P.stt(Targ[u], pb[:, 0:128], IMB[:, tt, g:g + 1], K.negm[:, dr, :], ALU.add, ALU.add,
                      [("ps", u), "IMB", "consts"], [("Targ", u)])
                P.act(Dm[u], Targ[u], AF.Exp, [("Targ", u)], [("Dm", u)])
                P.tt("dve", WT[u], pb[:, 128:256], Dm[u], ALU.mult, [("ps", u), ("Dm", u)], [("WT", u)])
                if not first or not last:
                    P.act(Ebc[u], pb[:, 0:128], AF.Exp, [("ps", u)], [("Ebc", u)])
                if not first:
                    P.tt("pool", Qtl[u][:, 0, :], QK[:, 0, cs], Ebc[u], ALU.mult, [("QK", 0), ("Ebc", u)], [("Qtl", u)])
                    P.tt("pool", Qtl[u][:, 1, :], QK[:, 1, cs], Ebc[u], ALU.mult, [("QK", 1), ("Ebc", u)], [("Qtl", u)])
                for mc in range(3):
                    lh = Vst[:, tt, mc * 128:(mc + 1) * 128] if mc < 2 else onesb
                    P.mm(pn[:, mc * 128:(mc + 1) * 128], lh, WT[u], True, first, ["Vst", "onesb", ("WT", u)], [("ps", 2 + u)])
                    if not first:
                        for kc in range(2):
                            P.mm(pn[:, mc * 128:(mc + 1) * 128], Cb_[:, kc, mc * 128:(mc + 1) * 128], Qtl[u][:, kc, :], False, kc == 1,
                                 [("Cb", dr), ("Qtl", u)], [("ps", 2 + u)])
                P.act(rr[u], pn[:, 256:384], AF.Abs, [("ps", 2 + u)], [("rr", u)])
                P.ts("dve", rr[u], rr[u], 1.0, None, ALU.max, None, [("rr", u)], [("rr", u)])
                P.recip(rr[u], rr[u], [("rr", u)], [("rr", u)])
                for mc in range(2):
                    if dr == 0:
                        P.tt("dve", HS[:, mc, cs], pn[:, mc * 128:(mc + 1) * 128], rr[u], ALU.mult,
                             [("ps", 2 + u), ("rr", u)], [("HS", mc)])
                    else:
                        P.tt("dve", tmo[mc], pn[:, mc * 128:(mc + 1) * 128], rr[u], ALU.mult,
                             [("ps", 2 + u), ("rr", u)], [("tmo", mc)])
                        P.tt("pool", HS[:, mc, cs], HS[:, mc, cs], tmo[mc], ALU.add, [("HS", mc), ("tmo", mc)], [("HS", mc)])
                if last:
                    continue
                sc_ = smallc[u]
                P.copy("dve", sc_[:, 0:1], pb[:, te:te + 1], [("ps", u)], [("smallc", u)])
                P.act(sc_[:, 1:2], IMB[:, tt, g:g + 1], AF.Exp, ["IMB", ("smallc", u)], [("smallc", u)], bias=sc_[:, 0:1])
                P.tr(psT[:, 0:128], QK[:, 2, cs], identb, [("QK", 2), "identb"], [("ps", 6)])
                P.tr(psT[:, 128:256], QK[:, 3, cs], identb, [("QK", 3), "identb"], [("ps", 6)])
                P.ts("dve", ktl[u], psT[:, 0:256], sc_[:, 1:2], None, ALU.mult, None, [("ps", 6), ("smallc", u)], [("ktl", u)])
                for kc in range(2):
                    pc_ = K.ps[4 + kc]
                    P.mm(pc_[:, 0:256], ktl[u][:, kc * 128:(kc + 1) * 128], Vst[:, tt, :], True, True, [("ktl", u), "Vst"], [("ps", 4 + kc)])
                    P.mm(pc_[:, 256:384], ktl[u][:, kc * 128:(kc + 1) * 128], onesb, True, True, [("ktl", u), "onesb"], [("ps", 4 + kc)])
                    P.stt(C_[:, kc, :], C_[:, kc, :], Ebc[u][:, te:te + 1], pc_[:, 0:384], ALU.mult, ALU.add,
                          [("C", dr), ("Ebc", u), ("ps", 4 + kc)], [("C", dr)])
                P.copy("act", Cb_, C_, [("C", dr)], [("Cb", dr)])
        HN = QK[:, 0:2, :]
        for bi, (x0, n, p0) in enumerate(tok_blocks()):
            for mc in range(2):
                sq = K.sq[mc]
                P.act(sq[:, 0:n], HS[:, mc, x0:x0 + n], AF.Square, [("HS", mc)], [("sq", mc)])
                P.mm(K.ps[7][:, 0:n], K.ones, sq[:, 0:n], mc == 0, mc == 1, [("sq", mc), "consts"], [("ps", 7)])
            rs = K.tmpB[bi % 3]
            rk = ("tmpB", bi % 3)
            P.act(rs[:, 0:n], K.ps[7][:, 0:n], AF.Sqrt, [("ps", 7), "consts"], [rk], bias=K.epsc[:, 0:1], scale=4.0)
            P.recip(rs[:, 0:n], rs[:, 0:n], [rk], [rk])
            for mc in range(2):
                P.tt("dve", HS[:, mc, x0:x0 + n], HS[:, mc, x0:x0 + n], rs[:, 0:n], ALU.mult, [("HS", mc), rk], [("HS", mc)])
                P.stt(HN[:, mc, x0:x0 + n], HS[:, mc, x0:x0 + n], K.mlng[:, 2 * hd + mc:2 * hd + mc + 1], Og[:, mc, x0:x0 + n],
                      ALU.mult, ALU.mult, [("HS", mc), ("Og", mc), "consts"], [("QK", mc)])
        emit_down_proj(P, K, wo, HN, G1, b, l, 2, "wo", "QK")
    ar.release()
    ar.release()
    P.barrier()


S5_G2 = 32
S5_T = 256
S5_NC = L // S5_T
GELU_C = 1.5957691216057308


def s5_scalars(P, K, ar, W, LR, LI, LDT, tag, want_q):
    def T():
        return ar.alloc([1, W], F32)[:, 0, :]

    def k(n):
        return (tag, n)
    lr, er, c, s_ = [T() for _ in range(4)]
    ar.mark()
    dt, ang, t1, t2 = [T() for _ in range(4)]
    P.act(dt, LDT, AF.Exp, [k("in")], [k("dt")])
    P.ts("dve", lr, LR, -1e-4, None, ALU.min, None, [k("in")], [k("lr")])
    P.tt("dve", t1, lr, dt, ALU.mult, [k("lr"), k("dt")], [k("t1")])
    P.act(er, t1, AF.Exp, [k("t1")], [k("er")])
    P.tt("dve", ang, LI, dt, ALU.mult, [k("in"), k("dt")], [k("ang")])
    ki = ar.alloc([1, W], I32)[:, 0, :]
    kr, ph, x2 = T(), T(), T()
    P.ts("dve", t1, ang, 1.0 / (2.0 * np.pi), 0.25, ALU.mult, ALU.add, [k("ang")], [k("t1")])
    P.copy("dve", ki, t1, [k("t1")], [k("ki")])
    P.copy("dve", kr, ki, [k("ki")], [k("kr")])
    P.stt(ph, kr, -6.28125, ang, ALU.mult, ALU.add, [k("kr"), k("ang")], [k("ph")])
    P.stt(ph, kr, -1.9353071795864769e-3, ph, ALU.mult, ALU.add, [k("kr"), k("ph")], [k("ph")])
    P.ts("dve", ph, ph, 0.25, None, ALU.mult, None, [k("ph")], [k("ph")])
    P.tt("dve", x2, ph, ph, ALU.mult, [k("ph")], [k("x2")])
    import math
    sco = [(-1.0) ** i / math.factorial(2 * i + 1) for i in range(1, 7)]
    cco = [(-1.0) ** i / math.factorial(2 * i) for i in range(1, 7)]
    P.ts("dve", s_, x2, sco[5], None, ALU.mult, None, [k("x2")], [k("s")])
    for i in range(4, -1, -1):
        P.stt(s_, s_, sco[i], x2, ALU.add, ALU.mult, [k("s"), k("x2")], [k("s")])
    P.stt(s_, s_, 1.0, ph, ALU.add, ALU.mult, [k("s"), k("ph")], [k("s")])
    P.ts("dve", c, x2, cco[5], None, ALU.mult, None, [k("x2")], [k("c")])
    for i in range(4, -1, -1):
        P.stt(c, c, cco[i], x2, ALU.add, ALU.mult, [k("c"), k("x2")], [k("c")])
    P.ts("dve", c, c, 1.0, None, ALU.add, None, [k("c")], [k("c")])
    for it in range(2):
        P.tt("dve", t1, c, c, ALU.mult, [k("c")], [k("t1")])
        P.tt("dve", t2, s_, s_, ALU.mult, [k("s")], [k("t2")])
        P.stt(s_, c, 2.0, s_, ALU.mult, ALU.mult, [k("c"), k("s")], [k("s")])
        P.tt("dve", c, t1, t2, ALU.subtract, [k("t1"), k("t2")], [k("c")])
    out = {"er": er, "c": c, "s": s_}
    ar.release()
    P.barrier()
    if want_q:
        are, aim, den, nr, qre, qim, t1, t2 = [T() for _ in range(8)]
        P.tt("dve", are, er, c, ALU.mult, [k("er"), k("c")], [k("are")])
        P.tt("dve", aim, er, s_, ALU.mult, [k("er"), k("s")], [k("aim")])
        P.tt("dve", t1, lr, lr, ALU.mult, [k("lr")], [k("t1")])
        P.tt("dve", den, LI, LI, ALU.mult, [k("in")], [k("den")])
        P.tt("dve", den, den, t1, ALU.add, [k("den"), k("t1")], [k("den")])
        P.recip(den, den, [k("den")], [k("den")])
        P.ts("dve", nr, are, -1.0, None, ALU.add, None, [k("are")], [k("nr")])
        P.tt("dve", t1, nr, lr, ALU.mult, [k("nr"), k("lr")], [k("t1")])
        P.tt("dve", t2, aim, LI, ALU.mult, [k("aim"), k("in")], [k("t2")])
        P.tt("dve", qre, t1, t2, ALU.add, [k("t1"), k("t2")], [k("qre")])
        P.tt("dve", qre, qre, den, ALU.mult, [k("qre"), k("den")], [k("qre")])
        P.tt("dve", t1, aim, lr, ALU.mult, [k("aim"), k("lr")], [k("t1")])
        P.tt("dve", t2, nr, LI, ALU.mult, [k("nr"), k("in")], [k("t2")])
        P.tt("dve", qim, t1, t2, ALU.subtract, [k("t1"), k("t2")], [k("qim")])
        P.tt("dve", qim, qim, den, ALU.mult, [k("qim"), k("den")], [k("qim")])
        out["qre"] = qre
        out["qim"] = qim
    return out


def emit_s5_gen(P, K, js):
    ar = K.arena
    d = K.d
    ar.mark()
    p1 = ar.alloc([3, 64], F32)
    P.dma("sp", p1, d["s5p1"][js], writes=[("g1", "in")])
    sc = s5_scalars(P, K, ar, 64, p1[:, 0, :], p1[:, 1, :], p1[:, 2, :], "g1", False)
    P.copy("dve", K.S5R[:, js, 0, :], sc["er"], [("g1", "er")], [("S5R", js)])
    for dh in range(2):
        ar.mark()
        COS = ar.alloc([32, S5_T], F32)
        SIN = ar.alloc([32, S5_T], F32)
        t1 = ar.alloc([32, S5_T // 2], F32)
        t2 = ar.alloc([32, S5_T // 2], F32)
        P.copy("dve", COS[:, :, 0:1], sc["c"][:, dh * 32:(dh + 1) * 32].unsqueeze(2), [("g1", "c")], ["COS"])
        P.copy("dve", SIN[:, :, 0:1], sc["s"][:, dh * 32:(dh + 1) * 32].unsqueeze(2), [("g1", "s")], ["SIN"])
        for kk in range(8):
            ln = 2 ** kk
            pr = COS[:, :, ln - 1:ln].to_broadcast([128, 32, ln])
            pi_ = SIN[:, :, ln - 1:ln].to_broadcast([128, 32, ln])
            a1 = t1[:, :, 0:ln]
            a2 = t2[:, :, 0:ln]
            P.tt("dve", a1, COS[:, :, 0:ln], pr, ALU.mult, ["COS"], ["t1"])
            P.tt("dve", a2, SIN[:, :, 0:ln], pi_, ALU.mult, ["SIN"], ["t2"])
            P.tt("dve", COS[:, :, ln:2 * ln], a1, a2, ALU.subtract, ["t1", "t2", "COS"], ["COSn"])
            P.tt("dve", a1, COS[:, :, 0:ln], pi_, ALU.mult, ["COS", "SIN", "COSn"], ["t1"])
            P.tt("dve", a2, SIN[:, :, 0:ln], pr, ALU.mult, ["SIN", "COS", "COSn"], ["t2"])
            P.tt("dve", SIN[:, :, ln:2 * ln], a1, a2, ALU.add, ["t1", "t2", "SIN"], ["SIN"])
            P.copy("dve", COS[:, :, 0:1], COS[:, :, 0:1], ["COSn", "COS"], ["COS"])
        P.copy("dve", K.S5R[:, js, 1, dh * 32:(dh + 1) * 32].unsqueeze(2), COS[:, :, S5_T - 1:S5_T], ["COS"], [("S5R", js)])
        P.copy("dve", K.S5R[:, js, 2, dh * 32:(dh + 1) * 32].unsqueeze(2), SIN[:, :, S5_T - 1:S5_T], ["SIN"], [("S5R", js)])
        P.dma("sp", d["s5t"][js, :, 0, dh * 32:(dh + 1) * 32, :], COS, reads=["COS"], writes=[("s5t", js, 0, dh)])
        P.dma("sp", d["s5t"][js, :, 1, dh * 32:(dh + 1) * 32, :], SIN, reads=["SIN"], writes=[("s5t", js, 1, dh)])
        ar.release()
        P.barrier()
    ar.release()
    P.barrier()
    ar.mark()
    CC = ar.alloc([2, 1024], F32)
    P.dma("sp", CC, d["s5c"][js], writes=["CC"])
    for var in range(2):
        WC = ar.alloc([2, S5_G2, 128], BF16)
        P.memset("pool", WC, 0.0, [("WC", var)])
        WCv = WC.rearrange("p d (c k) m -> p d c k m", k=4)
        CCv = CC[:, var, :].rearrange("p (d c k h) -> p d c k h", d=2, c=8, k=4)
        for gl in range(2):
            for k4 in range(4):
                o_ = WCv[64 * gl:64 * gl + 64, :, :, k4, 32 * k4 + 16 * gl:32 * k4 + 16 * gl + 16]
                i_ = CCv[64 * gl:64 * gl + 64, :, :, k4, :]
                if var == 0:
                    P.copy("dve", o_, i_, ["CC"], [("WC", var)])
                else:
                    P.ts("dve", o_, i_, -1.0, None, ALU.mult, None, ["CC"], [("WC", var)])
        P.dma("sp", d["s5wc"][js, :, var], WC, reads=[("WC", var)], writes=[("s5wc", js, var)])
    ar.release()
    P.barrier()
    ar.mark()
    p2 = ar.alloc([5, 1024], F32)
    P.dma("sp", p2, d["s5p2"][js], writes=[("g2", "in"), "BB"])
    sc = s5_scalars(P, K, ar, 1024, p2[:, 0, :], p2[:, 1, :], p2[:, 2, :], "g2", True)
    t1 = p2[:, 0, :]
    t2 = p2[:, 1, :]
    Bb = ar.alloc([1, 1024], F32)[:, 0, :]
    WB = ar.alloc([2, S5_G2, 128], BF16)
    WBv = WB.rearrange("p d (c k) (g m) -> p d c k g m", k=4, g=2)
    for var in range(2):
        x1, x2 = (p2[:, 3, :], p2[:, 4, :]) if var == 0 else (p2[:, 4, :], p2[:, 3, :])
        P.tt("dve", t1, sc["qre"], x1, ALU.mult, [("g2", "qre"), "BB", ("g2", "in"), ("g2", "lr")], ["bt1"])
        P.tt("dve", t2, sc["qim"], x2, ALU.mult, [("g2", "qim"), "BB", ("g2", "in"), ("g2", "ang"), ("g2", "den")], ["bt2"])
        P.tt("dve", Bb, t1, t2, ALU.subtract if var == 0 else ALU.add, ["bt1", "bt2"], ["Bb"])
        P.memset("pool", WB, 0.0, ["WB"])
        Bv = Bb.rearrange("p (d c m) -> p d c m", d=2, c=8)
        for k4 in range(4):
            for gl in range(2):
                P.ts("dve", WBv[:, :, :, k4, gl, :], Bv, K.mk8[:, 2 * k4 + gl:2 * k4 + gl + 1], None, ALU.mult, None,
                     ["Bb", "consts"], ["WB"])
        P.dma("sp", d["s5wb"][js, :, var], WB, reads=["WB"], writes=[("s5wb", js, var)])
    ar.release()
    P.barrier()


def emit_s5(P, K, l, b):
    js = l // 3
    ar = K.arena
    d = K.d
    G1 = K.MOD[:, l, 16:24, :]
    ar.mark()
    H = ar.alloc([NCH, LP], BF16)
    emit_norm_mod(P, K, K.A1, K.MOD[:, l, 0:8, :], l, b, H, keep_rstd=True)
    SH1 = K.MOD[:, l, 0:8, :]
    RSTD = K.RSTD
    Wc = [ar.alloc([2, 2, 2, 4, 128], BF16) if False else ar.alloc([4, 8, 128], BF16) for _ in range(2)]
    T_ = S5_T
    tabs = [ar.alloc([2, T_], F32) for _ in range(4)]
    NB2 = 2
    tq = [[ar.alloc([1, T_], F32)[:, 0, :] for _ in range(4)] for _ in range(NB2)]
    bt = [[ar.alloc([1, T_], F32)[:, 0, :] for _ in range(2)] for _ in range(NB2)]
    zz = [[ar.alloc([1, T_], F32)[:, 0, :] for _ in range(2)] for _ in range(NB2)]
    PP = [ar.alloc([4, T_], BF16) for _ in range(NB2)]
    car = [[ar.alloc([1, 4], F32)[:, 0, :] for _ in range(2)] for _ in range(2)]
    segs = [(0, 0, 256, 4)] + [(0, 256, 256, b)] + [(bk, 0, 512, b) for bk in (1, 2, 3)] + [(4, 0, 256, b)]
    Rt = K.S5R[:, js, 0, :]
    CAc = K.S5R[:, js, 1, :]
    CAs = K.S5R[:, js, 2, :]
    it = 0
    ntab = 0
    NCK = S5_NC
    for c in range(NCH):
        wb = c % 2
        bank_started = set()
        for v in range(2):
            P.dma("sp", Wc[wb][:, v, :, :].rearrange("p (d k) m -> p d k m", d=2),
                  d["s5wb"][js, :, v].rearrange("p d (c k) m -> p d c k m", k=4)[:, :, c, :, :], writes=[("Wc", wb, v)])
            P.dma("sp", Wc[wb][:, 2 + v, :, :].rearrange("p (d k) m -> p d k m", d=2),
                  d["s5wc"][js, :, v].rearrange("p d (c k) m -> p d c k m", k=4)[:, :, c, :, :], writes=[("Wc", wb, 2 + v)])
        for k4 in range(4):
            streams = []
            for dr in range(2):
                cg = dr * 32 + 4 * c + k4
                tb = tabs[ntab % 4]
                tk_ = ("tabs", ntab % 4)
                ntab += 1
                P.dma("sp", tb, d["s5t"][js, :, :, cg, :], writes=[tk_])
                order = list(range(NCK)) if dr == 0 else [0] + list(range(NCK - 1, 0, -1))
                streams.append((dr, cg, tb, tk_, order))
            for oi in range(NCK):
                for (dr, cg, tb, tk_, order) in streams:
                    tt_ = order[oi]
                    u = it % NB2
                    it += 1
                    pc = 1 if tt_ == 0 else LCTX + 2 + T_ * (tt_ - 1)
                    pv = K.ps[5 + (it % 2)]
                    pvk = ("ps", 5 + (it % 2))
                    wi = dr * 4 + k4
                    P.mm(pv[:, 0:T_], Wc[wb][:, 0, wi, :], H[:, c, pc:pc + T_], True, True, [("Wc", wb, 0), ("H", c)], [pvk])
                    P.mm(pv[:, T_:2 * T_], Wc[wb][:, 1, wi, :], H[:, c, pc:pc + T_], True, True, [("Wc", wb, 1), ("H", c)], [pvk])
                    if dr == 0:
                        vr = pv[:, 0:T_]
                        vi = pv[:, T_:2 * T_]
                    else:
                        vr = pv[:, T_ - 1::-1]
                        vi = pv[:, 2 * T_ - 1:T_ - 1:-1]
                    cosT = tb[:, 0, :]
                    sinT = tb[:, 1, :]
                    q = tq[u]
                    qk = [("tq", u, i) for i in range(4)]
                    P.tt("dve", q[0], vr, cosT, ALU.mult, [pvk, tk_], [qk[0]])
                    P.tt("dve", q[1], vi, sinT, ALU.mult, [pvk, tk_], [qk[1]])
                    P.tt("dve", q[2], vi, cosT, ALU.mult, [pvk, tk_], [qk[2]])
                    P.tt("dve", q[3], vr, sinT, ALU.mult, [pvk, tk_], [qk[3]])
                    P.tt("pool", bt[u][0], q[0], q[1], ALU.add, [qk[0], qk[1]], [("bt", u, 0)])
                    P.tt("pool", bt[u][1], q[2], q[3], ALU.subtract, [qk[2], qk[3]], [("bt", u, 1)])
                    Rb = Rt[:, cg:cg + 1].to_broadcast([128, T_])
                    cr = car[dr]
                    for ri in range(2):
                        init = 0.0 if oi == 0 else cr[ri][:, 0:1]
                        P.scan(zz[u][ri], Rb, bt[u][ri], init, [("bt", u, ri), ("S5R", js), ("car", dr, ri)], [("zz", u, ri)])
                    zr = zz[u][0]
                    zi = zz[u][1]
                    if oi < NCK - 1:
                        P.tt("pool", cr[0][:, 1:2], zi[:, T_ - 1:T_], CAs[:, cg:cg + 1], ALU.mult, [("zz", u, 1), ("S5R", js)], [("cart", dr, 0)])
                        P.tt("pool", cr[1][:, 1:2], zi[:, T_ - 1:T_], CAc[:, cg:cg + 1], ALU.mult, [("zz", u, 1), ("S5R", js)], [("cart", dr, 1)])
                        P.stt(cr[0][:, 0:1], zr[:, T_ - 1:T_], CAc[:, cg:cg + 1], cr[0][:, 1:2], ALU.mult, ALU.subtract,
                              [("zz", u, 0), ("cart", dr, 0), ("S5R", js)], [("car", dr, 0)])
                        P.stt(cr[1][:, 0:1], zr[:, T_ - 1:T_], CAs[:, cg:cg + 1], cr[1][:, 1:2], ALU.mult, ALU.add,
                              [("zz", u, 0), ("cart", dr, 1), ("S5R", js)], [("car", dr, 1)])
                    Pu = PP[u]

                    def po(i):
                        return Pu[:, i, :] if dr == 0 else Pu[:, i, T_ - 1::-1]
                    pk = [("PP", u, i) for i in range(4)]
                    P.tt("dve", po(0), zr, cosT, ALU.mult, [("zz", u, 0), tk_], [pk[0]])
                    P.stt(po(1), zi, -1.0, sinT, ALU.mult, ALU.mult, [("zz", u, 1), tk_], [pk[1]])
                    P.tt("pool", po(2), zr, sinT, ALU.mult, [("zz", u, 0), tk_], [pk[2]])
                    P.tt("pool", po(3), zi, cosT, ALU.mult, [("zz", u, 1), tk_], [pk[3]])
                    bki = (tt_ * T_) // 512
                    yb = K.ps[bki]
                    y0 = (tt_ * T_) % 512
                    ys = yb[:, y0:y0 + T_]
                    lastw = (k4 == 3 and dr == 1)
                    for i in range(4):
                        st = bki not in bank_started
                        bank_started.add(bki)
                        P.mm(ys, Wc[wb][:, 2 + (i // 2), wi, :], Pu[:, i, :], st, lastw and i == 3,
                             [("Wc", wb, 2 + i // 2), pk[i]], [("ps", bki)])
        for si, (bk, o0, n, bcol) in enumerate(segs):
            x0 = bk * 512 + o0
            p0 = 1 + x0 if x0 < LCTX else 2 + x0
            f = [K.tmpB[i][:, 0:n] for i in range(3)]
            fk = [("tmpB", i) for i in range(3)]
            P.tt("dve", f[0], K.X[:, c, x0:x0 + n], RSTD[:, x0:x0 + n], ALU.mult, [("X", c)] + RSTD_KEYS, [fk[0]])
            P.act(f[0], f[0], AF.Identity, [fk[0], ("MOD", l), ("A", l)], [fk[0]],
                  scale=K.A1[:, l, c, bcol:bcol + 1], bias=SH1[:, c, bcol:bcol + 1])
            P.stt(f[1], f[0], K.s5d[:, js, c:c + 1], K.ps[bk][:, o0:o0 + n], ALU.mult, ALU.add,
                  [fk[0], ("ps", bk), "consts"], [fk[1]])
            P.tt("pool", f[2], f[1], f[1], ALU.mult, [fk[1]], [fk[2]])
            P.ts("pool", f[2], f[2], 0.044715, 1.0, ALU.mult, ALU.add, [fk[2]], [fk[2]])
            P.tt("pool", f[2], f[2], f[1], ALU.mult, [fk[2], fk[1]], [fk[2]])
            P.act(f[2], f[2], AF.Sigmoid, [fk[2]], [fk[2]], scale=GELU_C)
            P.tt("dve", H[:, c, p0:p0 + n], f[1], f[2], ALU.mult, [fk[1], fk[2]], [("H", c)])
    gw = [ar.alloc([2, NCH, 128], BF16) for _ in range(2)]
    sg = [ar.alloc([1, 512], F32)[:, 0, :] for _ in range(2)]
    gv = d["s5_glu_w"][js].rearrange("(kc p) n -> p kc n", p=128)
    n_ = 0
    for oc in range(NCH):
        wb = oc % 2
        P.dma("pool", gw[wb][:, 0, :, :], gv[:, :, oc * 128:(oc + 1) * 128], writes=[("gw", wb, 0)])
        P.dma("pool", gw[wb][:, 1, :, :], gv[:, :, D + oc * 128:D + (oc + 1) * 128], writes=[("gw", wb, 1)])
        for (x0, n, p0) in tok_blocks():
            u = n_ % 2
            n_ += 1
            for half in range(2):
                pst = K.ps[2 * half + u]
                for kc in range(NCH):
                    P.mm(pst[:, 0:n], gw[wb][:, half, kc, :], H[:, kc, p0:p0 + n], kc == 0, kc == NCH - 1,
                         [("gw", wb, half), ("H", kc)], [("ps", 2 * half + u)])
            P.act(sg[u][:, 0:n], K.ps[2 + u][:, 0:n], AF.Sigmoid, [("ps", 2 + u), "consts"], [("sg", u)],
                  bias=K.s5gb[:, js, 8 + oc:9 + oc])
            P.stt(sg[u][:, 0:n], K.ps[u][:, 0:n], K.s5gb[:, js, oc:oc + 1], sg[u][:, 0:n], ALU.add, ALU.mult,
                  [("ps", u), ("sg", u), "consts"], [("sg", u)])
            bcol = 4 if x0 == 0 else b
            P.stt(K.X[:, oc, x0:x0 + n], sg[u][:, 0:n], G1[:, oc, bcol:bcol + 1], K.X[:, oc, x0:x0 + n], ALU.mult, ALU.add,
                  [("sg", u), ("MOD", l), ("X", oc)], [("X", oc)])
    ar.release()
    P.barrier()


def emit_final(P, K, b):
    K.arena.mark()
    emit_rstd(P, K, "f")
    i = 0
    for bi, (c0, n, p0) in enumerate(tok_blocks()):
        if bi == 0:
            continue
        for c in range(NCH):
            tmp = K.tmpB[i % 3]
            tk = ("tmpB", i % 3)
            i += 1
            P.tt("dve", tmp[:, 0:n], K.X[:, c, c0:c0 + n], K.RSTD[:, c0:c0 + n], ALU.mult, [("X", c), ("rstd", bi)], [tk])
            P.act(tmp[:, 0:n], tmp[:, 0:n], AF.Identity, [tk, "consts"], [tk], scale=K.fg[:, c:c + 1])
            P.dma("sp", K.d["out"][b, c, :, c0 - LCTX:c0 - LCTX + n], tmp[:, 0:n], reads=[tk], writes=[("out", b, c, bi)])
    K.arena.release()
    P.barrier()


def build_program(NB=4, layers=(0, 1, 2, 3), mixers=True, final=True):
    nc = bass.Bass("TRN2", target_bir_lowering=False)
    d = {}

    def din(name, shape, dt=F32):
        d[name] = nc.dram_tensor(name, list(shape), dt, kind="ExternalInput").ap()

    din("xin", [NB, NCH, 128, L])
    din("cT", [128, NCH, 5])
    din("ada_w", [DEPTH, D, 6 * D])
    din("consts", [128, K_CONST_COLS])
    din("ffn_w_in", [DEPTH, D, 2 * DFF])
    din("ffn_w_out", [DEPTH, DFF, D])
    s5_js = sorted(set(l // 3 for l in layers if l % 3 == 0)) if mixers else []
    if s5_js:
        din("s5p1", [2, 128, 3, 64])
        din("s5p2", [2, 128, 5, 1024])
        din("s5c", [2, 128, 2, 1024])
        din("s5_glu_w", [2, D, 2 * D])
        d["s5t"] = nc.dram_tensor("s5t", [2, 128, 2, 64, S5_T], F32, kind="Internal").ap()
        d["s5wb"] = nc.dram_tensor("s5wb", [2, 128, 2, 2, S5_G2, 128], BF16, kind="Internal").ap()
        d["s5wc"] = nc.dram_tensor("s5wc", [2, 128, 2, 2, S5_G2, 128], BF16, kind="Internal").ap()
    if mixers and any(l % 3 == 1 for l in layers):
        din("ml_w_in", [1, D, 4 * D])
        din("ml_w_out", [1, D, D])
        din("ml_wif", [128, NCH, 16])
        for nm, shp in (("ml_sq", [ML_H, 128, 4, L]), ("ml_sv", [ML_H, 128, 18, 256]), ("ml_so", [ML_H, 128, 2, L])):
            d[nm] = nc.dram_tensor(nm, shp, BF16, kind="Internal").ap()
    if mixers and any(l % 3 == 2 for l in layers):
        din("na_w_qkv", [1, D, 3 * D])
        din("na_w_out", [1, D, D])
        din("na_bias", [1, NA_H, 128, NA_NT * 128])
    d["out"] = nc.dram_tensor("out", [NB, NCH, 128, LLAT], F32, kind="ExternalOutput").ap()

    with contextlib.ExitStack() as es:
        P = Prog(nc, es)
        K = Ctx()
        K.d = d
        K.NB = NB
        NBYTES = 204 * 1024
        big = es.enter_context(nc.sbuf_tensor("arena", [128, NBYTES // 4], F32))
        K.arena = Arena(big, NBYTES)
        ar = K.arena
        K.ps = [es.enter_context(nc.psum_tensor(f"ps{i}", [128, 512], F32)) for i in range(8)]
        K.X = ar.alloc([NCH, L], F32)
        cst = ar.alloc([1, K_CONST_COLS], F32)[:, 0, :]
        P.dma("sp", cst, d["consts"][:, :], writes=["consts"])
        o = 0

        def take(n, shape=None):
            nonlocal o
            v = cst[:, o:o + n]
            o += n
            return v
        K.ones = take(128)
        K.epsc = take(1)
        K.adab = take(DEPTH * 48).rearrange("p (l j) -> p l j", l=DEPTH)
        K.n1g = take(DEPTH * NCH).rearrange("p (l c) -> p l c", l=DEPTH)
        K.n2g = take(DEPTH * NCH).rearrange("p (l c) -> p l c", l=DEPTH)
        K.fg = take(NCH)
        K.cw = take(DEPTH * 44 * 3).rearrange("p (l c k) -> p l c k", l=DEPTH, c=44)
        K.onec = take(1)
        K.ident = take(128)
        K.tri = take(256).rearrange("p (d t) -> p d t", d=2)
        K.negm = take(256).rearrange("p (d t) -> p d t", d=2)
        K.mlbif = take(16)
        K.mlcw = take(48).rearrange("p (c k) -> p c k", k=3)
        K.mlng = take(NCH)
        K.ropeq = take(128).rearrange("p (t c) -> p t c", t=2)
        K.ropek = take(128).rearrange("p (t c) -> p t c", t=2)
        K.halfpi = take(1)
        K.mk8 = take(8)
        K.s5d = take(2 * NCH).rearrange("p (j c) -> p j c", j=2)
        K.s5gb = take(2 * 16).rearrange("p (j c) -> p j c", j=2)
        assert o == K_CONST_COLS, (o, K_CONST_COLS)
        K.MOD = ar.alloc([DEPTH, 48, 5], F32)
        K.A1 = ar.alloc([DEPTH, NCH, 5], F32)
        K.A2 = ar.alloc([DEPTH, NCH, 5], F32)
        K.tmpB = [ar.alloc([1, 512], F32)[:, 0, :] for _ in range(3)]
        K.sq = [ar.alloc([1, 512], F32)[:, 0, :] for _ in range(2)]

        K.S5R = ar.alloc([2, 3, 64], F32)
        emit_mods(P, K)
        for js in s5_js:
            emit_s5_gen(P, K, js)
        for b in range(NB):
            for c in range(NCH):
                P.dma("sp", K.X[:, c, :], d["xin"][b, c], writes=[("X", c)])
            for l in layers:
                if mixers:
                    if l % 3 == 2:
                        emit_na(P, K, l, b)
                    if l % 3 == 1:
                        emit_mlstm(P, K, l, b)
                    if l % 3 == 0:
                        emit_s5(P, K, l, b)
                emit_ffn(P, K, l, b)
            if final:
                emit_final(P, K, b)
            else:
                for c in range(NCH):
                    P.dma("sp", d["out"][b, c], K.X[:, c, LCTX:L], reads=[("X", c)], writes=[("out", b, c)])
                P.barrier()
        P.emit()
    nc._used_inputs = set(d.keys()) - {"out", "ml_sq", "ml_sv", "ml_so", "s5t", "s5wb", "s5wc"}
    return nc


K_CONST_COLS = 128 + 1 + DEPTH * 48 + DEPTH * NCH * 2 + NCH + DEPTH * 44 * 3 + 1 + 128 + 256 + 256 + 16 + 48 + NCH + 128 + 128 + 1 + 8 + 16 + 32


def make_consts(inp):
    cols = []
    cols.append(np.full((128, 128), 1.0 / D, np.float32))
    cols.append(np.full((128, 1), EPS, np.float32))
    ab = np.asarray(inp["ada_b"], np.float32).reshape(DEPTH, 48, 128).transpose(2, 0, 1).reshape(128, -1)
    cols.append(ab)
    for nm in ("norm1_g", "norm2_g"):
        g = np.asarray(inp[nm], np.float32).reshape(DEPTH, NCH, 128).transpose(2, 0, 1).reshape(128, -1)
        cols.append(g)
    cols.append(np.asarray(inp["final_g"], np.float32).reshape(NCH, 128).T)
    cw = np.asarray(inp["ffn_conv"], np.float32).reshape(DEPTH, 3, 44, 128).transpose(3, 0, 2, 1).reshape(128, -1)
    cols.append(cw)
    cols.append(np.ones((128, 1), np.float32))
    cols.append(np.eye(128, dtype=np.float32))
    si = np.arange(128)[:, None]
    ti = np.arange(128)[None, :]
    trif = (si <= ti).astype(np.float32)
    trib = (si >= ti).astype(np.float32)
    cols.append(np.concatenate([trif, trib], axis=1))
    cols.append(np.concatenate([(1 - trif) * NEGM, (1 - trib) * NEGM], axis=1).astype(np.float32))
    bif = np.asarray(inp["ml_b_if"], np.float32)[0].reshape(1, 16)
    cols.append(np.repeat(bif, 128, axis=0))
    mc_ = np.asarray(inp["ml_conv"], np.float32)[0]
    cwm = np.empty((128, 16, 3), np.float32)
    for qk in range(2):
        for hd in range(ML_H):
            base = qk * D + hd * ML_DH
            for ab in range(2):
                idx = np.concatenate([base + 64 * ab + np.arange(64), base + 128 + 64 * ab + np.arange(64)])
                cwm[:, (qk * ML_H + hd) * 2 + ab, :] = mc_[:, idx].T
    cols.append(cwm.reshape(128, 48))
    cols.append(np.asarray(inp["ml_norm_g"], np.float32)[0].reshape(NCH, 128).T)
    inv = (10000.0 ** (-np.arange(64, dtype=np.float32) / 64.0)).astype(np.float32)
    pos = np.arange(64, dtype=np.float32)
    ang = (pos[None, :] * inv[:, None]).astype(np.float32)
    tab = np.stack([np.cos(ang), np.sin(ang)], axis=1).astype(np.float32)
    tab = np.concatenate([tab, tab], axis=0)
    cols.append((tab / np.float32(16.0)).reshape(128, 128).astype(np.float32))
    cols.append(tab.reshape(128, 128))
    cols.append(np.full((128, 1), np.pi / 2, np.float32))
    cols.append((np.arange(128)[:, None] // 16 == np.arange(8)[None, :]).astype(np.float32))
    cols.append(np.asarray(inp["s5_d"], np.float32).reshape(2, NCH, 128).transpose(2, 0, 1).reshape(128, 16))
    cols.append(np.asarray(inp["s5_glu_b"], np.float32).reshape(2, 16, 128).transpose(2, 0, 1).reshape(128, 32))
    out = np.ascontiguousarray(np.concatenate(cols, axis=1), dtype=np.float32)
    assert out.shape[1] == K_CONST_COLS
    return out


def make_s5_params(inp):
    f = lambda k: np.asarray(inp[k], np.float32)
    lre, lim, ldt = f("s5_lam_re"), f("s5_lam_im"), f("s5_log_dt")
    bre, bim, cre, cim = f("s5_b_re"), f("s5_b_im"), f("s5_c_re"), f("s5_c_im")

    def lay1(a):
        a = a.reshape(2, 2, 32, 2, 64).transpose(0, 3, 4, 1, 2)
        return a.reshape(2, 128, 64)

    def lay2(a):
        a = a.reshape(2, 2, 8, 8, 64).transpose(0, 3, 1, 2, 4)
        a = np.broadcast_to(a[:, :, None], (2, 8, 16, 2, 8, 64))
        return a.reshape(2, 128, 1024)
    ldt4 = np.broadcast_to(ldt[..., None], (2, 2, 64, 64))
    p1 = np.stack([lay1(lre), lay1(lim), lay1(ldt4)], axis=2)
    b2 = []
    for bb in (bre, bim):
        a = bb.reshape(2, 2, 8, 8, 64, 16).transpose(0, 3, 5, 1, 2, 4)
        b2.append(a.reshape(2, 128, 1024))
    p2 = np.stack([lay2(lre), lay2(lim), lay2(ldt4), b2[0], b2[1]], axis=2)
    cc = []
    for c_ in (cre, cim):
        a = c_.reshape(2, 2, 32, 2, 16, 64).transpose(0, 3, 5, 1, 2, 4)
        cc.append(a.reshape(2, 128, 1024))
    s5c = np.stack(cc, axis=2)
    return (np.ascontiguousarray(p1, dtype=np.float32), np.ascontiguousarray(p2, dtype=np.float32),
            np.ascontiguousarray(s5c, dtype=np.float32))


def make_in_maps(inp, NB, ncores):
    x = np.asarray(inp["x"], np.float32)
    ctx = np.asarray(inp["ctx"], np.float32)
    c = np.asarray(inp["c"], np.float32)
    c_ctx = np.asarray(inp["c_ctx"], np.float32)
    consts = make_consts(inp)
    s5p1, s5p2, s5c = make_s5_params(inp)
    shared = {
        "s5p1": s5p1, "s5p2": s5p2, "s5c": s5c,
        "s5_glu_w": np.ascontiguousarray(inp["s5_glu_w"], dtype=np.float32),
        "ada_w": np.ascontiguousarray(inp["ada_w"], dtype=np.float32),
        "consts": consts,
        "ffn_w_in": np.ascontiguousarray(inp["ffn_w_in"], dtype=np.float32),
        "ffn_w_out": np.ascontiguousarray(inp["ffn_w_out"], dtype=np.float32),
        "ml_w_in": np.ascontiguousarray(inp["ml_w_in"], dtype=np.float32),
        "ml_w_out": np.ascontiguousarray(inp["ml_w_out"], dtype=np.float32),
        "ml_wif": np.ascontiguousarray(np.asarray(inp["ml_w_if"], np.float32)[0].transpose(1, 0, 2).reshape(NCH, 128, 16).transpose(1, 0, 2)),
        "na_w_qkv": np.ascontiguousarray(inp["na_w_qkv"], dtype=np.float32),
        "na_w_out": np.ascontiguousarray(inp["na_w_out"], dtype=np.float32),
        "na_bias": make_na_bias(np.asarray(inp["na_rpb"])[0])[None],
    }
    maps = []
    for core in range(ncores):
        b0 = core * NB
        xin = np.empty((NB, D, L), np.float32)
        for i in range(NB):
            xin[i, :, :LCTX] = ctx[b0 + i].T
            xin[i, :, LCTX:] = x[b0 + i].T
        cc = np.concatenate([c[b0:b0 + NB], np.zeros((4 - NB, D), np.float32), c_ctx[None]], axis=0)
        cT = np.ascontiguousarray(cc.reshape(5, NCH, 128).transpose(2, 1, 0))
        m = dict(shared)
        m["xin"] = xin.reshape(NB, NCH, 128, L)
        m["cT"] = cT
        maps.append(m)
    return maps


def gather_out(res, NB, ncores):
    outs = []
    for core in range(ncores):
        o = np.asarray(res.results[core]["out"]).reshape(NB, D, LLAT)
        outs.append(np.ascontiguousarray(o.transpose(0, 2, 1)))
    return np.concatenate(outs, axis=0)


def nc_inputs(nc):
    return getattr(nc, "_used_inputs")


def kernel(**inputs):
    NB = 4
    nc = build_program(NB=NB)
    maps = make_in_maps(inputs, NB, NCORES)
    maps = [{k: v for k, v in m.items() if k in nc_inputs(nc)} for m in maps]
    res = run_bass_kernel_spmd(nc, maps, core_ids=list(range(NCORES)))
    return gather_out(res, NB, NCORES).astype(np.float32)
```

```python
import contextlib
import numpy as np
import concourse.bass as bass
import concourse.mybir as mybir
from concourse.bass_utils import run_bass_kernel_spmd

F32 = mybir.dt.float32
BF16 = mybir.dt.bfloat16
I32 = mybir.dt.int32
AF = mybir.ActivationFunctionType
ALU = mybir.AluOpType
ENGS = ("pe", "act", "dve", "pool", "sp")
NDMASEM = 12

D = 1024
NCH = 8
LCTX = 256
LLAT = 2048
L = LCTX + LLAT
LP = L + 3
DEPTH = 4
DFF = 2816
NJ = DFF // 128
EPS = 1e-6
NCORES = 8


class Prog:
    def __init__(self, nc, es):
        self.nc = nc
        self.es = es
        self.ops = {e: [] for e in ENGS}
        self.cnt = {e: 0 for e in ENGS}
        self.sems = {}
        for e in ENGS:
            self.sems[("e", e)] = es.enter_context(nc.semaphore("sem_" + e))
        self.dman = {e: 0 for e in ENGS}
        for e in ("sp", "pool", "act"):
            for i in range(NDMASEM):
                self.sems[("d", e, i)] = es.enter_context(nc.semaphore(f"dsem_{e}_{i}"))
        self.seen = {e: {} for e in ENGS}
        self.bw = {}
        self.br = {}

    def _deps(self, eng, reads, writes, extra=()):
        d = {}

        def add(ev):
            if ev is None:
                return
            sk, v = ev
            if d.get(sk, 0) < v:
                d[sk] = v
        for r in reads:
            add(self.bw.get(r))
        for w in writes:
            add(self.bw.get(w))
            for sk, v in self.br.get(w, {}).items():
                add((sk, v))
        for ev in extra:
            add(ev)
        out = []
        seen = self.seen[eng]
        for sk, v in d.items():
            if eng == "pe" and sk == ("e", "pe"):
                continue
            if seen.get(sk, 0) >= v:
                continue
            seen[sk] = v
            out.append((sk, v))
        return out

    def _commit(self, ev, reads, writes):
        sk, v = ev
        for r in reads:
            self.br.setdefault(r, {})[sk] = v
        for w in writes:
            self.bw[w] = ev
            self.br[w] = {}

    def op(self, eng, fn, reads=(), writes=()):
        reads = list(reads)
        writes = list(writes)
        waits = self._deps(eng, reads, writes)
        self.cnt[eng] += 1
        ev = (("e", eng), self.cnt[eng])
        self.ops[eng].append((waits, fn, (("e", eng), 1)))
        self._commit(ev, reads, writes)
        return ev

    def dma(self, eng, out, in_, reads=(), writes=(), **kw):
        reads = list(reads)
        writes = list(writes)
        n = self.dman[eng]
        self.dman[eng] += 1
        slot = n % NDMASEM
        use = n // NDMASEM + 1
        sk = ("d", eng, slot)
        extra = [(sk, 16 * (use - 1))] if use > 1 else []
        waits = self._deps(eng, reads, writes, extra)
        ev = (sk, 16 * use)
        self.ops[eng].append((waits, (lambda e: e.dma_start(out=out, in_=in_, **kw)), (sk, 16)))
        self._commit(ev, reads, writes)
        return ev

    def barrier(self):
        evs = [(("e", e), self.cnt[e]) for e in ENGS if self.cnt[e] > 0]
        for e in ("sp", "pool", "act"):
            n = self.dman[e]
            for slot in range(min(n, NDMASEM)):
                uses = (n - 1 - slot) // NDMASEM + 1
                evs.append((("d", e, slot), 16 * uses))
        for eng in ENGS:
            waits = []
            for sk, v in evs:
                if sk == ("e", eng) and eng == "pe":
                    continue
                if self.seen[eng].get(sk, 0) >= v:
                    continue
                self.seen[eng][sk] = v
                waits.append((sk, v))
            if waits:
                self.ops[eng].append((waits, None, None))
        self.bw = {}
        self.br = {}

    def wait_all(self, eng, keys):
        waits = self._deps(eng, keys, [])
        self.ops[eng].append((waits, None, None))

    def emit(self):
        nc = self.nc
        with nc.Block() as block:
            def run(engobj, name):
                for waits, fn, inc in self.ops[name]:
                    attach = None
                    if fn is not None and name != "pe" and inc is not None and inc[0][0] == "e" and waits:
                        attach = waits[-1]
                        waits = waits[:-1]
                    for sk, v in waits:
                        engobj.wait_ge(self.sems[sk], v)
                    if fn is None:
                        continue
                    ins = fn(engobj)
                    if attach is not None:
                        ins._wait_ge(self.sems[attach[0]], attach[1])
                    if inc is not None:
                        ins.then_inc(self.sems[inc[0]], inc[1])

            @block.sync
            def _(e):
                run(e, "sp")

            @block.scalar
            def _(e):
                run(e, "act")

            @block.vector
            def _(e):
                run(e, "dve")

            @block.gpsimd
            def _(e):
                run(e, "pool")

            @block.tensor
            def _(e):
                run(e, "pe")

    def mm(self, out, lhsT, rhs, start, stop, reads, writes):
        return self.op("pe", lambda e: e.matmul(out, lhsT=lhsT, rhs=rhs, start=start, stop=stop), reads, writes)

    def tr(self, out, in_, ident, reads, writes):
        return self.op("pe", lambda e: e.transpose(out, in_, ident), reads, writes)

    def act(self, out, in_, func, reads, writes, scale=None, bias=None):
        kw = {}
        if scale is not None:
            kw["scale"] = scale
        if bias is not None:
            kw["bias"] = bias
        return self.op("act", lambda e: e.activation(out=out, in_=in_, func=func, **kw), reads, writes)

    def ts(self, eng, out, in0, s1, s2, op0, op1, reads, writes):
        if op1 is None:
            return self.op(eng, lambda e: e.tensor_scalar(out=out, in0=in0, scalar1=s1, scalar2=None, op0=op0), reads, writes)
        return self.op(eng, lambda e: e.tensor_scalar(out=out, in0=in0, scalar1=s1, scalar2=s2, op0=op0, op1=op1), reads, writes)

    def stt(self, out, in0, scalar, in1, op0, op1, reads, writes):
        return self.op("dve", lambda e: e.scalar_tensor_tensor(out=out, in0=in0, scalar=scalar, in1=in1, op0=op0, op1=op1), reads, writes)

    def tt(self, eng, out, in0, in1, op, reads, writes):
        return self.op(eng, lambda e: e.tensor_tensor(out=out, in0=in0, in1=in1, op=op), reads, writes)

    def copy(self, eng, out, in_, reads, writes):
        if eng == "act":
            return self.op("act", lambda e: e.copy(out=out, in_=in_), reads, writes)
        return self.op(eng, lambda e: e.tensor_copy(out=out, in_=in_), reads, writes)

    def memset(self, eng, ap, val, writes):
        return self.op(eng, lambda e: e.memset(ap, val), (), writes)

    def recip(self, out, in_, reads, writes):
        return self.op("dve", lambda e: e.reciprocal(out=out, in_=in_), reads, writes)

    def scan(self, out, d0, d1, init, reads, writes):
        return self.op("dve", lambda e: e.tensor_tensor_scan(out=out, data0=d0, data1=d1, initial=init, op0=ALU.mult, op1=ALU.add), reads, writes)


class Arena:
    def __init__(self, t, nbytes):
        self.t = t
        self.n = nbytes
        self.off = 0
        self.marks = []

    def mark(self):
        self.marks.append(self.off)

    def release(self):
        self.off = self.marks.pop()

    def alloc(self, shape, dtype):
        esz = 2 if dtype == BF16 else 4
        n = int(np.prod(shape)) * esz
        n4 = (n + 63) // 64 * 64
        assert self.off + n4 <= self.n, f"arena overflow {self.off}+{n4}>{self.n}"
        a = self.t[:, self.off // 4:(self.off + n4) // 4]
        self.off += n4
        if dtype != F32:
            a = a.bitcast(dtype)
        a = a[:, 0:int(np.prod(shape))]
        if len(shape) == 2:
            return a.rearrange("p (a b) -> p a b", a=shape[0])
        if len(shape) == 3:
            return a.rearrange("p (a b c) -> p a b c", a=shape[0], b=shape[1])
        return a


class Ctx:
    pass


def ffn_blocks():
    blks = [(0, LCTX + 2)]
    sizes = [410, 410, 410, 410, 408]
    s = 0
    for sz in sizes:
        blks.append((LCTX + 1 + s, sz + 2))
        s += sz
    return blks


def tok_blocks():
    out = [(0, LCTX, 1)]
    for i in range(4):
        out.append((LCTX + 512 * i, 512, LCTX + 2 + 512 * i))
    return out


def emit_mods(P, K):
    nc = P.nc
    ar = K.arena
    ar.mark()
    cs = ar.alloc([NCH, 5], F32)
    P.dma("sp", cs, K.d["cT"][:, :, :], writes=["cs"])
    P.act(cs, cs, AF.Silu, ["cs"], ["cs"])
    wt = [ar.alloc([NCH, 512], F32) for _ in range(3)]
    n = 0
    for l in range(DEPTH):
        wv = K.d["ada_w"][l].rearrange("(kc p) n -> p kc n", p=128)
        psb = K.ps[l % 2]
        for jg in range(12):
            buf = n % 3
            n += 1
            P.dma("sp", wt[buf], wv[:, :, jg * 512:(jg + 1) * 512], writes=[("adaw", buf)])
            for jj in range(4):
                j = jg * 4 + jj
                for kc in range(NCH):
                    P.mm(psb[:, j * 5:(j + 1) * 5], wt[buf][:, kc, jj * 128:(jj + 1) * 128], cs[:, kc, :],
                         kc == 0, kc == NCH - 1, [("adaw", buf), "cs"], [("ps", l % 2)])
        P.tt("dve", K.MOD[:, l, :, :], psb[:, 0:240].rearrange("p (j b) -> p j b", b=5),
             K.adab[:, l, :].unsqueeze(2).to_broadcast([128, 48, 5]), ALU.add,
             [("ps", l % 2), "consts"], [("MOD", l)])
        for (dst, g, s) in ((K.A1, K.n1g, 1), (K.A2, K.n2g, 4)):
            P.ts("dve", dst[:, l, :, :], K.MOD[:, l, s * 8:(s + 1) * 8, :], 1.0, None, ALU.add, None,
                 [("MOD", l)], [("A", l)])
            P.tt("dve", dst[:, l, :, :], dst[:, l, :, :], g[:, l, :].unsqueeze(2).to_broadcast([128, NCH, 5]),
                 ALU.mult, [("A", l), "consts"], [("A", l)])
    ar.release()
    P.barrier()


def emit_rstd(P, K, tagp):
    K.RSTD = K.arena.alloc([1, L], F32)[:, 0, :]
    for bi, (c0, n, _) in enumerate(tok_blocks()):
        pb = K.ps[7]
        for c in range(NCH):
            sq = K.sq[c % 2]
            P.act(sq[:, 0:n], K.X[:, c, c0:c0 + n], AF.Square, [("X", c)], [("sq", c % 2)])
            P.mm(pb[:, 0:n], K.ones, sq[:, 0:n], c == 0, c == NCH - 1, [("sq", c % 2), "consts"], [("ps", 7)])
        P.act(K.RSTD[:, c0:c0 + n], pb[:, 0:n], AF.Sqrt, [("ps", 7), "consts"], [("rstd", bi)], bias=K.epsc[:, 0:1])
        P.recip(K.RSTD[:, c0:c0 + n], K.RSTD[:, c0:c0 + n], [("rstd", bi)], [("rstd", bi)])


RSTD_KEYS = [("rstd", i) for i in range(5)]


def emit_norm_mod(P, K, A, SH, l, b, H, keep_rstd=False):
    K.arena.mark()
    emit_rstd(P, K, "n")
    i = 0
    for bi, (c0, n, p0) in enumerate(tok_blocks()):
        bcol = 4 if bi == 0 else b
        for c in range(NCH):
            tmp = K.tmpB[i % 3]
            tk = ("tmpB", i % 3)
            i += 1
            P.tt("dve", tmp[:, 0:n], K.X[:, c, c0:c0 + n], K.RSTD[:, c0:c0 + n], ALU.mult, [("X", c), ("rstd", bi)], [tk])
            P.act(H[:, c, p0:p0 + n], tmp[:, 0:n], AF.Identity, [tk, ("MOD", l), ("A", l)], [("H", c)],
                  scale=A[:, l, c, bcol:bcol + 1], bias=SH[:, c, bcol:bcol + 1])
    if keep_rstd:
        K.arena.marks.pop()
    else:
        K.arena.release()
        P.barrier()


def zero_pads(P, H, nchunks, keyname):
    for col in (0, LCTX + 1, LP - 1):
        P.memset("pool", H[:, :, col:col + 1], 0.0, [(keyname, c) for c in range(nchunks)])


def emit_ffn(P, K, l, b):
    ar = K.arena
    ar.mark()
    H = ar.alloc([NCH, LP], BF16)
    zero_pads(P, H, NCH, "H")
    emit_norm_mod(P, K, K.A2, K.MOD[:, l, 24:32, :], l, b, H)
    G2 = K.MOD[:, l, 40:48, :]
    groups = [(0, 4), (4, 4), (8, 4), (12, 4), (16, 3), (19, 3)]
    GM = 4
    M = ar.alloc([GM, LP], BF16)
    win = [ar.alloc([2, NCH, 128], BF16) for _ in range(3)]
    wout = [ar.alloc([GM, D], BF16) for _ in range(2)]
    acc = [[ar.alloc([1, 412], F32) for _ in range(3)] for _ in range(2)]
    wiv = K.d["ffn_w_in"][l].rearrange("(kc p) n -> p kc n", p=128)
    wov = K.d["ffn_w_out"][l].rearrange("(j p) n -> p j n", p=128)
    fb = ffn_blocks()
    nw = 0
    nacc = 0
    nps = 0
    issued = set()

    def issue_win(j):
        if j >= NJ or j in issued:
            return
        issued.add(j)
        wb_ = j % 3
        P.dma("pool", win[wb_][:, 0, :, :], wiv[:, :, j * 128:(j + 1) * 128], writes=[("win", wb_, 0)])
        P.dma("pool", win[wb_][:, 1, :, :], wiv[:, :, (NJ + j) * 128:(NJ + j + 1) * 128], writes=[("win", wb_, 1)])

    def issue_wout(gi_):
        if gi_ >= len(groups) or ("g", gi_) in issued:
            return
        issued.add(("g", gi_))
        j0_, nj_ = groups[gi_]
        P.dma("pool", wout[gi_ % 2][:, 0:nj_, :], wov[:, j0_:j0_ + nj_, :], writes=[("wout", gi_ % 2)])
    issue_win(0)
    issue_wout(0)
    for gi, (j0, nj) in enumerate(groups):
        wo = wout[gi % 2]
        for jl in range(nj):
            j = j0 + jl
            wb = j % 3
            issue_win(j)
            issue_win(j + 1)
            if jl == 1:
                issue_wout(gi + 1)
            for (c0, n) in fb:
                pa = nps % 2
                nps += 1
                ab = nacc % 2
                nacc += 1
                for half in range(2):
                    pst = K.ps[2 * half + pa]
                    for kc in range(NCH):
                        P.mm(pst[:, 0:n], win[wb][:, half, kc, :], H[:, kc, c0:c0 + n], kc == 0, kc == NCH - 1,
                             [("win", wb, half), ("H", kc)], [("ps", 2 * half + pa)])
                no = n - 2
                accs = acc[ab]
                for half in range(2):
                    pst = K.ps[2 * half + pa]
                    ch = j if half == 0 else NJ + j
                    a = accs[half][:, 0, 0:no]
                    kr = [("ps", 2 * half + pa), "consts"]
                    kw = [("acc", ab, half)]
                    P.act(a, pst[:, 1:1 + no], AF.Identity, kr, kw, scale=K.cw[:, l, ch, 1:2])
                    P.stt(a, pst[:, 0:no], K.cw[:, l, ch, 0:1], a, ALU.mult, ALU.add, kr + kw, kw)
                    P.stt(a, pst[:, 2:2 + no], K.cw[:, l, ch, 2:3], a, ALU.mult, ALU.add, kr + kw, kw)
                sg = accs[2][:, 0, 0:no]
                P.act(sg, accs[1][:, 0, 0:no], AF.Silu, [("acc", ab, 1)], [("acc", ab, 2)])
                P.tt("pool", M[:, jl, c0 + 1:c0 + 1 + no], accs[0][:, 0, 0:no], sg, ALU.mult,
                     [("acc", ab, 0), ("acc", ab, 2)], [("M", jl)])
        for oc in range(NCH):
            for (x0, n, p0) in tok_blocks():
                pi = 4 + (nps % 2)
                nps += 1
                for jl in range(nj):
                    P.mm(K.ps[pi][:, 0:n], wo[:, jl, oc * 128:(oc + 1) * 128], M[:, jl, p0:p0 + n], jl == 0, jl == nj - 1,
                         [("wout", gi % 2), ("M", jl)], [("ps", pi)])
                bcol = 4 if x0 == 0 else b
                P.stt(K.X[:, oc, x0:x0 + n], K.ps[pi][:, 0:n], G2[:, oc, bcol:bcol + 1], K.X[:, oc, x0:x0 + n],
                      ALU.mult, ALU.add, [("ps", pi), ("MOD", l), ("X", oc)], [("X", oc)])
    ar.release()
    P.barrier()


NA_H = 16
GRID_W = 64
NA_NT = 21


def na_qtile_info(i):
    if i == 0:
        return list(range(0, 4)), 5
    if i == 1:
        return list(range(0, 4)), 9
    if i == 14:
        return list(range(12, 16)), 13
    if i == 15:
        return list(range(12, 16)), 17
    return list(range(i - 2, i + 3)), 0


def make_na_bias(rpb):
    rpb = np.asarray(rpb, np.float32)
    tiles = [(5, j) for j in range(3, 8)]
    for i in (0, 1):
        tiles += [(i, j) for j in range(0, 4)]
    for i in (14, 15):
        tiles += [(i, j) for j in range(12, 16)]
    assert len(tiles) == NA_NT
    out = np.empty((NA_H, 128, NA_NT, 128), np.float32)
    a = np.arange(2)[:, None]
    col = np.arange(64)[None, :]
    for ti, (i, j) in enumerate(tiles):
        kr = (2 * j + a + 0 * col).reshape(128)
        kc = (0 * a + col).reshape(128)
        qr = (2 * i + a + 0 * col).reshape(128)
        qc = kc.copy()
        r0 = np.clip(qr - 4, 0, 24)
        c0 = np.clip(qc - 8, 0, 48)
        valid = ((kr[:, None] >= r0[None, :]) & (kr[:, None] <= r0[None, :] + 7)
                 & (kc[:, None] >= c0[None, :]) & (kc[:, None] < c0[None, :] + 16))
        dr = np.clip(kr[:, None] - qr[None, :] + 7, 0, 14)
        dc = np.clip(kc[:, None] - qc[None, :], -15, 15) + 15
        vals = rpb[:, dr, dc]
        out[:, :, ti, :] = np.where(valid[None], vals, np.float32(-30000.0))
    return np.ascontiguousarray(out.reshape(NA_H, 128, NA_NT * 128))


def emit_down_proj(P, K, wo, OT, G, b, l, nk, wkey, okey):
    n_ = 0
    for oc in range(NCH):
        for (x0, n, p0) in tok_blocks():
            pi = 4 + (n_ % 2)
            n_ += 1
            for kc in range(nk):
                P.mm(K.ps[pi][:, 0:n], wo[:, kc, oc * 128:(oc + 1) * 128], OT[:, kc, x0:x0 + n], kc == 0, kc == nk - 1,
                     [wkey, (okey, kc)], [("ps", pi)])
            bcol = 4 if x0 == 0 else b
            P.stt(K.X[:, oc, x0:x0 + n], K.ps[pi][:, 0:n], G[:, oc, bcol:bcol + 1], K.X[:, oc, x0:x0 + n],
                  ALU.mult, ALU.add, [("ps", pi), ("MOD", l), ("X", oc)], [("X", oc)])


def emit_na(P, K, l, b):
    jn = l // 3
    ar = K.arena
    ar.mark()
    H = ar.alloc([NCH, LP], BF16)
    emit_norm_mod(P, K, K.A1, K.MOD[:, l, 0:8, :], l, b, H)
    G1 = K.MOD[:, l, 16:24, :]
    OTs = [ar.alloc([1, L], BF16) for _ in range(2)]
    wos = [ar.alloc([1, D], BF16) for _ in range(2)]
    wqkv = [ar.alloc([3, NCH, 128], BF16) for _ in range(2)]
    Qt = ar.alloc([1, L], BF16)[:, 0, :]
    Kt = ar.alloc([1, L], BF16)[:, 0, :]
    V = ar.alloc([18, 128], BF16)
    BI1 = ar.alloc([NA_NT, 128], F32)
    BI = [BI1, BI1]
    Tb = [ar.alloc([1, 640], F32)[:, 0, :] for _ in range(2)]
    PT = [ar.alloc([1, 896], BF16)[:, 0, :] for _ in range(2)]
    rden = [ar.alloc([1, 128], F32)[:, 0, :] for _ in range(2)]
    onesb = ar.alloc([1, 128], BF16)[:, 0, :]
    P.memset("pool", onesb, 1.0, ["onesb"])
    wv_ = K.d["na_w_qkv"][jn].rearrange("(kc p) n -> p kc n", p=128)
    nu = 0
    npj = 0
    wov_ = K.d["na_w_out"][jn].rearrange("(kc p) n -> p kc n", p=128)
    for hp in range(NCH):
        wb = hp % 2
        OT = OTs[wb]
        for t3 in range(3):
            P.dma("pool", wqkv[wb][:, t3, :, :], wv_[:, :, t3 * D + hp * 128:t3 * D + (hp + 1) * 128], writes=[("wqkv", wb, t3)])
        P.dma("pool", wos[wb], wov_[:, hp:hp + 1, :], writes=[("wos", wb)])
        for (x0, n, p0) in tok_blocks():
            for t3, dst, dk_ in ((0, Qt, "Qt"), (1, Kt, "Kt")):
                pi = 6 + (npj % 2)
                npj += 1
                for kc in range(NCH):
                    P.mm(K.ps[pi][:, 0:n], wqkv[wb][:, t3, kc, :], H[:, kc, p0:p0 + n], kc == 0, kc == NCH - 1,
                         [("wqkv", wb, t3), ("H", kc)], [("ps", pi)])
                if t3 == 0:
                    P.act(dst[:, x0:x0 + n], K.ps[pi][:, 0:n], AF.Identity, [("ps", pi)], [dk_], scale=0.125)
                else:
                    P.copy("dve", dst[:, x0:x0 + n], K.ps[pi][:, 0:n], [("ps", pi)], [dk_])
        for tt in range(18):
            pc = 1 + 128 * tt if tt < 2 else LCTX + 2 + 128 * (tt - 2)
            pi = 6 + (npj % 2)
            npj += 1
            for kc in range(NCH):
                P.mm(K.ps[pi][:, 0:128], H[:, kc, pc:pc + 128], wqkv[wb][:, 2, kc, :], kc == 0, kc == NCH - 1,
                     [("wqkv", wb, 2), ("H", kc)], [("ps", pi)])
            P.copy("act", V[:, tt, :], K.ps[pi][:, 0:128], [("ps", pi)], ["V"])
        for hh in range(2):
            h = 2 * hp + hh
            hb = 64 * hh
            P.dma("sp", BI[hh], K.d["na_bias"][jn, h].rearrange("p (t q) -> p t q", q=128), writes=[("BI", 0)])
            for qt in range(18):
                u = nu % 2
                nu += 1
                if qt < 2:
                    lat_k, base = [], 0
                else:
                    lat_k, base = na_qtile_info(qt - 2)
                nk = len(lat_k)
                ktiles = [2 + j for j in lat_k] + [0, 1]
                qs = slice(qt * 128, (qt + 1) * 128)

                def sreg(i0, i1):
                    assert i0 // 4 == (i1 - 1) // 4
                    bk = 2 * u + i0 // 4
                    return K.ps[bk][:, (i0 % 4) * 128:(i0 % 4) * 128 + (i1 - i0) * 128], ("ps", bk)
                for idx, kt in enumerate(ktiles):
                    reg, rk = sreg(idx, idx + 1)
                    P.mm(reg, Kt[hb:hb + 64, kt * 128:(kt + 1) * 128], Qt[hb:hb + 64, qs],
                         True, True, ["Qt", "Kt"], [rk])
                if nk:
                    for (i0, i1) in ((0, min(nk, 4)), (4, nk)):
                        if i1 <= i0:
                            continue
                        reg, rk = sreg(i0, i1)
                        P.tt("dve", Tb[u][:, i0 * 128:i1 * 128], reg,
                             BI[hh][:, base + i0:base + i1, :].rearrange("p t q -> p (t q)"), ALU.add,
                             [rk, ("BI", 0)], [("Tb", u)])
                    P.act(PT[u][:, 0:nk * 128], Tb[u][:, 0:nk * 128], AF.Exp, [("Tb", u)], [("PT", u)])
                reg, rk = sreg(nk, nk + 2)
                P.act(PT[u][:, nk * 128:(nk + 2) * 128], reg, AF.Exp, [rk], [("PT", u)])
                Ops = K.ps[4 + u]
                nkt = len(ktiles)
                for idx, kt in enumerate(ktiles):
                    P.mm(Ops[:, 0:128], V[:, kt, :], PT[u][:, idx * 128:(idx + 1) * 128], idx == 0, idx == nkt - 1,
                         ["V", ("PT", u)], [("ps", 4 + u)])
                for idx, kt in enumerate(ktiles):
                    P.mm(Ops[:, 128:256], onesb, PT[u][:, idx * 128:(idx + 1) * 128], idx == 0, idx == nkt - 1,
                         ["onesb", ("PT", u)], [("ps", 4 + u)])
                P.recip(rden[u][hb:hb + 64, :], Ops[hb:hb + 64, 128:256], [("ps", 4 + u)], [("rden", u)])
                P.tt("dve", OT[hb:hb + 64, 0, qs], Ops[hb:hb + 64, 0:128], rden[u][hb:hb + 64, :], ALU.mult,
                     [("ps", 4 + u), ("rden", u)], [(("OT", wb), 0)])
        emit_down_proj(P, K, wos[wb], OT, G1, b, l, 1, ("wos", wb), ("OT", wb))
    ar.release()
    P.barrier()


ML_H = 4
ML_DH = 256
NEGM = -30000.0


def ml_blocks():
    blks = [(0, LCTX + 2, None, 0)]
    s = 0
    for sz in (448, 448, 448, 448, 256):
        blks.append((LCTX + 1 + s, sz + 2, s // 64, sz // 64))
        s += sz
    return blks


def tile_pcol(tt):
    return 1 + 128 * tt if tt < 2 else LCTX + 2 + 128 * (tt - 2)


def emit_mlstm(P, K, l, b):
    jn = l // 3
    ar = K.arena
    d = K.d
    G1 = K.MOD[:, l, 16:24, :]
    ar.mark()
    GT = ar.alloc([18, 16], F32)
    LFt = ar.alloc([18, 8], F32)
    IMB = ar.alloc([18, 8], F32)
    identb = ar.alloc([1, 128], BF16)[:, 0, :]
    onesb = ar.alloc([1, 128], BF16)[:, 0, :]
    P.memset("pool", onesb, 1.0, ["onesb"])
    P.copy("dve", identb, K.ident, ["consts"], ["identb"])
    ar.mark()
    H = ar.alloc([NCH, LP], BF16)
    zero_pads(P, H, NCH, "H")
    emit_norm_mod(P, K, K.A1, K.MOD[:, l, 0:8, :], l, b, H)
    wif = ar.alloc([NCH, 16], BF16)
    P.dma("pool", wif, d["ml_wif"][:, :, :], writes=["wif"])
    for tt in range(18):
        pc = tile_pcol(tt)
        for kc in range(NCH):
            P.mm(K.ps[7][:, tt * 16:(tt + 1) * 16], H[:, kc, pc:pc + 128], wif[:, kc, :], kc == 0, kc == NCH - 1,
                 ["wif", ("H", kc)], [("ps", 7)])
    P.tt("dve", GT, K.ps[7][:, 0:288].rearrange("p (t g) -> p t g", g=16),
         K.mlbif.unsqueeze(1).to_broadcast([128, 18, 16]), ALU.add, [("ps", 7), "consts"], ["GT"])
    GTv = GT.rearrange("p t (d g) -> p t d g", d=2)
    LFv = LFt.rearrange("p t (d h) -> p t d h", d=2)
    IMv = IMB.rearrange("p t (d h) -> p t d h", d=2)
    P.act(LFv, GTv[:, :, :, 4:8], AF.Exp, ["GT"], ["LFt"], scale=-1.0)
    P.act(LFv, LFv, AF.Ln, ["LFt", "consts"], ["LFt"], bias=K.onec[:, 0:1])
    P.ts("dve", LFt, LFt, -1.0, None, ALU.mult, None, ["LFt"], ["LFt"])
    for tt in range(18):
        for dr in range(2):
            P.mm(K.ps[6][:, tt * 8 + dr * 4:tt * 8 + dr * 4 + 4], K.tri[:, dr, :], LFt[:, tt, dr * 4:(dr + 1) * 4], True, True,
                 ["LFt", "consts"], [("ps", 6)])
    P.tt("dve", IMv, GTv[:, :, :, 0:4], K.ps[6][:, 0:144].rearrange("p (t d h) -> p t d h", d=2, h=4), ALU.subtract,
         ["GT", ("ps", 6)], ["IMB"])
    wqk = ar.alloc([4, NCH, 128], BF16)
    wvo = ar.alloc([2, NCH, 256], BF16)
    QK = ar.alloc([4, L], BF16)
    Vst = ar.alloc([18, 256], BF16)
    Og = ar.alloc([2, L], BF16)
    accs = [[ar.alloc([1, 450], F32)[:, 0, :] for _ in range(2)] for _ in range(2)]
    rt = [ar.alloc([1, 448], F32)[:, 0, :] for _ in range(4)]
    wv_ = d["ml_w_in"][jn].rearrange("(kc p) n -> p kc n", p=128)
    npj = 0
    for hd in range(ML_H):
        for qk in range(2):
            base = qk * D + hd * ML_DH
            for ab in range(2):
                for hf in range(2):
                    c0 = base + 128 * hf + 64 * ab
                    P.dma("pool", wqk[:, 2 * qk + ab, :, 64 * hf:64 * hf + 64], wv_[:, :, c0:c0 + 64], writes=[("wqk", 2 * qk + ab)])
        for vo in range(2):
            c0 = (2 + vo) * D + hd * ML_DH
            P.dma("pool", wvo[:, vo, :, :], wv_[:, :, c0:c0 + 256], writes=[("wvo", vo)])
        for qk in range(2):
            for (c0, n, r0, nr) in ml_blocks():
                no = n - 2
                for ab in range(2):
                    pi = 2 * ab + (npj % 2)
                    for kc in range(NCH):
                        P.mm(K.ps[pi][:, 0:n], wqk[:, 2 * qk + ab, kc, :], H[:, kc, c0:c0 + n], kc == 0, kc == NCH - 1,
                             [("wqk", 2 * qk + ab), ("H", kc)], [("ps", pi)])
                    a = accs[ab][npj % 2][:, 0:no]
                    ak = ("macc", ab, npj % 2)
                    cwi = (qk * ML_H + hd) * 2 + ab
                    P.act(a, K.ps[pi][:, 1:1 + no], AF.Identity, [("ps", pi), "consts"], [ak], scale=K.mlcw[:, cwi, 1:2])
                    P.stt(a, K.ps[pi][:, 0:no], K.mlcw[:, cwi, 0:1], a, ALU.mult, ALU.add, [("ps", pi), "consts", ak], [ak])
                    P.stt(a, K.ps[pi][:, 2:2 + no], K.mlcw[:, cwi, 2:3], a, ALU.mult, ALU.add, [("ps", pi), "consts", ak], [ak])
                    P.act(a, a, AF.Silu, [ak], [ak])
                A_ = accs[0][npj % 2][:, 0:no]
                B_ = accs[1][npj % 2][:, 0:no]
                kA = ("macc", 0, npj % 2)
                kB = ("macc", 1, npj % 2)
                npj += 1
                x0 = c0
                if r0 is None:
                    for ab, src, sk in ((0, A_, kA), (1, B_, kB)):
                        if qk == 0:
                            P.act(QK[:, ab, 0:LCTX], src, AF.Identity, [sk], [("QK", ab)], scale=1.0 / 16.0)
                        else:
                            P.copy("dve", QK[:, 2 + ab, 0:LCTX], src, [sk], [("QK", 2 + ab)])
                    continue
                xs = c0 - 1
                tb = K.ropeq if qk == 0 else K.ropek
                for hfp in range(2):
                    ps_ = slice(64 * hfp, 64 * hfp + 64)
                    if hfp == 0:
                        cosv = tb[ps_, 0, r0:r0 + nr].unsqueeze(2).to_broadcast([64, nr, 64])
                        sinv = tb[ps_, 1, r0:r0 + nr].unsqueeze(2).to_broadcast([64, nr, 64])
                    else:
                        cosv = tb[ps_, 0, :].unsqueeze(1).to_broadcast([64, nr, 64])
                        sinv = tb[ps_, 1, :].unsqueeze(1).to_broadcast([64, nr, 64])

                    def v3(t_):
                        return t_[ps_, 0:no].rearrange("p (r c) -> p r c", c=64)
                    rk = [("rt", i, hfp) for i in range(4)]
                    P.tt("pool", v3(rt[0]), v3(A_), cosv, ALU.mult, [kA, "consts"], [rk[0]])
                    P.tt("pool", v3(rt[1]), v3(B_), sinv, ALU.mult, [kB, "consts"], [rk[1]])
                    P.tt("pool", v3(rt[2]), v3(A_), sinv, ALU.mult, [kA, "consts"], [rk[2]])
                    P.tt("pool", v3(rt[3]), v3(B_), cosv, ALU.mult, [kB, "consts"], [rk[3]])
                    P.tt("dve", QK[ps_, 2 * qk + 0, xs:xs + no], rt[0][ps_, 0:no], rt[1][ps_, 0:no], ALU.subtract,
                         [rk[0], rk[1]], [("QK", 2 * qk)])
                    P.tt("dve", QK[ps_, 2 * qk + 1, xs:xs + no], rt[2][ps_, 0:no], rt[3][ps_, 0:no], ALU.add,
                         [rk[2], rk[3]], [("QK", 2 * qk + 1)])
        for tt in range(18):
            pc = tile_pcol(tt)
            pi = 4 + (tt % 2)
            for kc in range(NCH):
                P.mm(K.ps[pi][:, 0:256], H[:, kc, pc:pc + 128], wvo[:, 0, kc, :], kc == 0, kc == NCH - 1,
                     [("wvo", 0), ("H", kc)], [("ps", pi)])
            P.copy("act", Vst[:, tt, :], K.ps[pi][:, 0:256], [("ps", pi)], ["Vst"])
        n_ = 0
        for mc in range(2):
            for (x0, n, p0) in tok_blocks():
                pi = 4 + (n_ % 2)
                n_ += 1
                for kc in range(NCH):
                    P.mm(K.ps[pi][:, 0:n], wvo[:, 1, kc, mc * 128:(mc + 1) * 128], H[:, kc, p0:p0 + n], kc == 0, kc == NCH - 1,
                         [("wvo", 1), ("H", kc)], [("ps", pi)])
                P.act(Og[:, mc, x0:x0 + n], K.ps[pi][:, 0:n], AF.Sigmoid, [("ps", pi)], [("Og", mc)])
        P.dma("sp", d["ml_sq"][hd], QK, reads=[("QK", i) for i in range(4)], writes=[("ml_sq", hd)])
        P.dma("sp", d["ml_sv"][hd], Vst, reads=["Vst"], writes=[("ml_sv", hd)])
        P.dma("sp", d["ml_so"][hd], Og, reads=[("Og", 0), ("Og", 1)], writes=[("ml_so", hd)])
    ar.release()
    P.barrier()
    ar.mark()
    QK = ar.alloc([4, L], BF16)
    Vst = ar.alloc([18, 256], BF16)
    Og = ar.alloc([2, L], BF16)
    HS = ar.alloc([2, L], F32)
    wo = ar.alloc([2, D], BF16)
    Cst = [ar.alloc([2, 384], F32) for _ in range(2)]
    Cb = [ar.alloc([2, 384], BF16) for _ in range(2)]
    LFr = [ar.alloc([1, 128], F32)[:, 0, :] for _ in range(2)]
    Targ = [ar.alloc([1, 128], F32)[:, 0, :] for _ in range(2)]
    Dm = [ar.alloc([1, 128], F32)[:, 0, :] for _ in range(2)]
    WT = [ar.alloc([1, 128], BF16)[:, 0, :] for _ in range(2)]
    Ebc = [ar.alloc([1, 128], F32)[:, 0, :] for _ in range(2)]
    Qtl = [ar.alloc([2, 128], BF16) for _ in range(2)]
    ktl = [ar.alloc([1, 256], BF16)[:, 0, :] for _ in range(2)]
    rr = [ar.alloc([1, 128], F32)[:, 0, :] for _ in range(2)]
    tmo = [ar.alloc([1, 128], F32)[:, 0, :] for _ in range(2)]
    smallc = [ar.alloc([1, 4], F32)[:, 0, :] for _ in range(2)]
    psT = K.ps[6].bitcast(BF16)
    wov = d["ml_w_out"][jn].rearrange("(kc p) n -> p kc n", p=128)
    for hd in range(ML_H):
        P.dma("sp", QK, d["ml_sq"][hd], writes=[("QK", i) for i in range(4)])
        P.dma("sp", Vst, d["ml_sv"][hd], writes=["Vst"])
        P.dma("sp", Og, d["ml_so"][hd], writes=[("Og", 0), ("Og", 1)])
        P.dma("pool", wo, wov[:, 2 * hd:2 * hd + 2, :], writes=["wo"])
        it = 0
        for dr in range(2):
            order = [0, 1] + list(range(2, 18)) if dr == 0 else [1, 0] + list(range(17, 1, -1))
            te = 127 if dr == 0 else 0
            C_ = Cst[dr]
            Cb_ = Cb[dr]
            P.memset("pool", C_, 0.0, [("C", dr)])
            for oi, tt in enumerate(order):
                u = it % 2
                it += 1
                first = oi == 0
                last = oi == len(order) - 1
                cs = slice(tt * 128, (tt + 1) * 128)
                g = dr * 4 + hd
                pb = K.ps[u]
                pn = K.ps[2 + u]
                P.copy("pool", LFr[u], LFt[:, tt, g:g + 1].to_broadcast([128, 128]), ["LFt"], [("LFr", u)])
                P.mm(pb[:, 0:128], LFr[u], K.tri[:, dr, :], True, True, [("LFr", u), "consts"], [("ps", u)])
                P.mm(pb[:, 128:256], QK[:, 2, cs], QK[:, 0, cs], True, False, [("QK", 2), ("QK", 0)], [("ps", u)])
                P.mm(pb[:, 128:256], QK[:, 3, cs], QK[:, 1, cs], False, True, [("QK", 3), ("QK", 1)], [("ps", u)])
                P.stt(Targ[u], pb[:, 0:128], IMB[:, tt, g:g + 1], K.negm[:, dr, :], ALU.add, ALU.add,
                      [("ps", u), "IMB", "consts"], [("Targ", u)])
                P.act(Dm[u], Targ[u], AF.Exp, [("Targ", u)], [("Dm", u)])
                P.tt("dve", WT[u], pb[:, 128:256], Dm[u], ALU.mult, [("ps", u), ("Dm", u)], [("WT", u)])
                if not first or not last:
                    P.act(Ebc[u], pb[:, 0:128], AF.Exp, [("ps", u)], [("Ebc", u)])
                if not first:
                    P.tt("pool", Qtl[u][:, 0, :], QK[:, 0, cs], Ebc[u], ALU.mult, [("QK", 0), ("Ebc", u)], [("Qtl", u)])
                    P.tt("pool", Qtl[u][:, 1, :], QK[:, 1, cs], Ebc[u], ALU.mult, [("QK", 1), ("Ebc", u)], [("Qtl", u)])
                for mc in range(3):
                    lh = Vst[:, tt, mc * 128:(mc + 1) * 128] if mc < 2 else onesb
                    P.mm(pn[:, mc * 128:(mc + 1) * 128], lh, WT[u], True, first, ["Vst", "onesb", ("WT", u)], [("ps", 2 + u)])
                    if not first:
                        for kc in range(2):
                            P.mm(pn[:, mc * 128:(mc + 1) * 128], Cb_[:, kc, mc * 128:(mc + 1) * 128], Qtl[u][:, kc, :], False, kc == 1,
                                 [("Cb", dr), ("Qtl", u)], [("ps", 2 + u)])
                P.act(rr[u], pn[:, 256:384], AF.Abs, [("ps", 2 + u)], [("rr", u)])
                P.ts("dve", rr[u], rr[u], 1.0, None, ALU.max, None, [("rr", u)], [("rr", u)])
                P.recip(rr[u], rr[u], [("rr", u)], [("rr", u)])
                for mc in range(2):
                    if dr == 0:
                        P.tt("dve", HS[:, mc, cs], pn[:, mc * 128:(mc + 1) * 128], rr[u], ALU.mult,
                             [("ps", 2 + u), ("rr", u)], [("HS", mc)])
                    else:
                        P.tt("dve", tmo[mc], pn[:, mc * 128:(mc + 1) * 128], rr[u], ALU.mult,
                             [("ps", 2 + u), ("rr", u)], [("tmo", mc)])
                        P.tt("pool", HS[:, mc, cs], HS[:, mc, cs], tmo[mc], ALU.add, [("HS", mc), ("tmo", mc)], [("HS", mc)])
                if last:
                    continue
                sc_ = smallc[u]
                P.copy("dve", sc_[:, 0:1], pb[:, te:te + 1], [("ps", u)], [("smallc", u)])
                P.act(sc_[:, 1:2], IMB[:, tt, g:g + 1], AF.Exp, ["IMB", ("smallc", u)], [("smallc", u)], bias=sc_[:, 0:1])
                P.tr(psT[:, 0:128], QK[:, 2, cs], identb, [("QK", 2), "identb"], [("ps", 6)])
                P.tr(psT[:, 128:256], QK[:, 3, cs], identb, [("QK", 3), "identb"], [("ps", 6)])
                P.ts("dve", ktl[u], psT[:, 0:256], sc_[:, 1:2], None, ALU.mult, None, [("ps", 6), ("smallc", u)], [("ktl", u)])
                for kc in range(2):
                    pc_ = K.ps[4 + kc]
                    P.mm(pc_[:, 0:256], ktl[u][:, kc * 128:(kc + 1) * 128], Vst[:, tt, :], True, True, [("ktl", u), "Vst"], [("ps", 4 + kc)])
                    P.mm(pc_[:, 256:384], ktl[u][:, kc * 128:(kc + 1) * 128], onesb, True, True, [("ktl", u), "onesb"], [("ps", 4 + kc)])
                    P.stt(C_[:, kc, :], C_[:, kc, :], Ebc[u][:, te:te + 1], pc_[:, 0:384], ALU.mult, ALU.add,
                          [("C", dr), ("Ebc", u), ("ps", 4 + kc)], [("C", dr)])
                P.copy("act", Cb_, C_, [("C", dr)], [("Cb", dr)])
        HN = QK[:, 0:2, :]
        for bi, (x0, n, p0) in enumerate(tok_blocks()):
            for mc in range(2):
                sq = K.sq[mc]
                P.act(sq[:, 0:n], HS[:, mc, x0:x0 + n], AF.Square, [("HS", mc)], [("sq", mc)])
                P.mm(K.ps[7][:, 0:n], K.ones, sq[:, 0:n], mc == 0, mc == 1, [("sq", mc), "consts"], [("ps", 7)])
            rs = K.tmpB[bi % 3]
            rk = ("tmpB", bi % 3)
            P.act(rs[:, 0:n], K.ps[7][:, 0:n], AF.Sqrt, [("ps", 7), "consts"], [rk], bias=K.epsc[:, 0:1], scale=4.0)
            P.recip(rs[:, 0:n], rs[:, 0:n], [rk], [rk])
            for mc in range(2):
                P.tt("dve", HS[:, mc, x0:x0 + n], HS[:, mc, x0:x0 + n], rs[:, 0:n], ALU.mult, [("HS", mc), rk], [("HS", mc)])
                P.stt(HN[:, mc, x0:x0 + n], HS[:, mc, x0:x0 + n], K.mlng[:, 2 * hd + mc:2 * hd + mc + 1], Og[:, mc, x0:x0 + n],
                      ALU.mult, ALU.mult, [("HS", mc), ("Og", mc), "consts"], [("QK", mc)])
        emit_down_proj(P, K, wo, HN, G1, b, l, 2, "wo", "QK")
    ar.release()
    ar.release()
    P.barrier()


S5_G2 = 32
S5_T = 256
S5_NC = L // S5_T
GELU_C = 1.5957691216057308


def s5_scalars(P, K, ar, W, LR, LI, LDT, tag, want_q):
    def T():
        return ar.alloc([1, W], F32)[:, 0, :]

    def k(n):
        return (tag, n)
    lr, er, c, s_ = [T() for _ in range(4)]
    ar.mark()
    dt, ang, t1, t2 = [T() for _ in range(4)]
    P.act(dt, LDT, AF.Exp, [k("in")], [k("dt")])
    P.ts("dve", lr, LR, -1e-4, None, ALU.min, None, [k("in")], [k("lr")])
    P.tt("dve", t1, lr, dt, ALU.mult, [k("lr"), k("dt")], [k("t1")])
    P.act(er, t1, AF.Exp, [k("t1")], [k("er")])
    P.tt("dve", ang, LI, dt, ALU.mult, [k("in"), k("dt")], [k("ang")])
    ki = ar.alloc([1, W], I32)[:, 0, :]
    kr, ph, x2 = T(), T(), T()
    P.ts("dve", t1, ang, 1.0 / (2.0 * np.pi), 0.25, ALU.mult, ALU.add, [k("ang")], [k("t1")])
    P.copy("dve", ki, t1, [k("t1")], [k("ki")])
    P.copy("dve", kr, ki, [k("ki")], [k("kr")])
    P.stt(ph, kr, -6.28125, ang, ALU.mult, ALU.add, [k("kr"), k("ang")], [k("ph")])
    P.stt(ph, kr, -1.9353071795864769e-3, ph, ALU.mult, ALU.add, [k("kr"), k("ph")], [k("ph")])
    P.ts("dve", ph, ph, 0.25, None, ALU.mult, None, [k("ph")], [k("ph")])
    P.tt("dve", x2, ph, ph, ALU.mult, [k("ph")], [k("x2")])
    import math
    sco = [(-1.0) ** i / math.factorial(2 * i + 1) for i in range(1, 7)]
    cco = [(-1.0) ** i / math.factorial(2 * i) for i in range(1, 7)]
    P.ts("dve", s_, x2, sco[5], None, ALU.mult, None, [k("x2")], [k("s")])
    for i in range(4, -1, -1):
        P.stt(s_, s_, sco[i], x2, ALU.add, ALU.mult, [k("s"), k("x2")], [k("s")])
    P.stt(s_, s_, 1.0, ph, ALU.add, ALU.mult, [k("s"), k("ph")], [k("s")])
    P.ts("dve", c, x2, cco[5], None, ALU.mult, None, [k("x2")], [k("c")])
    for i in range(4, -1, -1):
        P.stt(c, c, cco[i], x2, ALU.add, ALU.mult, [k("c"), k("x2")], [k("c")])
    P.ts("dve", c, c, 1.0, None, ALU.add, None, [k("c")], [k("c")])
    for it in range(2):
        P.tt("dve", t1, c, c, ALU.mult, [k("c")], [k("t1")])
        P.tt("dve", t2, s_, s_, ALU.mult, [k("s")], [k("t2")])
        P.stt(s_, c, 2.0, s_, ALU.mult, ALU.mult, [k("c"), k("s")], [k("s")])
        P.tt("dve", c, t1, t2, ALU.subtract, [k("t1"), k("t2")], [k("c")])
    out = {"er": er, "c": c, "s": s_}
    ar.release()
    P.barrier()
    if want_q:
        are, aim, den, nr, qre, qim, t1, t2 = [T() for _ in range(8)]
        P.tt("dve", are, er, c, ALU.mult, [k("er"), k("c")], [k("are")])
        P.tt("dve", aim, er, s_, ALU.mult, [k("er"), k("s")], [k("aim")])
        P.tt("dve", t1, lr, lr, ALU.mult, [k("lr")], [k("t1")])
        P.tt("dve", den, LI, LI, ALU.mult, [k("in")], [k("den")])
        P.tt("dve", den, den, t1, ALU.add, [k("den"), k("t1")], [k("den")])
        P.recip(den, den, [k("den")], [k("den")])
        P.ts("dve", nr, are, -1.0, None, ALU.add, None, [k("are")], [k("nr")])
        P.tt("dve", t1, nr, lr, ALU.mult, [k("nr"), k("lr")], [k("t1")])
        P.tt("dve", t2, aim, LI, ALU.mult, [k("aim"), k("in")], [k("t2")])
        P.tt("dve", qre, t1, t2, ALU.add, [k("t1"), k("t2")], [k("qre")])
        P.tt("dve", qre, qre, den, ALU.mult, [k("qre"), k("den")], [k("qre")])
        P.tt("dve", t1, aim, lr, ALU.mult, [k("aim"), k("lr")], [k("t1")])
        P.tt("dve", t2, nr, LI, ALU.mult, [k("nr"), k("in")], [k("t2")])
        P.tt("dve", qim, t1, t2, ALU.subtract, [k("t1"), k("t2")], [k("qim")])
        P.tt("dve", qim, qim, den, ALU.mult, [k("qim"), k("den")], [k("qim")])
        out["qre"] = qre
        out["qim"] = qim
    return out


def emit_s5_gen(P, K, js):
    ar = K.arena
    d = K.d
    ar.mark()
    p1 = ar.alloc([3, 64], F32)
    P.dma("sp", p1, d["s5p1"][js], writes=[("g1", "in")])
    sc = s5_scalars(P, K, ar, 64, p1[:, 0, :], p1[:, 1, :], p1[:, 2, :], "g1", False)
    P.copy("dve", K.S5R[:, js, 0, :], sc["er"], [("g1", "er")], [("S5R", js)])
    for dh in range(2):
        ar.mark()
        COS = ar.alloc([32, S5_T], F32)
        SIN = ar.alloc([32, S5_T], F32)
        t1 = ar.alloc([32, S5_T // 2], F32)
        t2 = ar.alloc([32, S5_T // 2], F32)
        P.copy("dve", COS[:, :, 0:1], sc["c"][:, dh * 32:(dh + 1) * 32].unsqueeze(2), [("g1", "c")], ["COS"])
        P.copy("dve", SIN[:, :, 0:1], sc["s"][:, dh * 32:(dh + 1) * 32].unsqueeze(2), [("g1", "s")], ["SIN"])
        for kk in range(8):
            ln = 2 ** kk
            pr = COS[:, :, ln - 1:ln].to_broadcast([128, 32, ln])
            pi_ = SIN[:, :, ln - 1:ln].to_broadcast([128, 32, ln])
            a1 = t1[:, :, 0:ln]
            a2 = t2[:, :, 0:ln]
            P.tt("dve", a1, COS[:, :, 0:ln], pr, ALU.mult, ["COS"], ["t1"])
            P.tt("dve", a2, SIN[:, :, 0:ln], pi_, ALU.mult, ["SIN"], ["t2"])
            P.tt("dve", COS[:, :, ln:2 * ln], a1, a2, ALU.subtract, ["t1", "t2", "COS"], ["COSn"])
            P.tt("dve", a1, COS[:, :, 0:ln], pi_, ALU.mult, ["COS", "SIN", "COSn"], ["t1"])
            P.tt("dve", a2, SIN[:, :, 0:ln], pr, ALU.mult, ["SIN", "COS", "COSn"], ["t2"])
            P.tt("dve", SIN[:, :, ln:2 * ln], a1, a2, ALU.add, ["t1", "t2", "SIN"], ["SIN"])
            P.copy("dve", COS[:, :, 0:1], COS[:, :, 0:1], ["COSn", "COS"], ["COS"])
        P.copy("dve", K.S5R[:, js, 1, dh * 32:(dh + 1) * 32].unsqueeze(2), COS[:, :, S5_T - 1:S5_T], ["COS"], [("S5R", js)])
        P.copy("dve", K.S5R[:, js, 2, dh * 32:(dh + 1) * 32].unsqueeze(2), SIN[:, :, S5_T - 1:S5_T], ["SIN"], [("S5R", js)])
        P.dma("sp", d["s5t"][js, :, 0, dh * 32:(dh + 1) * 32, :], COS, reads=["COS"], writes=[("s5t", js, 0, dh)])
        P.dma("sp", d["s5t"][js, :, 1, dh * 32:(dh + 1) * 32, :], SIN, reads=["SIN"], writes=[("s5t", js, 1, dh)])
        ar.release()
        P.barrier()
    ar.release()
    P.barrier()
    ar.mark()
    CC = ar.alloc([2, 1024], F32)
    P.dma("sp", CC, d["s5c"][js], writes=["CC"])
    for var in range(2):
        WC = ar.alloc([2, S5_G2, 128], BF16)
        P.memset("pool", WC, 0.0, [("WC", var)])
        WCv = WC.rearrange("p d (c k) m -> p d c k m", k=4)
        CCv = CC[:, var, :].rearrange("p (d c k h) -> p d c k h", d=2, c=8, k=4)
        for gl in range(2):
            for k4 in range(4):
                o_ = WCv[64 * gl:64 * gl + 64, :, :, k4, 32 * k4 + 16 * gl:32 * k4 + 16 * gl + 16]
                i_ = CCv[64 * gl:64 * gl + 64, :, :, k4, :]
                if var == 0:
                    P.copy("dve", o_, i_, ["CC"], [("WC", var)])
                else:
                    P.ts("dve", o_, i_, -1.0, None, ALU.mult, None, ["CC"], [("WC", var)])
        P.dma("sp", d["s5wc"][js, :, var], WC, reads=[("WC", var)], writes=[("s5wc", js, var)])
    ar.release()
    P.barrier()
    ar.mark()
    p2 = ar.alloc([5, 1024], F32)
    P.dma("sp", p2, d["s5p2"][js], writes=[("g2", "in"), "BB"])
    sc = s5_scalars(P, K, ar, 1024, p2[:, 0, :], p2[:, 1, :], p2[:, 2, :], "g2", True)
    t1 = p2[:, 0, :]
    t2 = p2[:, 1, :]
    Bb = ar.alloc([1, 1024], F32)[:, 0, :]
    WB = ar.alloc([2, S5_G2, 128], BF16)
    WBv = WB.rearrange("p d (c k) (g m) -> p d c k g m", k=4, g=2)
    for var in range(2):
        x1, x2 = (p2[:, 3, :], p2[:, 4, :]) if var == 0 else (p2[:, 4, :], p2[:, 3, :])
        P.tt("dve", t1, sc["qre"], x1, ALU.mult, [("g2", "qre"), "BB", ("g2", "in"), ("g2", "lr")], ["bt1"])
        P.tt("dve", t2, sc["qim"], x2, ALU.mult, [("g2", "qim"), "BB", ("g2", "in"), ("g2", "ang"), ("g2", "den")], ["bt2"])
        P.tt("dve", Bb, t1, t2, ALU.subtract if var == 0 else ALU.add, ["bt1", "bt2"], ["Bb"])
        P.memset("pool", WB, 0.0, ["WB"])
        Bv = Bb.rearrange("p (d c m) -> p d c m", d=2, c=8)
        for k4 in range(4):
            for gl in range(2):
                P.ts("dve", WBv[:, :, :, k4, gl, :], Bv, K.mk8[:, 2 * k4 + gl:2 * k4 + gl + 1], None, ALU.mult, None,
                     ["Bb", "consts"], ["WB"])
        P.dma("sp", d["s5wb"][js, :, var], WB, reads=["WB"], writes=[("s5wb", js, var)])
    ar.release()
    P.barrier()


def emit_s5(P, K, l, b):
    js = l // 3
    ar = K.arena
    d = K.d
    G1 = K.MOD[:, l, 16:24, :]
    ar.mark()
    H = ar.alloc([NCH, LP], BF16)
    emit_norm_mod(P, K, K.A1, K.MOD[:, l, 0:8, :], l, b, H, keep_rstd=True)
    SH1 = K.MOD[:, l, 0:8, :]
    RSTD = K.RSTD
    Wc1 = ar.alloc([4, 8, 128], BF16)
    Wc = [Wc1, Wc1]
    ar.mark()
    T_ = S5_T
    tabs = [ar.alloc([2, T_], F32) for _ in range(4)]
    NB2 = 4
    tq = [[ar.alloc([1, T_], F32)[:, 0, :] for _ in range(4)] for _ in range(NB2)]
    bt = [[ar.alloc([1, T_], F32)[:, 0, :] for _ in range(2)] for _ in range(NB2)]
    zz = [[ar.alloc([1, T_], F32)[:, 0, :] for _ in range(2)] for _ in range(NB2)]
    PP = [ar.alloc([4, T_], BF16) for _ in range(NB2)]
    car = [[ar.alloc([1, 4], F32)[:, 0, :] for _ in range(2)] for _ in range(4)]
    segs = [(0, 0, 256, 4)] + [(0, 256, 256, b)] + [(bk, 0, 512, b) for bk in (1, 2, 3)] + [(4, 0, 256, b)]
    Rt = K.S5R[:, js, 0, :]
    CAc = K.S5R[:, js, 1, :]
    CAs = K.S5R[:, js, 2, :]
    it = 0
    ntab = 0
    NCK = S5_NC
    for c in range(NCH):
        wb = 0
        bank_started = set()
        for v in range(2):
            P.dma("sp", Wc[wb][:, v, :, :].rearrange("p (d k) m -> p d k m", d=2),
                  d["s5wb"][js, :, v].rearrange("p d (c k) m -> p d c k m", k=4)[:, :, c, :, :], writes=[("Wc", wb, v)])
            P.dma("sp", Wc[wb][:, 2 + v, :, :].rearrange("p (d k) m -> p d k m", d=2),
                  d["s5wc"][js, :, v].rearrange("p d (c k) m -> p d c k m", k=4)[:, :, c, :, :], writes=[("Wc", wb, 2 + v)])
        for k4g in range(4):
            streams = []
            for k4 in (k4g,):
                for dr in range(2):
                    cg = dr * 32 + 4 * c + k4
                    si_ = 2 * (k4g % 2) + dr
                    tb = tabs[si_]
                    tk_ = ("tabs", si_)
                    P.dma("sp", tb, d["s5t"][js, :, :, cg, :], writes=[tk_])
                    order = list(range(NCK)) if dr == 0 else [0] + list(range(NCK - 1, 0, -1))
                    streams.append((dr, cg, tb, tk_, order, k4, si_))
            pvs = {}

            def stage1(oi):
                nonlocal it
                for (dr, cg, tb, tk_, order, k4, si_) in streams:
                    tt_ = order[oi]
                    it += 1
                    pc = 1 if tt_ == 0 else LCTX + 2 + T_ * (tt_ - 1)
                    bki_ = 5 + (it % 3)
                    pv = K.ps[bki_]
                    pvk = ("ps", bki_)
                    wi = dr * 4 + k4
                    P.mm(pv[:, 0:T_], Wc[wb][:, 0, wi, :], H[:, c, pc:pc + T_], True, True, [("Wc", wb, 0), ("H", c)], [pvk])
                    P.mm(pv[:, T_:2 * T_], Wc[wb][:, 1, wi, :], H[:, c, pc:pc + T_], True, True, [("Wc", wb, 1), ("H", c)], [pvk])
                    pvs[(oi, si_)] = (pv, pvk)
            stage1(0)
            for oi in range(NCK):
                for (dr, cg, tb, tk_, order, k4, si_) in streams:
                    u = si_
                    pv, pvk = pvs[(oi, si_)]
                    if dr == 0:
                        vr = pv[:, 0:T_]
                        vi = pv[:, T_:2 * T_]
                    else:
                        vr = pv[:, T_ - 1::-1]
                        vi = pv[:, 2 * T_ - 1:T_ - 1:-1]
                    cosT = tb[:, 0, :]
                    sinT = tb[:, 1, :]
                    q = tq[u]
                    qk = [("tq", u, i) for i in range(4)]
                    P.tt("dve", q[0], vr, cosT, ALU.mult, [pvk, tk_], [qk[0]])
                    P.tt("dve", q[1], vi, sinT, ALU.mult, [pvk, tk_], [qk[1]])
                    P.tt("dve", q[2], vi, cosT, ALU.mult, [pvk, tk_], [qk[2]])
                    P.tt("dve", q[3], vr, sinT, ALU.mult, [pvk, tk_], [qk[3]])
                for (dr, cg, tb, tk_, order, k4, si_) in streams:
                    u = si_
                    q = tq[u]
                    qk = [("tq", u, i) for i in range(4)]
                    P.tt("pool", bt[u][0], q[0], q[1], ALU.add, [qk[0], qk[1]], [("bt", u, 0)])
                    P.tt("pool", bt[u][1], q[2], q[3], ALU.subtract, [qk[2], qk[3]], [("bt", u, 1)])
                for (dr, cg, tb, tk_, order, k4, si_) in streams:
                    u = si_
                    Rb = Rt[:, cg:cg + 1].to_broadcast([128, T_])
                    cr = car[si_]
                    for ri in range(2):
                        init = 0.0 if oi == 0 else cr[ri][:, 0:1]
                        P.scan(zz[u][ri], Rb, bt[u][ri], init, [("bt", u, ri), ("S5R", js), ("car", si_, ri)], [("zz", u, ri)])
                if oi < NCK - 1:
                    for (dr, cg, tb, tk_, order, k4, si_) in streams:
                        u = si_
                        cr = car[si_]
                        zr = zz[u][0]
                        zi = zz[u][1]
                        P.tt("pool", cr[0][:, 1:2], zi[:, T_ - 1:T_], CAs[:, cg:cg + 1], ALU.mult, [("zz", u, 1), ("S5R", js)], [("cart", si_, 0)])
                        P.tt("pool", cr[1][:, 1:2], zi[:, T_ - 1:T_], CAc[:, cg:cg + 1], ALU.mult, [("zz", u, 1), ("S5R", js)], [("cart", si_, 1)])
                for (dr, cg, tb, tk_, order, k4, si_) in streams:
                    u = si_
                    zr = zz[u][0]
                    zi = zz[u][1]
                    cosT = tb[:, 0, :]
                    sinT = tb[:, 1, :]
                    Pu = PP[u]

                    def po(i, Pu=Pu, dr=dr):
                        return Pu[:, i, :] if dr == 0 else Pu[:, i, T_ - 1::-1]
                    pk = [("PP", u, i) for i in range(4)]
                    P.tt("dve", po(0), zr, cosT, ALU.mult, [("zz", u, 0), tk_], [pk[0]])
                    P.stt(po(1), zi, -1.0, sinT, ALU.mult, ALU.mult, [("zz", u, 1), tk_], [pk[1]])
                    P.tt("pool", po(2), zr, sinT, ALU.mult, [("zz", u, 0), tk_], [pk[2]])
                    P.tt("pool", po(3), zi, cosT, ALU.mult, [("zz", u, 1), tk_], [pk[3]])
                if oi < NCK - 1:
                    for (dr, cg, tb, tk_, order, k4, si_) in streams:
                        u = si_
                        cr = car[si_]
                        zr = zz[u][0]
                        P.stt(cr[0][:, 0:1], zr[:, T_ - 1:T_], CAc[:, cg:cg + 1], cr[0][:, 1:2], ALU.mult, ALU.subtract,
                              [("zz", u, 0), ("cart", si_, 0), ("S5R", js)], [("car", si_, 0)])
                        P.stt(cr[1][:, 0:1], zr[:, T_ - 1:T_], CAs[:, cg:cg + 1], cr[1][:, 1:2], ALU.mult, ALU.add,
                              [("zz", u, 0), ("cart", si_, 1), ("S5R", js)], [("car", si_, 1)])
                    stage1(oi + 1)
                for (dr, cg, tb, tk_, order, k4, si_) in streams:
                    u = si_
                    tt_ = order[oi]
                    Pu = PP[u]
                    pk = [("PP", u, i) for i in range(4)]
                    wi = dr * 4 + k4
                    bki = (tt_ * T_) // 512
                    yb = K.ps[bki]
                    y0 = (tt_ * T_) % 512
                    ys = yb[:, y0:y0 + T_]
                    for i in range(4):
                        st = bki not in bank_started
                        bank_started.add(bki)
                        P.mm(ys, Wc[wb][:, 2 + (i // 2), wi, :], Pu[:, i, :], st, False,
                             [("Wc", wb, 2 + i // 2), pk[i]], [("ps", bki)])
        for si, (bk, o0, n, bcol) in enumerate(segs):
            x0 = bk * 512 + o0
            p0 = 1 + x0 if x0 < LCTX else 2 + x0
            f = [K.tmpB[i][:, 0:n] for i in range(3)]
            fk = [("tmpB", i) for i in range(3)]
            P.tt("dve", f[0], K.X[:, c, x0:x0 + n], RSTD[:, x0:x0 + n], ALU.mult, [("X", c)] + RSTD_KEYS, [fk[0]])
            P.act(f[0], f[0], AF.Identity, [fk[0], ("MOD", l), ("A", l)], [fk[0]],
                  scale=K.A1[:, l, c, bcol:bcol + 1], bias=SH1[:, c, bcol:bcol + 1])
            P.stt(f[1], f[0], K.s5d[:, js, c:c + 1], K.ps[bk][:, o0:o0 + n], ALU.mult, ALU.add,
                  [fk[0], ("ps", bk), "consts"], [fk[1]])
            P.tt("pool", f[2], f[1], f[1], ALU.mult, [fk[1]], [fk[2]])
            P.ts("pool", f[2], f[2], 0.044715, 1.0, ALU.mult, ALU.add, [fk[2]], [fk[2]])
            P.tt("pool", f[2], f[2], f[1], ALU.mult, [fk[2], fk[1]], [fk[2]])
            P.act(f[2], f[2], AF.Sigmoid, [fk[2]], [fk[2]], scale=GELU_C)
            P.tt("dve", H[:, c, p0:p0 + n], f[1], f[2], ALU.mult, [fk[1], fk[2]], [("H", c)])
    ar.release()
    P.barrier()
    gw = [ar.alloc([2, NCH, 128], BF16) for _ in range(2)]
    sg = [ar.alloc([1, 512], F32)[:, 0, :] for _ in range(2)]
    gv = d["s5_glu_w"][js].rearrange("(kc p) n -> p kc n", p=128)
    n_ = 0
    for oc in range(NCH):
        wb = oc % 2
        P.dma("pool", gw[wb][:, 0, :, :], gv[:, :, oc * 128:(oc + 1) * 128], writes=[("gw", wb, 0)])
        P.dma("pool", gw[wb][:, 1, :, :], gv[:, :, D + oc * 128:D + (oc + 1) * 128], writes=[("gw", wb, 1)])
        for (x0, n, p0) in tok_blocks():
            u = n_ % 2
            n_ += 1
            for half in range(2):
                pst = K.ps[2 * half + u]
                for kc in range(NCH):
                    P.mm(pst[:, 0:n], gw[wb][:, half, kc, :], H[:, kc, p0:p0 + n], kc == 0, kc == NCH - 1,
                         [("gw", wb, half), ("H", kc)], [("ps", 2 * half + u)])
            P.act(sg[u][:, 0:n], K.ps[2 + u][:, 0:n], AF.Sigmoid, [("ps", 2 + u), "consts"], [("sg", u)],
                  bias=K.s5gb[:, js, 8 + oc:9 + oc])
            P.stt(sg[u][:, 0:n], K.ps[u][:, 0:n], K.s5gb[:, js, oc:oc + 1], sg[u][:, 0:n], ALU.add, ALU.mult,
                  [("ps", u), ("sg", u), "consts"], [("sg", u)])
            bcol = 4 if x0 == 0 else b
            P.stt(K.X[:, oc, x0:x0 + n], sg[u][:, 0:n], G1[:, oc, bcol:bcol + 1], K.X[:, oc, x0:x0 + n], ALU.mult, ALU.add,
                  [("sg", u), ("MOD", l), ("X", oc)], [("X", oc)])
    ar.release()
    P.barrier()


def emit_final(P, K, b):
    K.arena.mark()
    emit_rstd(P, K, "f")
    i = 0
    for bi, (c0, n, p0) in enumerate(tok_blocks()):
        if bi == 0:
            continue
        for c in range(NCH):
            tmp = K.tmpB[i % 3]
            tk = ("tmpB", i % 3)
            i += 1
            P.tt("dve", tmp[:, 0:n], K.X[:, c, c0:c0 + n], K.RSTD[:, c0:c0 + n], ALU.mult, [("X", c), ("rstd", bi)], [tk])
            P.act(tmp[:, 0:n], tmp[:, 0:n], AF.Identity, [tk, "consts"], [tk], scale=K.fg[:, c:c + 1])
            P.dma("sp", K.d["out"][b, c, :, c0 - LCTX:c0 - LCTX + n], tmp[:, 0:n], reads=[tk], writes=[("out", b, c, bi)])
    K.arena.release()
    P.barrier()


def build_program(NB=4, layers=(0, 1, 2, 3), mixers=True, final=True):
    nc = bass.Bass("TRN2", target_bir_lowering=False)
    d = {}

    def din(name, shape, dt=F32):
        d[name] = nc.dram_tensor(name, list(shape), dt, kind="ExternalInput").ap()

    din("xin", [NB, NCH, 128, L])
    din("cT", [128, NCH, 5])
    din("ada_w", [DEPTH, D, 6 * D])
    din("consts", [128, K_CONST_COLS])
    din("ffn_w_in", [DEPTH, D, 2 * DFF])
    din("ffn_w_out", [DEPTH, DFF, D])
    s5_js = sorted(set(l // 3 for l in layers if l % 3 == 0)) if mixers else []
    if s5_js:
        din("s5p1", [2, 128, 3, 64])
        din("s5p2", [2, 128, 5, 1024])
        din("s5c", [2, 128, 2, 1024])
        din("s5_glu_w", [2, D, 2 * D])
        d["s5t"] = nc.dram_tensor("s5t", [2, 128, 2, 64, S5_T], F32, kind="Internal").ap()
        d["s5wb"] = nc.dram_tensor("s5wb", [2, 128, 2, 2, S5_G2, 128], BF16, kind="Internal").ap()
        d["s5wc"] = nc.dram_tensor("s5wc", [2, 128, 2, 2, S5_G2, 128], BF16, kind="Internal").ap()
    if mixers and any(l % 3 == 1 for l in layers):
        din("ml_w_in", [1, D, 4 * D])
        din("ml_w_out", [1, D, D])
        din("ml_wif", [128, NCH, 16])
        for nm, shp in (("ml_sq", [ML_H, 128, 4, L]), ("ml_sv", [ML_H, 128, 18, 256]), ("ml_so", [ML_H, 128, 2, L])):
            d[nm] = nc.dram_tensor(nm, shp, BF16, kind="Internal").ap()
    if mixers and any(l % 3 == 2 for l in layers):
        din("na_w_qkv", [1, D, 3 * D])
        din("na_w_out", [1, D, D])
        din("na_bias", [1, NA_H, 128, NA_NT * 128])
    d["out"] = nc.dram_tensor("out", [NB, NCH, 128, LLAT], F32, kind="ExternalOutput").ap()

    with contextlib.ExitStack() as es:
        P = Prog(nc, es)
        K = Ctx()
        K.d = d
        K.NB = NB
        NBYTES = 204 * 1024
        big = es.enter_context(nc.sbuf_tensor("arena", [128, NBYTES // 4], F32))
        K.arena = Arena(big, NBYTES)
        ar = K.arena
        K.ps = [es.enter_context(nc.psum_tensor(f"ps{i}", [128, 512], F32)) for i in range(8)]
        K.X = ar.alloc([NCH, L], F32)
        cst = ar.alloc([1, K_CONST_COLS], F32)[:, 0, :]
        P.dma("sp", cst, d["consts"][:, :], writes=["consts"])
        o = 0

        def take(n, shape=None):
            nonlocal o
            v = cst[:, o:o + n]
            o += n
            return v
        K.ones = take(128)
        K.epsc = take(1)
        K.adab = take(DEPTH * 48).rearrange("p (l j) -> p l j", l=DEPTH)
        K.n1g = take(DEPTH * NCH).rearrange("p (l c) -> p l c", l=DEPTH)
        K.n2g = take(DEPTH * NCH).rearrange("p (l c) -> p l c", l=DEPTH)
        K.fg = take(NCH)
        K.cw = take(DEPTH * 44 * 3).rearrange("p (l c k) -> p l c k", l=DEPTH, c=44)
        K.onec = take(1)
        K.ident = take(128)
        K.tri = take(256).rearrange("p (d t) -> p d t", d=2)
        K.negm = take(256).rearrange("p (d t) -> p d t", d=2)
        K.mlbif = take(16)
        K.mlcw = take(48).rearrange("p (c k) -> p c k", k=3)
        K.mlng = take(NCH)
        K.ropeq = take(128).rearrange("p (t c) -> p t c", t=2)
        K.ropek = take(128).rearrange("p (t c) -> p t c", t=2)
        K.halfpi = take(1)
        K.mk8 = take(8)
        K.s5d = take(2 * NCH).rearrange("p (j c) -> p j c", j=2)
        K.s5gb = take(2 * 16).rearrange("p (j c) -> p j c", j=2)
        assert o == K_CONST_COLS, (o, K_CONST_COLS)
        K.MOD = ar.alloc([DEPTH, 48, 5], F32)
        K.A1 = ar.alloc([DEPTH, NCH, 5], F32)
        K.A2 = ar.alloc([DEPTH, NCH, 5], F32)
        K.tmpB = [ar.alloc([1, 512], F32)[:, 0, :] for _ in range(3)]
        K.sq = [ar.alloc([1, 512], F32)[:, 0, :] for _ in range(2)]

        K.S5R = ar.alloc([2, 3, 64], F32)
        emit_mods(P, K)
        for js in s5_js:
            emit_s5_gen(P, K, js)
        for b in range(NB):
            for c in range(NCH):
                P.dma("sp", K.X[:, c, :], d["xin"][b, c], writes=[("X", c)])
            for l in layers:
                if mixers:
                    if l % 3 == 2:
                        emit_na(P, K, l, b)
                    if l % 3 == 1:
                        emit_mlstm(P, K, l, b)
                    if l % 3 == 0:
                        emit_s5(P, K, l, b)
                emit_ffn(P, K, l, b)
            if final:
                emit_final(P, K, b)
            else:
                for c in range(NCH):
                    P.dma("sp", d["out"][b, c], K.X[:, c, LCTX:L], reads=[("X", c)], writes=[("out", b, c)])
                P.barrier()
        P.emit()
    nc._used_inputs = set(d.keys()) - {"out", "ml_sq", "ml_sv", "ml_so", "s5t", "s5wb", "s5wc"}
    return nc


K_CONST_COLS = 128 + 1 + DEPTH * 48 + DEPTH * NCH * 2 + NCH + DEPTH * 44 * 3 + 1 + 128 + 256 + 256 + 16 + 48 + NCH + 128 + 128 + 1 + 8 + 16 + 32


def make_consts(inp):
    cols = []
    cols.append(np.full((128, 128), 1.0 / D, np.float32))
    cols.append(np.full((128, 1), EPS, np.float32))
    ab = np.asarray(inp["ada_b"], np.float32).reshape(DEPTH, 48, 128).transpose(2, 0, 1).reshape(128, -1)
    cols.append(ab)
    for nm in ("norm1_g", "norm2_g"):
        g = np.asarray(inp[nm], np.float32).reshape(DEPTH, NCH, 128).transpose(2, 0, 1).reshape(128, -1)
        cols.append(g)
    cols.append(np.asarray(inp["final_g"], np.float32).reshape(NCH, 128).T)
    cw = np.asarray(inp["ffn_conv"], np.float32).reshape(DEPTH, 3, 44, 128).transpose(3, 0, 2, 1).reshape(128, -1)
    cols.append(cw)
    cols.append(np.ones((128, 1), np.float32))
    cols.append(np.eye(128, dtype=np.float32))
    si = np.arange(128)[:, None]
    ti = np.arange(128)[None, :]
    trif = (si <= ti).astype(np.float32)
    trib = (si >= ti).astype(np.float32)
    cols.append(np.concatenate([trif, trib], axis=1))
    cols.append(np.concatenate([(1 - trif) * NEGM, (1 - trib) * NEGM], axis=1).astype(np.float32))
    bif = np.asarray(inp["ml_b_if"], np.float32)[0].reshape(1, 16)
    cols.append(np.repeat(bif, 128, axis=0))
    mc_ = np.asarray(inp["ml_conv"], np.float32)[0]
    cwm = np.empty((128, 16, 3), np.float32)
    for qk in range(2):
        for hd in range(ML_H):
            base = qk * D + hd * ML_DH
            for ab in range(2):
                idx = np.concatenate([base + 64 * ab + np.arange(64), base + 128 + 64 * ab + np.arange(64)])
                cwm[:, (qk * ML_H + hd) * 2 + ab, :] = mc_[:, idx].T
    cols.append(cwm.reshape(128, 48))
    cols.append(np.asarray(inp["ml_norm_g"], np.float32)[0].reshape(NCH, 128).T)
    inv = (10000.0 ** (-np.arange(64, dtype=np.float32) / 64.0)).astype(np.float32)
    pos = np.arange(64, dtype=np.float32)
    ang = (pos[None, :] * inv[:, None]).astype(np.float32)
    tab = np.stack([np.cos(ang), np.sin(ang)], axis=1).astype(np.float32)
    tab = np.concatenate([tab, tab], axis=0)
    cols.append((tab / np.float32(16.0)).reshape(128, 128).astype(np.float32))
    cols.append(tab.reshape(128, 128))
    cols.append(np.full((128, 1), np.pi / 2, np.float32))
    cols.append((np.arange(128)[:, None] // 16 == np.arange(8)[None, :]).astype(np.float32))
    cols.append(np.asarray(inp["s5_d"], np.float32).reshape(2, NCH, 128).transpose(2, 0, 1).reshape(128, 16))
    cols.append(np.asarray(inp["s5_glu_b"], np.float32).reshape(2, 16, 128).transpose(2, 0, 1).reshape(128, 32))
    out = np.ascontiguousarray(np.concatenate(cols, axis=1), dtype=np.float32)
    assert out.shape[1] == K_CONST_COLS
    return out


def make_s5_params(inp):
    f = lambda k: np.asarray(inp[k], np.float32)
    lre, lim, ldt = f("s5_lam_re"), f("s5_lam_im"), f("s5_log_dt")
    bre, bim, cre, cim = f("s5_b_re"), f("s5_b_im"), f("s5_c_re"), f("s5_c_im")

    def lay1(a):
        a = a.reshape(2, 2, 32, 2, 64).transpose(0, 3, 4, 1, 2)
        return a.reshape(2, 128, 64)

    def lay2(a):
        a = a.reshape(2, 2, 8, 8, 64).transpose(0, 3, 1, 2, 4)
        a = np.broadcast_to(a[:, :, None], (2, 8, 16, 2, 8, 64))
        return a.reshape(2, 128, 1024)
    ldt4 = np.broadcast_to(ldt[..., None], (2, 2, 64, 64))
    p1 = np.stack([lay1(lre), lay1(lim), lay1(ldt4)], axis=2)
    b2 = []
    for bb in (bre, bim):
        a = bb.reshape(2, 2, 8, 8, 64, 16).transpose(0, 3, 5, 1, 2, 4)
        b2.append(a.reshape(2, 128, 1024))
    p2 = np.stack([lay2(lre), lay2(lim), lay2(ldt4), b2[0], b2[1]], axis=2)
    cc = []
    for c_ in (cre, cim):
        a = c_.reshape(2, 2, 32, 2, 16, 64).transpose(0, 3, 5, 1, 2, 4)
        cc.append(a.reshape(2, 128, 1024))
    s5c = np.stack(cc, axis=2)
    return (np.ascontiguousarray(p1, dtype=np.float32), np.ascontiguousarray(p2, dtype=np.float32),
            np.ascontiguousarray(s5c, dtype=np.float32))


def make_in_maps(inp, NB, ncores):
    x = np.asarray(inp["x"], np.float32)
    ctx = np.asarray(inp["ctx"], np.float32)
    c = np.asarray(inp["c"], np.float32)
    c_ctx = np.asarray(inp["c_ctx"], np.float32)
    consts = make_consts(inp)
    s5p1, s5p2, s5c = make_s5_params(inp)
    shared = {
        "s5p1": s5p1, "s5p2": s5p2, "s5c": s5c,
        "s5_glu_w": np.ascontiguousarray(inp["s5_glu_w"], dtype=np.float32),
        "ada_w": np.ascontiguousarray(inp["ada_w"], dtype=np.float32),
        "consts": consts,
        "ffn_w_in": np.ascontiguousarray(inp["ffn_w_in"], dtype=np.float32),
        "ffn_w_out": np.ascontiguousarray(inp["ffn_w_out"], dtype=np.float32),
        "ml_w_in": np.ascontiguousarray(inp["ml_w_in"], dtype=np.float32),
        "ml_w_out": np.ascontiguousarray(inp["ml_w_out"], dtype=np.float32),
        "ml_wif": np.ascontiguousarray(np.asarray(inp["ml_w_if"], np.float32)[0].transpose(1, 0, 2).reshape(NCH, 128, 16).transpose(1, 0, 2)),
        "na_w_qkv": np.ascontiguousarray(inp["na_w_qkv"], dtype=np.float32),
        "na_w_out": np.ascontiguousarray(inp["na_w_out"], dtype=np.float32),
        "na_bias": make_na_bias(np.asarray(inp["na_rpb"])[0])[None],
    }
    maps = []
    for core in range(ncores):
        b0 = core * NB
        xin = np.empty((NB, D, L), np.float32)
        for i in range(NB):
            xin[i, :, :LCTX] = ctx[b0 + i].T
            xin[i, :, LCTX:] = x[b0 + i].T
        cc = np.concatenate([c[b0:b0 + NB], np.zeros((4 - NB, D), np.float32), c_ctx[None]], axis=0)
        cT = np.ascontiguousarray(cc.reshape(5, NCH, 128).transpose(2, 1, 0))
        m = dict(shared)
        m["xin"] = xin.reshape(NB, NCH, 128, L)
        m["cT"] = cT
        maps.append(m)
    return maps


def gather_out(res, NB, ncores):
    outs = []
    for core in range(ncores):
        o = np.asarray(res.results[core]["out"]).reshape(NB, D, LLAT)
        outs.append(np.ascontiguousarray(o.transpose(0, 2, 1)))
    return np.concatenate(outs, axis=0)


def nc_inputs(nc):
    return getattr(nc, "_used_inputs")


def kernel(**inputs):
    NB = 4
    nc = build_program(NB=NB)
    maps = make_in_maps(inputs, NB, NCORES)
    maps = [{k: v for k, v in m.items() if k in nc_inputs(nc)} for m in maps]
    res = run_bass_kernel_spmd(nc, maps, core_ids=list(range(NCORES)))
    return gather_out(res, NB, NCORES).astype(np.float32)
```

```python
import contextlib
import numpy as np
import concourse.bass as bass
import concourse.mybir as mybir
from concourse.bass_utils import run_bass_kernel_spmd

F32 = mybir.dt.float32
BF16 = mybir.dt.bfloat16
I32 = mybir.dt.int32
AF = mybir.ActivationFunctionType
ALU = mybir.AluOpType
ENGS = ("pe", "act", "dve", "pool", "sp")
NDMASEM = 12

D = 1024
NCH = 8
LCTX = 256
LLAT = 2048
L = LCTX + LLAT
LP = L + 3
DEPTH = 4
DFF = 2816
NJ = DFF // 128
EPS = 1e-6
NCORES = 8


class Prog:
    def __init__(self, nc, es):
        self.nc = nc
        self.es = es
        self.ops = {e: [] for e in ENGS}
        self.cnt = {e: 0 for e in ENGS}
        self.sems = {}
        for e in ENGS:
            self.sems[("e", e)] = es.enter_context(nc.semaphore("sem_" + e))
        self.dman = {e: 0 for e in ENGS}
        for e in ("sp", "pool", "act"):
            for i in range(NDMASEM):
                self.sems[("d", e, i)] = es.enter_context(nc.semaphore(f"dsem_{e}_{i}"))
        self.seen = {e: {} for e in ENGS}
        self.bw = {}
        self.br = {}

    def _deps(self, eng, reads, writes, extra=()):
        d = {}

        def add(ev):
            if ev is None:
                return
            sk, v = ev
            if d.get(sk, 0) < v:
                d[sk] = v
        for r in reads:
            add(self.bw.get(r))
        for w in writes:
            add(self.bw.get(w))
            for sk, v in self.br.get(w, {}).items():
                add((sk, v))
        for ev in extra:
            add(ev)
        out = []
        seen = self.seen[eng]
        for sk, v in d.items():
            if eng == "pe" and sk == ("e", "pe"):
                continue
            if seen.get(sk, 0) >= v:
                continue
            seen[sk] = v
            out.append((sk, v))
        return out

    def _commit(self, ev, reads, writes):
        sk, v = ev
        for r in reads:
            self.br.setdefault(r, {})[sk] = v
        for w in writes:
            self.bw[w] = ev
            self.br[w] = {}

    def op(self, eng, fn, reads=(), writes=()):
        reads = list(reads)
        writes = list(writes)
        waits = self._deps(eng, reads, writes)
        self.cnt[eng] += 1
        ev = (("e", eng), self.cnt[eng])
        self.ops[eng].append((waits, fn, (("e", eng), 1)))
        self._commit(ev, reads, writes)
        return ev

    def dma(self, eng, out, in_, reads=(), writes=(), **kw):
        reads = list(reads)
        writes = list(writes)
        n = self.dman[eng]
        self.dman[eng] += 1
        slot = n % NDMASEM
        use = n // NDMASEM + 1
        sk = ("d", eng, slot)
        extra = [(sk, 16 * (use - 1))] if use > 1 else []
        waits = self._deps(eng, reads, writes, extra)
        ev = (sk, 16 * use)
        self.ops[eng].append((waits, (lambda e: e.dma_start(out=out, in_=in_, **kw)), (sk, 16)))
        self._commit(ev, reads, writes)
        return ev

    def barrier(self):
        evs = [(("e", e), self.cnt[e]) for e in ENGS if self.cnt[e] > 0]
        for e in ("sp", "pool", "act"):
            n = self.dman[e]
            for slot in range(min(n, NDMASEM)):
                uses = (n - 1 - slot) // NDMASEM + 1
                evs.append((("d", e, slot), 16 * uses))
        for eng in ENGS:
            waits = []
            for sk, v in evs:
                if sk == ("e", eng) and eng == "pe":
                    continue
                if self.seen[eng].get(sk, 0) >= v:
                    continue
                self.seen[eng][sk] = v
                waits.append((sk, v))
            if waits:
                self.ops[eng].append((waits, None, None))
        self.bw = {}
        self.br = {}

    def wait_all(self, eng, keys):
        waits = self._deps(eng, keys, [])
        self.ops[eng].append((waits, None, None))

    def emit(self):
        nc = self.nc
        with nc.Block() as block:
            def run(engobj, name):
                for waits, fn, inc in self.ops[name]:
                    attach = None
                    if fn is not None and name != "pe" and inc is not None and inc[0][0] == "e" and waits:
                        attach = waits[-1]
                        waits = waits[:-1]
                    for sk, v in waits:
                        engobj.wait_ge(self.sems[sk], v)
                    if fn is None:
                        continue
                    ins = fn(engobj)
                    if attach is not None:
                        ins._wait_ge(self.sems[attach[0]], attach[1])
                    if inc is not None:
                        ins.then_inc(self.sems[inc[0]], inc[1])

            @block.sync
            def _(e):
                run(e, "sp")

            @block.scalar
            def _(e):
                run(e, "act")

            @block.vector
            def _(e):
                run(e, "dve")

            @block.gpsimd
            def _(e):
                run(e, "pool")

            @block.tensor
            def _(e):
                run(e, "pe")

    def mm(self, out, lhsT, rhs, start, stop, reads, writes):
        return self.op("pe", lambda e: e.matmul(out, lhsT=lhsT, rhs=rhs, start=start, stop=stop), reads, writes)

    def tr(self, out, in_, ident, reads, writes):
        return self.op("pe", lambda e: e.transpose(out, in_, ident), reads, writes)

    def act(self, out, in_, func, reads, writes, scale=None, bias=None):
        kw = {}
        if scale is not None:
            kw["scale"] = scale
        if bias is not None:
            kw["bias"] = bias
        return self.op("act", lambda e: e.activation(out=out, in_=in_, func=func, **kw), reads, writes)

    def ts(self, eng, out, in0, s1, s2, op0, op1, reads, writes):
        if op1 is None:
            return self.op(eng, lambda e: e.tensor_scalar(out=out, in0=in0, scalar1=s1, scalar2=None, op0=op0), reads, writes)
        return self.op(eng, lambda e: e.tensor_scalar(out=out, in0=in0, scalar1=s1, scalar2=s2, op0=op0, op1=op1), reads, writes)

    def stt(self, out, in0, scalar, in1, op0, op1, reads, writes):
        return self.op("dve", lambda e: e.scalar_tensor_tensor(out=out, in0=in0, scalar=scalar, in1=in1, op0=op0, op1=op1), reads, writes)

    def tt(self, eng, out, in0, in1, op, reads, writes):
        return self.op(eng, lambda e: e.tensor_tensor(out=out, in0=in0, in1=in1, op=op), reads, writes)

    def copy(self, eng, out, in_, reads, writes):
        if eng == "act":
            return self.op("act", lambda e: e.copy(out=out, in_=in_), reads, writes)
        return self.op(eng, lambda e: e.tensor_copy(out=out, in_=in_), reads, writes)

    def memset(self, eng, ap, val, writes):
        return self.op(eng, lambda e: e.memset(ap, val), (), writes)

    def recip(self, out, in_, reads, writes):
        return self.op("dve", lambda e: e.reciprocal(out=out, in_=in_), reads, writes)

    def scan(self, out, d0, d1, init, reads, writes):
        return self.op("dve", lambda e: e.tensor_tensor_scan(out=out, data0=d0, data1=d1, initial=init, op0=ALU.mult, op1=ALU.add), reads, writes)


class Arena:
    def __init__(self, t, nbytes):
        self.t = t
        self.n = nbytes
        self.off = 0
        self.marks = []

    def mark(self):
        self.marks.append(self.off)

    def release(self):
        self.off = self.marks.pop()

    def alloc(self, shape, dtype):
        esz = 2 if dtype == BF16 else 4
        n = int(np.prod(shape)) * esz
        n4 = (n + 63) // 64 * 64
        assert self.off + n4 <= self.n, f"arena overflow {self.off}+{n4}>{self.n}"
        a = self.t[:, self.off // 4:(self.off + n4) // 4]
        self.off += n4
        if dtype != F32:
            a = a.bitcast(dtype)
        a = a[:, 0:int(np.prod(shape))]
        if len(shape) == 2:
            return a.rearrange("p (a b) -> p a b", a=shape[0])
        if len(shape) == 3:
            return a.rearrange("p (a b c) -> p a b c", a=shape[0], b=shape[1])
        return a


class Ctx:
    pass


def ffn_blocks():
    blks = [(0, LCTX + 2)]
    sizes = [410, 410, 410, 410, 408]
    s = 0
    for sz in sizes:
        blks.append((LCTX + 1 + s, sz + 2))
        s += sz
    return blks


def tok_blocks():
    out = [(0, LCTX, 1)]
    for i in range(4):
        out.append((LCTX + 512 * i, 512, LCTX + 2 + 512 * i))
    return out


def emit_mods(P, K):
    nc = P.nc
    ar = K.arena
    ar.mark()
    cs = ar.alloc([NCH, 5], F32)
    P.dma("sp", cs, K.d["cT"][:, :, :], writes=["cs"])
    P.act(cs, cs, AF.Silu, ["cs"], ["cs"])
    wt = [ar.alloc([NCH, 512], F32) for _ in range(3)]
    n = 0
    for l in range(DEPTH):
        wv = K.d["ada_w"][l].rearrange("(kc p) n -> p kc n", p=128)
        psb = K.ps[l % 2]
        for jg in range(12):
            buf = n % 3
            n += 1
            P.dma("sp", wt[buf], wv[:, :, jg * 512:(jg + 1) * 512], writes=[("adaw", buf)])
            for jj in range(4):
                j = jg * 4 + jj
                for kc in range(NCH):
                    P.mm(psb[:, j * 5:(j + 1) * 5], wt[buf][:, kc, jj * 128:(jj + 1) * 128], cs[:, kc, :],
                         kc == 0, kc == NCH - 1, [("adaw", buf), "cs"], [("ps", l % 2)])
        P.tt("dve", K.MOD[:, l, :, :], psb[:, 0:240].rearrange("p (j b) -> p j b", b=5),
             K.adab[:, l, :].unsqueeze(2).to_broadcast([128, 48, 5]), ALU.add,
             [("ps", l % 2), "consts"], [("MOD", l)])
        for (dst, g, s) in ((K.A1, K.n1g, 1), (K.A2, K.n2g, 4)):
            P.ts("dve", dst[:, l, :, :], K.MOD[:, l, s * 8:(s + 1) * 8, :], 1.0, None, ALU.add, None,
                 [("MOD", l)], [("A", l)])
            P.tt("dve", dst[:, l, :, :], dst[:, l, :, :], g[:, l, :].unsqueeze(2).to_broadcast([128, NCH, 5]),
                 ALU.mult, [("A", l), "consts"], [("A", l)])
    ar.release()
    P.barrier()


def emit_rstd(P, K, tagp):
    K.RSTD = K.arena.alloc([1, L], F32)[:, 0, :]
    for bi, (c0, n, _) in enumerate(tok_blocks()):
        pb = K.ps[7]
        for c in range(NCH):
            sq = K.sq[c % 2]
            P.act(sq[:, 0:n], K.X[:, c, c0:c0 + n], AF.Square, [("X", c)], [("sq", c % 2)])
            P.mm(pb[:, 0:n], K.ones, sq[:, 0:n], c == 0, c == NCH - 1, [("sq", c % 2), "consts"], [("ps", 7)])
        P.act(K.RSTD[:, c0:c0 + n], pb[:, 0:n], AF.Sqrt, [("ps", 7), "consts"], [("rstd", bi)], bias=K.epsc[:, 0:1])
        P.recip(K.RSTD[:, c0:c0 + n], K.RSTD[:, c0:c0 + n], [("rstd", bi)], [("rstd", bi)])


RSTD_KEYS = [("rstd", i) for i in range(5)]


def emit_norm_mod(P, K, A, SH, l, b, H, keep_rstd=False):
    K.arena.mark()
    emit_rstd(P, K, "n")
    i = 0
    for bi, (c0, n, p0) in enumerate(tok_blocks()):
        bcol = 4 if bi == 0 else b
        for c in range(NCH):
            tmp = K.tmpB[i % 3]
            tk = ("tmpB", i % 3)
            i += 1
            P.tt("dve", tmp[:, 0:n], K.X[:, c, c0:c0 + n], K.RSTD[:, c0:c0 + n], ALU.mult, [("X", c), ("rstd", bi)], [tk])
            P.act(H[:, c, p0:p0 + n], tmp[:, 0:n], AF.Identity, [tk, ("MOD", l), ("A", l)], [("H", c)],
                  scale=A[:, l, c, bcol:bcol + 1], bias=SH[:, c, bcol:bcol + 1])
    if keep_rstd:
        K.arena.marks.pop()
    else:
        K.arena.release()
        P.barrier()


def zero_pads(P, H, nchunks, keyname):
    for col in (0, LCTX + 1, LP - 1):
        P.memset("pool", H[:, :, col:col + 1], 0.0, [(keyname, c) for c in range(nchunks)])


def emit_ffn(P, K, l, b):
    ar = K.arena
    ar.mark()
    H = ar.alloc([NCH, LP], BF16)
    zero_pads(P, H, NCH, "H")
    emit_norm_mod(P, K, K.A2, K.MOD[:, l, 24:32, :], l, b, H)
    G2 = K.MOD[:, l, 40:48, :]
    groups = [(0, 4), (4, 4), (8, 4), (12, 4), (16, 3), (19, 3)]
    GM = 4
    M = ar.alloc([GM, LP], BF16)
    win = [ar.alloc([2, NCH, 128], BF16) for _ in range(3)]
    wout = [ar.alloc([GM, D], BF16) for _ in range(2)]
    acc = [[ar.alloc([1, 412], F32) for _ in range(3)] for _ in range(2)]
    wiv = K.d["ffn_w_in"][l].rearrange("(kc p) n -> p kc n", p=128)
    wov = K.d["ffn_w_out"][l].rearrange("(j p) n -> p j n", p=128)
    fb = ffn_blocks()
    nw = 0
    nacc = 0
    nps = 0
    issued = set()

    def issue_win(j):
        if j >= NJ or j in issued:
            return
        issued.add(j)
        wb_ = j % 3
        P.dma("pool", win[wb_][:, 0, :, :], wiv[:, :, j * 128:(j + 1) * 128], writes=[("win", wb_, 0)])
        P.dma("pool", win[wb_][:, 1, :, :], wiv[:, :, (NJ + j) * 128:(NJ + j + 1) * 128], writes=[("win", wb_, 1)])

    def issue_wout(gi_):
        if gi_ >= len(groups) or ("g", gi_) in issued:
            return
        issued.add(("g", gi_))
        j0_, nj_ = groups[gi_]
        P.dma("pool", wout[gi_ % 2][:, 0:nj_, :], wov[:, j0_:j0_ + nj_, :], writes=[("wout", gi_ % 2)])
    issue_win(0)
    issue_wout(0)
    for gi, (j0, nj) in enumerate(groups):
        wo = wout[gi % 2]
        for jl in range(nj):
            j = j0 + jl
            wb = j % 3
            issue_win(j)
            issue_win(j + 1)
            if jl == 1:
                issue_wout(gi + 1)
            for (c0, n) in fb:
                pa = nps % 2
                nps += 1
                ab = nacc % 2
                nacc += 1
                for half in range(2):
                    pst = K.ps[2 * half + pa]
                    for kc in range(NCH):
                        P.mm(pst[:, 0:n], win[wb][:, half, kc, :], H[:, kc, c0:c0 + n], kc == 0, kc == NCH - 1,
                             [("win", wb, half), ("H", kc)], [("ps", 2 * half + pa)])
                no = n - 2
                accs = acc[ab]
                for half in range(2):
                    pst = K.ps[2 * half + pa]
                    ch = j if half == 0 else NJ + j
                    a = accs[half][:, 0, 0:no]
                    kr = [("ps", 2 * half + pa), "consts"]
                    kw = [("acc", ab, half)]
                    P.act(a, pst[:, 1:1 + no], AF.Identity, kr, kw, scale=K.cw[:, l, ch, 1:2])
                    P.stt(a, pst[:, 0:no], K.cw[:, l, ch, 0:1], a, ALU.mult, ALU.add, kr + kw, kw)
                    P.stt(a, pst[:, 2:2 + no], K.cw[:, l, ch, 2:3], a, ALU.mult, ALU.add, kr + kw, kw)
                sg = accs[2][:, 0, 0:no]
                P.act(sg, accs[1][:, 0, 0:no], AF.Silu, [("acc", ab, 1)], [("acc", ab, 2)])
                P.tt("pool", M[:, jl, c0 + 1:c0 + 1 + no], accs[0][:, 0, 0:no], sg, ALU.mult,
                     [("acc", ab, 0), ("acc", ab, 2)], [("M", jl)])
        for oc in range(NCH):
            for (x0, n, p0) in tok_blocks():
                pi = 4 + (nps % 2)
                nps += 1
                for jl in range(nj):
                    P.mm(K.ps[pi][:, 0:n], wo[:, jl, oc * 128:(oc + 1) * 128], M[:, jl, p0:p0 + n], jl == 0, jl == nj - 1,
                         [("wout", gi % 2), ("M", jl)], [("ps", pi)])
                bcol = 4 if x0 == 0 else b
                P.stt(K.X[:, oc, x0:x0 + n], K.ps[pi][:, 0:n], G2[:, oc, bcol:bcol + 1], K.X[:, oc, x0:x0 + n],
                      ALU.mult, ALU.add, [("ps", pi), ("MOD", l), ("X", oc)], [("X", oc)])
    ar.release()
    P.barrier()


NA_H = 16
GRID_W = 64
NA_NT = 21


def na_qtile_info(i):
    if i == 0:
        return list(range(0, 4)), 5
    if i == 1:
        return list(range(0, 4)), 9
    if i == 14:
        return list(range(12, 16)), 13
    if i == 15:
        return list(range(12, 16)), 17
    return list(range(i - 2, i + 3)), 0


def make_na_bias(rpb):
    rpb = np.asarray(rpb, np.float32)
    tiles = [(5, j) for j in range(3, 8)]
    for i in (0, 1):
        tiles += [(i, j) for j in range(0, 4)]
    for i in (14, 15):
        tiles += [(i, j) for j in range(12, 16)]
    assert len(tiles) == NA_NT
    out = np.empty((NA_H, 128, NA_NT, 128), np.float32)
    a = np.arange(2)[:, None]
    col = np.arange(64)[None, :]
    for ti, (i, j) in enumerate(tiles):
        kr = (2 * j + a + 0 * col).reshape(128)
        kc = (0 * a + col).reshape(128)
        qr = (2 * i + a + 0 * col).reshape(128)
        qc = kc.copy()
        r0 = np.clip(qr - 4, 0, 24)
        c0 = np.clip(qc - 8, 0, 48)
        valid = ((kr[:, None] >= r0[None, :]) & (kr[:, None] <= r0[None, :] + 7)
                 & (kc[:, None] >= c0[None, :]) & (kc[:, None] < c0[None, :] + 16))
        dr = np.clip(kr[:, None] - qr[None, :] + 7, 0, 14)
        dc = np.clip(kc[:, None] - qc[None, :], -15, 15) + 15
        vals = rpb[:, dr, dc]
        out[:, :, ti, :] = np.where(valid[None], vals, np.float32(-30000.0))
    return np.ascontiguousarray(out.reshape(NA_H, 128, NA_NT * 128))


def emit_down_proj(P, K, wo, OT, G, b, l, nk, wkey, okey):
    n_ = 0
    for oc in range(NCH):
        for (x0, n, p0) in tok_blocks():
            pi = 4 + (n_ % 2)
            n_ += 1
            for kc in range(nk):
                P.mm(K.ps[pi][:, 0:n], wo[:, kc, oc * 128:(oc + 1) * 128], OT[:, kc, x0:x0 + n], kc == 0, kc == nk - 1,
                     [wkey, (okey, kc)], [("ps", pi)])
            bcol = 4 if x0 == 0 else b
            P.stt(K.X[:, oc, x0:x0 + n], K.ps[pi][:, 0:n], G[:, oc, bcol:bcol + 1], K.X[:, oc, x0:x0 + n],
                  ALU.mult, ALU.add, [("ps", pi), ("MOD", l), ("X", oc)], [("X", oc)])


def emit_na(P, K, l, b):
    jn = l // 3
    ar = K.arena
    ar.mark()
    H = ar.alloc([NCH, LP], BF16)
    emit_norm_mod(P, K, K.A1, K.MOD[:, l, 0:8, :], l, b, H)
    G1 = K.MOD[:, l, 16:24, :]
    OTs = [ar.alloc([1, L], BF16) for _ in range(2)]
    wos = [ar.alloc([1, D], BF16) for _ in range(2)]
    wqkv = [ar.alloc([3, NCH, 128], BF16) for _ in range(2)]
    Qt = ar.alloc([1, L], BF16)[:, 0, :]
    Kt = ar.alloc([1, L], BF16)[:, 0, :]
    V = ar.alloc([18, 128], BF16)
    BI1 = ar.alloc([NA_NT, 128], F32)
    BI = [BI1, BI1]
    Tb = [ar.alloc([1, 640], F32)[:, 0, :] for _ in range(2)]
    PT = [ar.alloc([1, 896], BF16)[:, 0, :] for _ in range(2)]
    rden = [ar.alloc([1, 128], F32)[:, 0, :] for _ in range(2)]
    onesb = ar.alloc([1, 128], BF16)[:, 0, :]
    P.memset("pool", onesb, 1.0, ["onesb"])
    wv_ = K.d["na_w_qkv"][jn].rearrange("(kc p) n -> p kc n", p=128)
    nu = 0
    npj = 0
    wov_ = K.d["na_w_out"][jn].rearrange("(kc p) n -> p kc n", p=128)
    for hp in range(NCH):
        wb = hp % 2
        OT = OTs[wb]
        for t3 in range(3):
            P.dma("pool", wqkv[wb][:, t3, :, :], wv_[:, :, t3 * D + hp * 128:t3 * D + (hp + 1) * 128], writes=[("wqkv", wb, t3)])
        P.dma("pool", wos[wb], wov_[:, hp:hp + 1, :], writes=[("wos", wb)])
        for (x0, n, p0) in tok_blocks():
            for t3, dst, dk_ in ((0, Qt, "Qt"), (1, Kt, "Kt")):
                pi = 6 + (npj % 2)
                npj += 1
                for kc in range(NCH):
                    P.mm(K.ps[pi][:, 0:n], wqkv[wb][:, t3, kc, :], H[:, kc, p0:p0 + n], kc == 0, kc == NCH - 1,
                         [("wqkv", wb, t3), ("H", kc)], [("ps", pi)])
                if t3 == 0:
                    P.act(dst[:, x0:x0 + n], K.ps[pi][:, 0:n], AF.Identity, [("ps", pi)], [dk_], scale=0.125)
                else:
                    P.copy("dve", dst[:, x0:x0 + n], K.ps[pi][:, 0:n], [("ps", pi)], [dk_])
        for tt in range(18):
            pc = 1 + 128 * tt if tt < 2 else LCTX + 2 + 128 * (tt - 2)
            pi = 6 + (npj % 2)
            npj += 1
            for kc in range(NCH):
                P.mm(K.ps[pi][:, 0:128], H[:, kc, pc:pc + 128], wqkv[wb][:, 2, kc, :], kc == 0, kc == NCH - 1,
                     [("wqkv", wb, 2), ("H", kc)], [("ps", pi)])
            P.copy("act", V[:, tt, :], K.ps[pi][:, 0:128], [("ps", pi)], ["V"])
        for hh in range(2):
            h = 2 * hp + hh
            hb = 64 * hh
            P.dma("sp", BI[hh], K.d["na_bias"][jn, h].rearrange("p (t q) -> p t q", q=128), writes=[("BI", 0)])
            for qt in range(18):
                u = nu % 2
                nu += 1
                if qt < 2:
                    lat_k, base = [], 0
                else:
                    lat_k, base = na_qtile_info(qt - 2)
                nk = len(lat_k)
                ktiles = [2 + j for j in lat_k] + [0, 1]
                qs = slice(qt * 128, (qt + 1) * 128)

                def sreg(i0, i1):
                    assert i0 // 4 == (i1 - 1) // 4
                    bk = 2 * u + i0 // 4
                    return K.ps[bk][:, (i0 % 4) * 128:(i0 % 4) * 128 + (i1 - i0) * 128], ("ps", bk)
                for idx, kt in enumerate(ktiles):
                    reg, rk = sreg(idx, idx + 1)
                    P.mm(reg, Kt[hb:hb + 64, kt * 128:(kt + 1) * 128], Qt[hb:hb + 64, qs],
                         True, True, ["Qt", "Kt"], [rk])
                if nk:
                    for (i0, i1) in ((0, min(nk, 4)), (4, nk)):
                        if i1 <= i0:
                            continue
                        reg, rk = sreg(i0, i1)
                        P.tt("dve", Tb[u][:, i0 * 128:i1 * 128], reg,
                             BI[hh][:, base + i0:base + i1, :].rearrange("p t q -> p (t q)"), ALU.add,
                             [rk, ("BI", 0)], [("Tb", u)])
                    P.act(PT[u][:, 0:nk * 128], Tb[u][:, 0:nk * 128], AF.Exp, [("Tb", u)], [("PT", u)])
                reg, rk = sreg(nk, nk + 2)
                P.act(PT[u][:, nk * 128:(nk + 2) * 128], reg, AF.Exp, [rk], [("PT", u)])
                Ops = K.ps[4 + u]
                nkt = len(ktiles)
                for idx, kt in enumerate(ktiles):
                    P.mm(Ops[:, 0:128], V[:, kt, :], PT[u][:, idx * 128:(idx + 1) * 128], idx == 0, idx == nkt - 1,
                         ["V", ("PT", u)], [("ps", 4 + u)])
                for idx, kt in enumerate(ktiles):
                    P.mm(Ops[:, 128:256], onesb, PT[u][:, idx * 128:(idx + 1) * 128], idx == 0, idx == nkt - 1,
                         ["onesb", ("PT", u)], [("ps", 4 + u)])
                P.recip(rden[u][hb:hb + 64, :], Ops[hb:hb + 64, 128:256], [("ps", 4 + u)], [("rden", u)])
                P.tt("dve", OT[hb:hb + 64, 0, qs], Ops[hb:hb + 64, 0:128], rden[u][hb:hb + 64, :], ALU.mult,
                     [("ps", 4 + u), ("rden", u)], [(("OT", wb), 0)])
        emit_down_proj(P, K, wos[wb], OT, G1, b, l, 1, ("wos", wb), ("OT", wb))
    ar.release()
    P.barrier()


ML_H = 4
ML_DH = 256
NEGM = -30000.0


def ml_blocks():
    blks = [(0, LCTX + 2, None, 0)]
    s = 0
    for sz in (448, 448, 448, 448, 256):
        blks.append((LCTX + 1 + s, sz + 2, s // 64, sz // 64))
        s += sz
    return blks


def tile_pcol(tt):
    return 1 + 128 * tt if tt < 2 else LCTX + 2 + 128 * (tt - 2)


def emit_mlstm(P, K, l, b):
    jn = l // 3
    ar = K.arena
    d = K.d
    G1 = K.MOD[:, l, 16:24, :]
    ar.mark()
    GT = ar.alloc([18, 16], F32)
    LFt = ar.alloc([18, 8], F32)
    IMB = ar.alloc([18, 8], F32)
    identb = ar.alloc([1, 128], BF16)[:, 0, :]
    onesb = ar.alloc([1, 128], BF16)[:, 0, :]
    P.memset("pool", onesb, 1.0, ["onesb"])
    P.copy("dve", identb, K.ident, ["consts"], ["identb"])
    ar.mark()
    H = ar.alloc([NCH, LP], BF16)
    zero_pads(P, H, NCH, "H")
    emit_norm_mod(P, K, K.A1, K.MOD[:, l, 0:8, :], l, b, H)
    wif = ar.alloc([NCH, 16], BF16)
    P.dma("pool", wif, d["ml_wif"][:, :, :], writes=["wif"])
    for tt in range(18):
        pc = tile_pcol(tt)
        for kc in range(NCH):
            P.mm(K.ps[7][:, tt * 16:(tt + 1) * 16], H[:, kc, pc:pc + 128], wif[:, kc, :], kc == 0, kc == NCH - 1,
                 ["wif", ("H", kc)], [("ps", 7)])
    P.tt("dve", GT, K.ps[7][:, 0:288].rearrange("p (t g) -> p t g", g=16),
         K.mlbif.unsqueeze(1).to_broadcast([128, 18, 16]), ALU.add, [("ps", 7), "consts"], ["GT"])
    GTv = GT.rearrange("p t (d g) -> p t d g", d=2)
    LFv = LFt.rearrange("p t (d h) -> p t d h", d=2)
    IMv = IMB.rearrange("p t (d h) -> p t d h", d=2)
    P.act(LFv, GTv[:, :, :, 4:8], AF.Exp, ["GT"], ["LFt"], scale=-1.0)
    P.act(LFv, LFv, AF.Ln, ["LFt", "consts"], ["LFt"], bias=K.onec[:, 0:1])
    P.ts("dve", LFt, LFt, -1.0, None, ALU.mult, None, ["LFt"], ["LFt"])
    for tt in range(18):
        for dr in range(2):
            P.mm(K.ps[6][:, tt * 8 + dr * 4:tt * 8 + dr * 4 + 4], K.tri[:, dr, :], LFt[:, tt, dr * 4:(dr + 1) * 4], True, True,
                 ["LFt", "consts"], [("ps", 6)])
    P.tt("dve", IMv, GTv[:, :, :, 0:4], K.ps[6][:, 0:144].rearrange("p (t d h) -> p t d h", d=2, h=4), ALU.subtract,
         ["GT", ("ps", 6)], ["IMB"])
    wqk = ar.alloc([4, NCH, 128], BF16)
    wvo = ar.alloc([2, NCH, 256], BF16)
    QK = ar.alloc([4, L], BF16)
    Vst = ar.alloc([18, 256], BF16)
    Og = ar.alloc([2, L], BF16)
    accs = [[ar.alloc([1, 450], F32)[:, 0, :] for _ in range(2)] for _ in range(2)]
    rt = [ar.alloc([1, 448], F32)[:, 0, :] for _ in range(4)]
    wv_ = d["ml_w_in"][jn].rearrange("(kc p) n -> p kc n", p=128)
    npj = 0
    for hd in range(ML_H):
        for qk in range(2):
            base = qk * D + hd * ML_DH
            for ab in range(2):
                for hf in range(2):
                    c0 = base + 128 * hf + 64 * ab
                    P.dma("pool", wqk[:, 2 * qk + ab, :, 64 * hf:64 * hf + 64], wv_[:, :, c0:c0 + 64], writes=[("wqk", 2 * qk + ab)])
        for vo in range(2):
            c0 = (2 + vo) * D + hd * ML_DH
            P.dma("pool", wvo[:, vo, :, :], wv_[:, :, c0:c0 + 256], writes=[("wvo", vo)])
        for qk in range(2):
            for (c0, n, r0, nr) in ml_blocks():
                no = n - 2
                for ab in range(2):
                    pi = 2 * ab + (npj % 2)
                    for kc in range(NCH):
                        P.mm(K.ps[pi][:, 0:n], wqk[:, 2 * qk + ab, kc, :], H[:, kc, c0:c0 + n], kc == 0, kc == NCH - 1,
                             [("wqk", 2 * qk + ab), ("H", kc)], [("ps", pi)])
                    a = accs[ab][npj % 2][:, 0:no]
                    ak = ("macc", ab, npj % 2)
                    cwi = (qk * ML_H + hd) * 2 + ab
                    P.act(a, K.ps[pi][:, 1:1 + no], AF.Identity, [("ps", pi), "consts"], [ak], scale=K.mlcw[:, cwi, 1:2])
                    P.stt(a, K.ps[pi][:, 0:no], K.mlcw[:, cwi, 0:1], a, ALU.mult, ALU.add, [("ps", pi), "consts", ak], [ak])
                    P.stt(a, K.ps[pi][:, 2:2 + no], K.mlcw[:, cwi, 2:3], a, ALU.mult, ALU.add, [("ps", pi), "consts", ak], [ak])
                    P.act(a, a, AF.Silu, [ak], [ak])
                A_ = accs[0][npj % 2][:, 0:no]
                B_ = accs[1][npj % 2][:, 0:no]
                kA = ("macc", 0, npj % 2)
                kB = ("macc", 1, npj % 2)
                npj += 1
                x0 = c0
                if r0 is None:
                    for ab, src, sk in ((0, A_, kA), (1, B_, kB)):
                        if qk == 0:
                            P.act(QK[:, ab, 0:LCTX], src, AF.Identity, [sk], [("QK", ab)], scale=1.0 / 16.0)
                        else:
                            P.copy("dve", QK[:, 2 + ab, 0:LCTX], src, [sk], [("QK", 2 + ab)])
                    continue
                xs = c0 - 1
                tb = K.ropeq if qk == 0 else K.ropek
                for hfp in range(2):
                    ps_ = slice(64 * hfp, 64 * hfp + 64)
                    if hfp == 0:
                        cosv = tb[ps_, 0, r0:r0 + nr].unsqueeze(2).to_broadcast([64, nr, 64])
                        sinv = tb[ps_, 1, r0:r0 + nr].unsqueeze(2).to_broadcast([64, nr, 64])
                    else:
                        cosv = tb[ps_, 0, :].unsqueeze(1).to_broadcast([64, nr, 64])
                        sinv = tb[ps_, 1, :].unsqueeze(1).to_broadcast([64, nr, 64])

                    def v3(t_):
                        return t_[ps_, 0:no].rearrange("p (r c) -> p r c", c=64)
                    rk = [("rt", i, hfp) for i in range(4)]
                    P.tt("pool", v3(rt[0]), v3(A_), cosv, ALU.mult, [kA, "consts"], [rk[0]])
                    P.tt("pool", v3(rt[1]), v3(B_), sinv, ALU.mult, [kB, "consts"], [rk[1]])
                    P.tt("pool", v3(rt[2]), v3(A_), sinv, ALU.mult, [kA, "consts"], [rk[2]])
                    P.tt("pool", v3(rt[3]), v3(B_), cosv, ALU.mult, [kB, "consts"], [rk[3]])
                    P.tt("dve", QK[ps_, 2 * qk + 0, xs:xs + no], rt[0][ps_, 0:no], rt[1][ps_, 0:no], ALU.subtract,
                         [rk[0], rk[1]], [("QK", 2 * qk)])
                    P.tt("dve", QK[ps_, 2 * qk + 1, xs:xs + no], rt[2][ps_, 0:no], rt[3][ps_, 0:no], ALU.add,
                         [rk[2], rk[3]], [("QK", 2 * qk + 1)])
        for tt in range(18):
            pc = tile_pcol(tt)
            pi = 4 + (tt % 2)
            for kc in range(NCH):
                P.mm(K.ps[pi][:, 0:256], H[:, kc, pc:pc + 128], wvo[:, 0, kc, :], kc == 0, kc == NCH - 1,
                     [("wvo", 0), ("H", kc)], [("ps", pi)])
            P.copy("act", Vst[:, tt, :], K.ps[pi][:, 0:256], [("ps", pi)], ["Vst"])
        n_ = 0
        for mc in range(2):
            for (x0, n, p0) in tok_blocks():
                pi = 4 + (n_ % 2)
                n_ += 1
                for kc in range(NCH):
                    P.mm(K.ps[pi][:, 0:n], wvo[:, 1, kc, mc * 128:(mc + 1) * 128], H[:, kc, p0:p0 + n], kc == 0, kc == NCH - 1,
                         [("wvo", 1), ("H", kc)], [("ps", pi)])
                P.act(Og[:, mc, x0:x0 + n], K.ps[pi][:, 0:n], AF.Sigmoid, [("ps", pi)], [("Og", mc)])
        P.dma("sp", d["ml_sq"][hd], QK, reads=[("QK", i) for i in range(4)], writes=[("ml_sq", hd)])
        P.dma("sp", d["ml_sv"][hd], Vst, reads=["Vst"], writes=[("ml_sv", hd)])
        P.dma("sp", d["ml_so"][hd], Og, reads=[("Og", 0), ("Og", 1)], writes=[("ml_so", hd)])
    ar.release()
    P.barrier()
    ar.mark()
    QK = ar.alloc([4, L], BF16)
    Vst = ar.alloc([18, 256], BF16)
    Og = ar.alloc([2, L], BF16)
    HS = ar.alloc([2, L], F32)
    wo = ar.alloc([2, D], BF16)
    Cst = [ar.alloc([2, 384], F32) for _ in range(2)]
    Cb = [ar.alloc([2, 384], BF16) for _ in range(2)]
    LFr = [ar.alloc([1, 128], F32)[:, 0, :] for _ in range(2)]
    Targ = [ar.alloc([1, 128], F32)[:, 0, :] for _ in range(2)]
    Dm = [ar.alloc([1, 128], F32)[:, 0, :] for _ in range(2)]
    WT = [ar.alloc([1, 128], BF16)[:, 0, :] for _ in range(2)]
    Ebc = [ar.alloc([1, 128], F32)[:, 0, :] for _ in range(2)]
    Qtl = [ar.alloc([2, 128], BF16) for _ in range(2)]
    ktl = [ar.alloc([1, 256], BF16)[:, 0, :] for _ in range(2)]
    rr = [ar.alloc([1, 128], F32)[:, 0, :] for _ in range(2)]
    tmo = [ar.alloc([1, 128], F32)[:, 0, :] for _ in range(2)]
    smallc = [ar.alloc([1, 4], F32)[:, 0, :] for _ in range(2)]
    psT = K.ps[6].bitcast(BF16)
    wov = d["ml_w_out"][jn].rearrange("(kc p) n -> p kc n", p=128)
    for hd in range(ML_H):
        P.dma("sp", QK, d["ml_sq"][hd], writes=[("QK", i) for i in range(4)])
        P.dma("sp", Vst, d["ml_sv"][hd], writes=["Vst"])
        P.dma("sp", Og, d["ml_so"][hd], writes=[("Og", 0), ("Og", 1)])
        P.dma("pool", wo, wov[:, 2 * hd:2 * hd + 2, :], writes=["wo"])
        it = 0
        orders = {0: list(range(18)), 1: [1, 0] + list(range(17, 1, -1))}
        pos = {dr_: {t_: i_ for i_, t_ in enumerate(orders[dr_])} for dr_ in range(2)}
        for dr in range(2):
            P.memset("pool", Cst[dr], 0.0, [("C", dr)])
        for oi in range(18):
            for dr in range(2):
                order = orders[dr]
                tt = order[oi]
                te = 127 if dr == 0 else 0
                C_ = Cst[dr]
                Cb_ = Cb[dr]
                hs_first = (pos[dr][tt], dr) < (pos[1 - dr][tt], 1 - dr)
                u = it % 2
                it += 1
                first = oi == 0
                last = oi == len(order) - 1
                cs = slice(tt * 128, (tt + 1) * 128)
                g = dr * 4 + hd
                pb = K.ps[u]
                pn = K.ps[2 + u]
                P.copy("pool", LFr[u], LFt[:, tt, g:g + 1].to_broadcast([128, 128]), ["LFt"], [("LFr", u)])
                P.mm(pb[:, 0:128], LFr[u], K.tri[:, dr, :], True, True, [("LFr", u), "consts"], [("ps", u)])
                P.mm(pb[:, 128:256], QK[:, 2, cs], QK[:, 0, cs], True, False, [("QK", 2), ("QK", 0)], [("ps", u)])
                P.mm(pb[:, 128:256], QK[:, 3, cs], QK[:, 1, cs], False, True, [("QK", 3), ("QK", 1)], [("ps", u)])
                P.stt(Targ[u], pb[:, 0:128], IMB[:, tt, g:g + 1], K.negm[:, dr, :], ALU.add, ALU.add,
                      [("ps", u), "IMB", "consts"], [("Targ", u)])
                P.act(Dm[u], Targ[u], AF.Exp, [("Targ", u)], [("Dm", u)])
                P.tt("dve", WT[u], pb[:, 128:256], Dm[u], ALU.mult, [("ps", u), ("Dm", u)], [("WT", u)])
                if not first or not last:
                    P.act(Ebc[u], pb[:, 0:128], AF.Exp, [("ps", u)], [("Ebc", u)])
                if not first:
                    P.tt("pool", Qtl[u][:, 0, :], QK[:, 0, cs], Ebc[u], ALU.mult, [("QK", 0), ("Ebc", u)], [("Qtl", u)])
                    P.tt("pool", Qtl[u][:, 1, :], QK[:, 1, cs], Ebc[u], ALU.mult, [("QK", 1), ("Ebc", u)], [("Qtl", u)])
                for mc in range(3):
                    lh = Vst[:, tt, mc * 128:(mc + 1) * 128] if mc < 2 else onesb
                    P.mm(pn[:, mc * 128:(mc + 1) * 128], lh, WT[u], True, first, ["Vst", "onesb", ("WT", u)], [("ps", 2 + u)])
                    if not first:
                        for kc in range(2):
                            P.mm(pn[:, mc * 128:(mc + 1) * 128], Cb_[:, kc, mc * 128:(mc + 1) * 128], Qtl[u][:, kc, :], False, kc == 1,
                                 [("Cb", dr), ("Qtl", u)], [("ps", 2 + u)])
                P.act(rr[u], pn[:, 256:384], AF.Abs, [("ps", 2 + u)], [("rr", u)])
                P.ts("dve", rr[u], rr[u], 1.0, None, ALU.max, None, [("rr", u)], [("rr", u)])
                P.recip(rr[u], rr[u], [("rr", u)], [("rr", u)])
                for mc in range(2):
                    if hs_first:
                        P.tt("dve", HS[:, mc, cs], pn[:, mc * 128:(mc + 1) * 128], rr[u], ALU.mult,
                             [("ps", 2 + u), ("rr", u)], [("HS", mc)])
                    else:
                        P.tt("dve", tmo[mc], pn[:, mc * 128:(mc + 1) * 128], rr[u], ALU.mult,
                             [("ps", 2 + u), ("rr", u)], [("tmo", mc)])
                        P.tt("pool", HS[:, mc, cs], HS[:, mc, cs], tmo[mc], ALU.add, [("HS", mc), ("tmo", mc)], [("HS", mc)])
                if last:
                    continue
                sc_ = smallc[u]
                P.copy("dve", sc_[:, 0:1], pb[:, te:te + 1], [("ps", u)], [("smallc", u)])
                P.act(sc_[:, 1:2], IMB[:, tt, g:g + 1], AF.Exp, ["IMB", ("smallc", u)], [("smallc", u)], bias=sc_[:, 0:1])
                P.tr(psT[:, 0:128], QK[:, 2, cs], identb, [("QK", 2), "identb"], [("ps", 6)])
                P.tr(psT[:, 128:256], QK[:, 3, cs], identb, [("QK", 3), "identb"], [("ps", 6)])
                P.ts("dve", ktl[u], psT[:, 0:256], sc_[:, 1:2], None, ALU.mult, None, [("ps", 6), ("smallc", u)], [("ktl", u)])
                for kc in range(2):
                    pc_ = K.ps[4 + kc]
                    P.mm(pc_[:, 0:256], ktl[u][:, kc * 128:(kc + 1) * 128], Vst[:, tt, :], True, True, [("ktl", u), "Vst"], [("ps", 4 + kc)])
                    P.mm(pc_[:, 256:384], ktl[u][:, kc * 128:(kc + 1) * 128], onesb, True, True, [("ktl", u), "onesb"], [("ps", 4 + kc)])
                    P.stt(C_[:, kc, :], C_[:, kc, :], Ebc[u][:, te:te + 1], pc_[:, 0:384], ALU.mult, ALU.add,
                          [("C", dr), ("Ebc", u), ("ps", 4 + kc)], [("C", dr)])
                P.copy("act", Cb_, C_, [("C", dr)], [("Cb", dr)])
        HN = QK[:, 0:2, :]
        for bi, (x0, n, p0) in enumerate(tok_blocks()):
            for mc in range(2):
                sq = K.sq[mc]
                P.act(sq[:, 0:n], HS[:, mc, x0:x0 + n], AF.Square, [("HS", mc)], [("sq", mc)])
                P.mm(K.ps[7][:, 0:n], K.ones, sq[:, 0:n], mc == 0, mc == 1, [("sq", mc), "consts"], [("ps", 7)])
            rs = K.tmpB[bi % 3]
            rk = ("tmpB", bi % 3)
            P.act(rs[:, 0:n], K.ps[7][:, 0:n], AF.Sqrt, [("ps", 7), "consts"], [rk], bias=K.epsc[:, 0:1], scale=4.0)
            P.recip(rs[:, 0:n], rs[:, 0:n], [rk], [rk])
            for mc in range(2):
                P.tt("dve", HS[:, mc, x0:x0 + n], HS[:, mc, x0:x0 + n], rs[:, 0:n], ALU.mult, [("HS", mc), rk], [("HS", mc)])
                P.stt(HN[:, mc, x0:x0 + n], HS[:, mc, x0:x0 + n], K.mlng[:, 2 * hd + mc:2 * hd + mc + 1], Og[:, mc, x0:x0 + n],
                      ALU.mult, ALU.mult, [("HS", mc), ("Og", mc), "consts"], [("QK", mc)])
        emit_down_proj(P, K, wo, HN, G1, b, l, 2, "wo", "QK")
    ar.release()
    ar.release()
    P.barrier()


S5_G2 = 32
S5_T = 256
S5_NC = L // S5_T
GELU_C = 1.5957691216057308


def s5_scalars(P, K, ar, W, LR, LI, LDT, tag, want_q):
    def T():
        return ar.alloc([1, W], F32)[:, 0, :]

    def k(n):
        return (tag, n)
    lr, er, c, s_ = [T() for _ in range(4)]
    ar.mark()
    dt, ang, t1, t2 = [T() for _ in range(4)]
    P.act(dt, LDT, AF.Exp, [k("in")], [k("dt")])
    P.ts("dve", lr, LR, -1e-4, None, ALU.min, None, [k("in")], [k("lr")])
    P.tt("dve", t1, lr, dt, ALU.mult, [k("lr"), k("dt")], [k("t1")])
    P.act(er, t1, AF.Exp, [k("t1")], [k("er")])
    P.tt("dve", ang, LI, dt, ALU.mult, [k("in"), k("dt")], [k("ang")])
    ki = ar.alloc([1, W], I32)[:, 0, :]
    kr, ph, x2 = T(), T(), T()
    P.ts("dve", t1, ang, 1.0 / (2.0 * np.pi), 0.25, ALU.mult, ALU.add, [k("ang")], [k("t1")])
    P.copy("dve", ki, t1, [k("t1")], [k("ki")])
    P.copy("dve", kr, ki, [k("ki")], [k("kr")])
    P.stt(ph, kr, -6.28125, ang, ALU.mult, ALU.add, [k("kr"), k("ang")], [k("ph")])
    P.stt(ph, kr, -1.9353071795864769e-3, ph, ALU.mult, ALU.add, [k("kr"), k("ph")], [k("ph")])
    P.ts("dve", ph, ph, 0.25, None, ALU.mult, None, [k("ph")], [k("ph")])
    P.tt("dve", x2, ph, ph, ALU.mult, [k("ph")], [k("x2")])
    import math
    sco = [(-1.0) ** i / math.factorial(2 * i + 1) for i in range(1, 7)]
    cco = [(-1.0) ** i / math.factorial(2 * i) for i in range(1, 7)]
    P.ts("dve", s_, x2, sco[5], None, ALU.mult, None, [k("x2")], [k("s")])
    for i in range(4, -1, -1):
        P.stt(s_, s_, sco[i], x2, ALU.add, ALU.mult, [k("s"), k("x2")], [k("s")])
    P.stt(s_, s_, 1.0, ph, ALU.add, ALU.mult, [k("s"), k("ph")], [k("s")])
    P.ts("dve", c, x2, cco[5], None, ALU.mult, None, [k("x2")], [k("c")])
    for i in range(4, -1, -1):
        P.stt(c, c, cco[i], x2, ALU.add, ALU.mult, [k("c"), k("x2")], [k("c")])
    P.ts("dve", c, c, 1.0, None, ALU.add, None, [k("c")], [k("c")])
    for it in range(2):
        P.tt("dve", t1, c, c, ALU.mult, [k("c")], [k("t1")])
        P.tt("dve", t2, s_, s_, ALU.mult, [k("s")], [k("t2")])
        P.stt(s_, c, 2.0, s_, ALU.mult, ALU.mult, [k("c"), k("s")], [k("s")])
        P.tt("dve", c, t1, t2, ALU.subtract, [k("t1"), k("t2")], [k("c")])
    out = {"er": er, "c": c, "s": s_}
    ar.release()
    P.barrier()
    if want_q:
        are, aim, den, nr, qre, qim, t1, t2 = [T() for _ in range(8)]
        P.tt("dve", are, er, c, ALU.mult, [k("er"), k("c")], [k("are")])
        P.tt("dve", aim, er, s_, ALU.mult, [k("er"), k("s")], [k("aim")])
        P.tt("dve", t1, lr, lr, ALU.mult, [k("lr")], [k("t1")])
        P.tt("dve", den, LI, LI, ALU.mult, [k("in")], [k("den")])
        P.tt("dve", den, den, t1, ALU.add, [k("den"), k("t1")], [k("den")])
        P.recip(den, den, [k("den")], [k("den")])
        P.ts("dve", nr, are, -1.0, None, ALU.add, None, [k("are")], [k("nr")])
        P.tt("dve", t1, nr, lr, ALU.mult, [k("nr"), k("lr")], [k("t1")])
        P.tt("dve", t2, aim, LI, ALU.mult, [k("aim"), k("in")], [k("t2")])
        P.tt("dve", qre, t1, t2, ALU.add, [k("t1"), k("t2")], [k("qre")])
        P.tt("dve", qre, qre, den, ALU.mult, [k("qre"), k("den")], [k("qre")])
        P.tt("dve", t1, aim, lr, ALU.mult, [k("aim"), k("lr")], [k("t1")])
        P.tt("dve", t2, nr, LI, ALU.mult, [k("nr"), k("in")], [k("t2")])
        P.tt("dve", qim, t1, t2, ALU.subtract, [k("t1"), k("t2")], [k("qim")])
        P.tt("dve", qim, qim, den, ALU.mult, [k("qim"), k("den")], [k("qim")])
        out["qre"] = qre
        out["qim"] = qim
    return out


def emit_s5_gen(P, K, js):
    ar = K.arena
    d = K.d
    ar.mark()
    p1 = ar.alloc([3, 64], F32)
    P.dma("sp", p1, d["s5p1"][js], writes=[("g1", "in")])
    sc = s5_scalars(P, K, ar, 64, p1[:, 0, :], p1[:, 1, :], p1[:, 2, :], "g1", False)
    P.copy("dve", K.S5R[:, js, 0, :], sc["er"], [("g1", "er")], [("S5R", js)])
    for dh in range(2):
        ar.mark()
        COS = ar.alloc([32, S5_T], F32)
        SIN = ar.alloc([32, S5_T], F32)
        t1 = ar.alloc([32, S5_T // 2], F32)
        t2 = ar.alloc([32, S5_T // 2], F32)
        P.copy("dve", COS[:, :, 0:1], sc["c"][:, dh * 32:(dh + 1) * 32].unsqueeze(2), [("g1", "c")], ["COS"])
        P.copy("dve", SIN[:, :, 0:1], sc["s"][:, dh * 32:(dh + 1) * 32].unsqueeze(2), [("g1", "s")], ["SIN"])
        for kk in range(8):
            ln = 2 ** kk
            pr = COS[:, :, ln - 1:ln].to_broadcast([128, 32, ln])
            pi_ = SIN[:, :, ln - 1:ln].to_broadcast([128, 32, ln])
            a1 = t1[:, :, 0:ln]
            a2 = t2[:, :, 0:ln]
            P.tt("dve", a1, COS[:, :, 0:ln], pr, ALU.mult, ["COS"], ["t1"])
            P.tt("dve", a2, SIN[:, :, 0:ln], pi_, ALU.mult, ["SIN"], ["t2"])
            P.tt("dve", COS[:, :, ln:2 * ln], a1, a2, ALU.subtract, ["t1", "t2", "COS"], ["COSn"])
            P.tt("dve", a1, COS[:, :, 0:ln], pi_, ALU.mult, ["COS", "SIN", "COSn"], ["t1"])
            P.tt("dve", a2, SIN[:, :, 0:ln], pr, ALU.mult, ["SIN", "COS", "COSn"], ["t2"])
            P.tt("dve", SIN[:, :, ln:2 * ln], a1, a2, ALU.add, ["t1", "t2", "SIN"], ["SIN"])
            P.copy("dve", COS[:, :, 0:1], COS[:, :, 0:1], ["COSn", "COS"], ["COS"])
        P.copy("dve", K.S5R[:, js, 1, dh * 32:(dh + 1) * 32].unsqueeze(2), COS[:, :, S5_T - 1:S5_T], ["COS"], [("S5R", js)])
        P.copy("dve", K.S5R[:, js, 2, dh * 32:(dh + 1) * 32].unsqueeze(2), SIN[:, :, S5_T - 1:S5_T], ["SIN"], [("S5R", js)])
        P.dma("sp", d["s5t"][js, :, 0, dh * 32:(dh + 1) * 32, :], COS, reads=["COS"], writes=[("s5t", js, 0, dh)])
        P.dma("sp", d["s5t"][js, :, 1, dh * 32:(dh + 1) * 32, :], SIN, reads=["SIN"], writes=[("s5t", js, 1, dh)])
        ar.release()
        P.barrier()
    ar.release()
    P.barrier()
    ar.mark()
    CC = ar.alloc([2, 1024], F32)
    P.dma("sp", CC, d["s5c"][js], writes=["CC"])
    for var in range(3):
        WC = ar.alloc([2, S5_G2, 128], BF16)
        P.memset("pool", WC, 0.0, [("WC", var)])
        WCv = WC.rearrange("p d (c k) m -> p d c k m", k=4)
        CCv = CC[:, var % 2, :].rearrange("p (d c k h) -> p d c k h", d=2, c=8, k=4)
        for gl in range(2):
            for k4 in range(4):
                o_ = WCv[64 * gl:64 * gl + 64, :, :, k4, 32 * k4 + 16 * gl:32 * k4 + 16 * gl + 16]
                i_ = CCv[64 * gl:64 * gl + 64, :, :, k4, :]
                if var == 0:
                    P.copy("dve", o_, i_, ["CC"], [("WC", var)])
                else:
                    P.ts("dve", o_, i_, -1.0, None, ALU.mult, None, ["CC"], [("WC", var)])
        P.dma("sp", d["s5wc"][js, :, var], WC, reads=[("WC", var)], writes=[("s5wc", js, var)])
    ar.release()
    P.barrier()
    ar.mark()
    p2 = ar.alloc([5, 1024], F32)
    P.dma("sp", p2, d["s5p2"][js], writes=[("g2", "in"), "BB"])
    sc = s5_scalars(P, K, ar, 1024, p2[:, 0, :], p2[:, 1, :], p2[:, 2, :], "g2", True)
    t1 = p2[:, 0, :]
    t2 = p2[:, 1, :]
    Bb = ar.alloc([1, 1024], F32)[:, 0, :]
    WB = ar.alloc([2, S5_G2, 128], BF16)
    WBv = WB.rearrange("p d (c k) (g m) -> p d c k g m", k=4, g=2)
    for var in range(2):
        x1, x2 = (p2[:, 3, :], p2[:, 4, :]) if var == 0 else (p2[:, 4, :], p2[:, 3, :])
        P.tt("dve", t1, sc["qre"], x1, ALU.mult, [("g2", "qre"), "BB", ("g2", "in"), ("g2", "lr")], ["bt1"])
        P.tt("dve", t2, sc["qim"], x2, ALU.mult, [("g2", "qim"), "BB", ("g2", "in"), ("g2", "ang"), ("g2", "den")], ["bt2"])
        P.tt("dve", Bb, t1, t2, ALU.subtract if var == 0 else ALU.add, ["bt1", "bt2"], ["Bb"])
        P.memset("pool", WB, 0.0, ["WB"])
        Bv = Bb.rearrange("p (d c m) -> p d c m", d=2, c=8)
        for k4 in range(4):
            for gl in range(2):
                P.ts("dve", WBv[:, :, :, k4, gl, :], Bv, K.mk8[:, 2 * k4 + gl:2 * k4 + gl + 1], None, ALU.mult, None,
                     ["Bb", "consts"], ["WB"])
        P.dma("sp", d["s5wb"][js, :, var], WB, reads=["WB"], writes=[("s5wb", js, var)])
    ar.release()
    P.barrier()


def emit_s5(P, K, l, b):
    js = l // 3
    ar = K.arena
    d = K.d
    G1 = K.MOD[:, l, 16:24, :]
    ar.mark()
    H = ar.alloc([NCH, LP], BF16)
    emit_norm_mod(P, K, K.A1, K.MOD[:, l, 0:8, :], l, b, H, keep_rstd=True)
    SH1 = K.MOD[:, l, 0:8, :]
    RSTD = K.RSTD
    Wc1 = ar.alloc([5, 8, 128], BF16)
    Wc = [Wc1, Wc1]
    ar.mark()
    T_ = S5_T
    tabs = [ar.alloc([2, T_], F32) for _ in range(4)]
    NB2 = 4
    tq = [[ar.alloc([1, T_], F32)[:, 0, :] for _ in range(4)] for _ in range(NB2)]
    bt = [[ar.alloc([1, T_], F32)[:, 0, :] for _ in range(2)] for _ in range(NB2)]
    zz = [[ar.alloc([1, T_], F32)[:, 0, :] for _ in range(2)] for _ in range(NB2)]
    PP = [ar.alloc([4, T_], BF16) for _ in range(NB2)]
    car = [[ar.alloc([1, 4], F32)[:, 0, :] for _ in range(2)] for _ in range(4)]
    segs = [(0, 0, 256, 4)] + [(0, 256, 256, b)] + [(bk, 0, 512, b) for bk in (1, 2, 3)] + [(4, 0, 256, b)]
    Rt = K.S5R[:, js, 0, :]
    CAc = K.S5R[:, js, 1, :]
    CAs = K.S5R[:, js, 2, :]
    it = 0
    ntab = 0
    NCK = S5_NC
    for c in range(NCH):
        wb = 0
        bank_started = set()
        for v in range(2):
            P.dma("sp", Wc[wb][:, v, :, :].rearrange("p (d k) m -> p d k m", d=2),
                  d["s5wb"][js, :, v].rearrange("p d (c k) m -> p d c k m", k=4)[:, :, c, :, :], writes=[("Wc", wb, v)])
        for v in range(3):
            P.dma("sp", Wc[wb][:, 2 + v, :, :].rearrange("p (d k) m -> p d k m", d=2),
                  d["s5wc"][js, :, v].rearrange("p d (c k) m -> p d c k m", k=4)[:, :, c, :, :], writes=[("Wc", wb, 2 + v)])
        for k4g in range(4):
            streams = []
            for k4 in (k4g,):
                for dr in range(2):
                    cg = dr * 32 + 4 * c + k4
                    si_ = 2 * (k4g % 2) + dr
                    tb = tabs[si_]
                    tk_ = ("tabs", si_)
                    P.dma("sp", tb, d["s5t"][js, :, :, cg, :], writes=[tk_])
                    order = list(range(NCK)) if dr == 0 else [0] + list(range(NCK - 1, 0, -1))
                    streams.append((dr, cg, tb, tk_, order, k4, si_))
            pvs = {}

            def stage1(oi):
                nonlocal it
                for (dr, cg, tb, tk_, order, k4, si_) in streams:
                    tt_ = order[oi]
                    it += 1
                    pc = 1 if tt_ == 0 else LCTX + 2 + T_ * (tt_ - 1)
                    bki_ = 5 + (it % 3)
                    pv = K.ps[bki_]
                    pvk = ("ps", bki_)
                    wi = dr * 4 + k4
                    P.mm(pv[:, 0:T_], Wc[wb][:, 0, wi, :], H[:, c, pc:pc + T_], True, True, [("Wc", wb, 0), ("H", c)], [pvk])
                    P.mm(pv[:, T_:2 * T_], Wc[wb][:, 1, wi, :], H[:, c, pc:pc + T_], True, True, [("Wc", wb, 1), ("H", c)], [pvk])
                    pvs[(oi, si_)] = (pv, pvk)
            stage1(0)
            for oi in range(NCK):
                for (dr, cg, tb, tk_, order, k4, si_) in streams:
                    u = si_
                    pv, pvk = pvs[(oi, si_)]
                    if dr == 0:
                        vr = pv[:, 0:T_]
                        vi = pv[:, T_:2 * T_]
                    else:
                        vr = pv[:, T_ - 1::-1]
                        vi = pv[:, 2 * T_ - 1:T_ - 1:-1]
                    cosT = tb[:, 0, :]
                    sinT = tb[:, 1, :]
                    q = tq[u]
                    qk = [("tq", u, i) for i in range(4)]
                    P.tt("dve", q[0], vr, cosT, ALU.mult, [pvk, tk_], [qk[0]])
                    P.tt("dve", q[1], vi, sinT, ALU.mult, [pvk, tk_], [qk[1]])
                    P.tt("dve", q[2], vi, cosT, ALU.mult, [pvk, tk_], [qk[2]])
                    P.tt("dve", q[3], vr, sinT, ALU.mult, [pvk, tk_], [qk[3]])
                for (dr, cg, tb, tk_, order, k4, si_) in streams:
                    u = si_
                    q = tq[u]
                    qk = [("tq", u, i) for i in range(4)]
                    P.tt("pool", bt[u][0], q[0], q[1], ALU.add, [qk[0], qk[1]], [("bt", u, 0)])
                    P.tt("pool", bt[u][1], q[2], q[3], ALU.subtract, [qk[2], qk[3]], [("bt", u, 1)])
                for (dr, cg, tb, tk_, order, k4, si_) in streams:
                    u = si_
                    Rb = Rt[:, cg:cg + 1].to_broadcast([128, T_])
                    cr = car[si_]
                    for ri in range(2):
                        init = 0.0 if oi == 0 else cr[ri][:, 0:1]
                        P.scan(zz[u][ri], Rb, bt[u][ri], init, [("bt", u, ri), ("S5R", js), ("car", si_, ri)], [("zz", u, ri)])
                if oi < NCK - 1:
                    for (dr, cg, tb, tk_, order, k4, si_) in streams:
                        u = si_
                        cr = car[si_]
                        zr = zz[u][0]
                        zi = zz[u][1]
                        P.tt("pool", cr[0][:, 1:2], zi[:, T_ - 1:T_], CAs[:, cg:cg + 1], ALU.mult, [("zz", u, 1), ("S5R", js)], [("cart", si_, 0)])
                        P.tt("pool", cr[1][:, 1:2], zi[:, T_ - 1:T_], CAc[:, cg:cg + 1], ALU.mult, [("zz", u, 1), ("S5R", js)], [("cart", si_, 1)])
                for (dr, cg, tb, tk_, order, k4, si_) in streams:
                    u = si_
                    zr = zz[u][0]
                    zi = zz[u][1]
                    cosT = tb[:, 0, :]
                    sinT = tb[:, 1, :]
                    Pu = PP[u]

                    def po(i, Pu=Pu, dr=dr):
                        return Pu[:, i, :] if dr == 0 else Pu[:, i, T_ - 1::-1]
                    pk = [("PP", u, i) for i in range(4)]
                    P.tt("dve", po(0), zr, cosT, ALU.mult, [("zz", u, 0), tk_], [pk[0]])
                    P.tt("pool", po(1), zi, sinT, ALU.mult, [("zz", u, 1), tk_], [pk[1]])
                    P.tt("pool", po(2), zr, sinT, ALU.mult, [("zz", u, 0), tk_], [pk[2]])
                    P.tt("pool", po(3), zi, cosT, ALU.mult, [("zz", u, 1), tk_], [pk[3]])
                if oi < NCK - 1:
                    for (dr, cg, tb, tk_, order, k4, si_) in streams:
                        u = si_
                        cr = car[si_]
                        zr = zz[u][0]
                        P.stt(cr[0][:, 0:1], zr[:, T_ - 1:T_], CAc[:, cg:cg + 1], cr[0][:, 1:2], ALU.mult, ALU.subtract,
                              [("zz", u, 0), ("cart", si_, 0), ("S5R", js)], [("car", si_, 0)])
                        P.stt(cr[1][:, 0:1], zr[:, T_ - 1:T_], CAs[:, cg:cg + 1], cr[1][:, 1:2], ALU.mult, ALU.add,
                              [("zz", u, 0), ("cart", si_, 1), ("S5R", js)], [("car", si_, 1)])
                    stage1(oi + 1)
                for (dr, cg, tb, tk_, order, k4, si_) in streams:
                    u = si_
                    tt_ = order[oi]
                    Pu = PP[u]
                    pk = [("PP", u, i) for i in range(4)]
                    wi = dr * 4 + k4
                    bki = (tt_ * T_) // 512
                    yb = K.ps[bki]
                    y0 = (tt_ * T_) % 512
                    ys = yb[:, y0:y0 + T_]
                    for i in range(4):
                        st = bki not in bank_started
                        bank_started.add(bki)
                        wv_i = (2, 4, 3, 3)[i]
                        P.mm(ys, Wc[wb][:, wv_i, wi, :], Pu[:, i, :], st, False,
                             [("Wc", wb, wv_i), pk[i]], [("ps", bki)])
        for si, (bk, o0, n, bcol) in enumerate(segs):
            x0 = bk * 512 + o0
            p0 = 1 + x0 if x0 < LCTX else 2 + x0
            f = [K.tmpB[i][:, 0:n] for i in range(3)]
            fk = [("tmpB", i) for i in range(3)]
            P.tt("dve", f[0], K.X[:, c, x0:x0 + n], RSTD[:, x0:x0 + n], ALU.mult, [("X", c)] + RSTD_KEYS, [fk[0]])
            P.act(f[0], f[0], AF.Identity, [fk[0], ("MOD", l), ("A", l)], [fk[0]],
                  scale=K.A1[:, l, c, bcol:bcol + 1], bias=SH1[:, c, bcol:bcol + 1])
            P.stt(f[1], f[0], K.s5d[:, js, c:c + 1], K.ps[bk][:, o0:o0 + n], ALU.mult, ALU.add,
                  [fk[0], ("ps", bk), "consts"], [fk[1]])
            P.tt("pool", f[2], f[1], f[1], ALU.mult, [fk[1]], [fk[2]])
            P.ts("pool", f[2], f[2], 0.044715, 1.0, ALU.mult, ALU.add, [fk[2]], [fk[2]])
            P.tt("pool", f[2], f[2], f[1], ALU.mult, [fk[2], fk[1]], [fk[2]])
            P.act(f[2], f[2], AF.Sigmoid, [fk[2]], [fk[2]], scale=GELU_C)
            P.tt("dve", H[:, c, p0:p0 + n], f[1], f[2], ALU.mult, [fk[1], fk[2]], [("H", c)])
    ar.release()
    P.barrier()
    gw = [ar.alloc([2, NCH, 128], BF16) for _ in range(2)]
    sg = [ar.alloc([1, 512], F32)[:, 0, :] for _ in range(2)]
    gv = d["s5_glu_w"][js].rearrange("(kc p) n -> p kc n", p=128)
    n_ = 0
    for oc in range(NCH):
        wb = oc % 2
        P.dma("pool", gw[wb][:, 0, :, :], gv[:, :, oc * 128:(oc + 1) * 128], writes=[("gw", wb, 0)])
        P.dma("pool", gw[wb][:, 1, :, :], gv[:, :, D + oc * 128:D + (oc + 1) * 128], writes=[("gw", wb, 1)])
        for (x0, n, p0) in tok_blocks():
            u = n_ % 2
            n_ += 1
            for half in range(2):
                pst = K.ps[2 * half + u]
                for kc in range(NCH):
                    P.mm(pst[:, 0:n], gw[wb][:, half, kc, :], H[:, kc, p0:p0 + n], kc == 0, kc == NCH - 1,
                         [("gw", wb, half), ("H", kc)], [("ps", 2 * half + u)])
            P.act(sg[u][:, 0:n], K.ps[2 + u][:, 0:n], AF.Sigmoid, [("ps", 2 + u), "consts"], [("sg", u)],
                  bias=K.s5gb[:, js, 8 + oc:9 + oc])
            P.stt(sg[u][:, 0:n], K.ps[u][:, 0:n], K.s5gb[:, js, oc:oc + 1], sg[u][:, 0:n], ALU.add, ALU.mult,
                  [("ps", u), ("sg", u), "consts"], [("sg", u)])
            bcol = 4 if x0 == 0 else b
            P.stt(K.X[:, oc, x0:x0 + n], sg[u][:, 0:n], G1[:, oc, bcol:bcol + 1], K.X[:, oc, x0:x0 + n], ALU.mult, ALU.add,
                  [("sg", u), ("MOD", l), ("X", oc)], [("X", oc)])
    ar.release()
    P.barrier()


def emit_final(P, K, b):
    K.arena.mark()
    emit_rstd(P, K, "f")
    i = 0
    for bi, (c0, n, p0) in enumerate(tok_blocks()):
        if bi == 0:
            continue
        for c in range(NCH):
            tmp = K.tmpB[i % 3]
            tk = ("tmpB", i % 3)
            i += 1
            P.tt("dve", tmp[:, 0:n], K.X[:, c, c0:c0 + n], K.RSTD[:, c0:c0 + n], ALU.mult, [("X", c), ("rstd", bi)], [tk])
            P.act(tmp[:, 0:n], tmp[:, 0:n], AF.Identity, [tk, "consts"], [tk], scale=K.fg[:, c:c + 1])
            P.dma("sp", K.d["out"][b, c, :, c0 - LCTX:c0 - LCTX + n], tmp[:, 0:n], reads=[tk], writes=[("out", b, c, bi)])
    K.arena.release()
    P.barrier()


def build_program(NB=4, layers=(0, 1, 2, 3), mixers=True, final=True):
    nc = bass.Bass("TRN2", target_bir_lowering=False)
    d = {}

    def din(name, shape, dt=F32):
        d[name] = nc.dram_tensor(name, list(shape), dt, kind="ExternalInput").ap()

    din("xin", [NB, NCH, 128, L])
    din("cT", [128, NCH, 5])
    din("ada_w", [DEPTH, D, 6 * D])
    din("consts", [128, K_CONST_COLS])
    din("ffn_w_in", [DEPTH, D, 2 * DFF])
    din("ffn_w_out", [DEPTH, DFF, D])
    s5_js = sorted(set(l // 3 for l in layers if l % 3 == 0)) if mixers else []
    if s5_js:
        din("s5p1", [2, 128, 3, 64])
        din("s5p2", [2, 128, 5, 1024])
        din("s5c", [2, 128, 2, 1024])
        din("s5_glu_w", [2, D, 2 * D])
        d["s5t"] = nc.dram_tensor("s5t", [2, 128, 2, 64, S5_T], F32, kind="Internal").ap()
        d["s5wb"] = nc.dram_tensor("s5wb", [2, 128, 2, 2, S5_G2, 128], BF16, kind="Internal").ap()
        d["s5wc"] = nc.dram_tensor("s5wc", [2, 128, 3, 2, S5_G2, 128], BF16, kind="Internal").ap()
    if mixers and any(l % 3 == 1 for l in layers):
        din("ml_w_in", [1, D, 4 * D])
        din("ml_w_out", [1, D, D])
        din("ml_wif", [128, NCH, 16])
        for nm, shp in (("ml_sq", [ML_H, 128, 4, L]), ("ml_sv", [ML_H, 128, 18, 256]), ("ml_so", [ML_H, 128, 2, L])):
            d[nm] = nc.dram_tensor(nm, shp, BF16, kind="Internal").ap()
    if mixers and any(l % 3 == 2 for l in layers):
        din("na_w_qkv", [1, D, 3 * D])
        din("na_w_out", [1, D, D])
        din("na_bias", [1, NA_H, 128, NA_NT * 128])
    d["out"] = nc.dram_tensor("out", [NB, NCH, 128, LLAT], F32, kind="ExternalOutput").ap()

    with contextlib.ExitStack() as es:
        P = Prog(nc, es)
        K = Ctx()
        K.d = d
        K.NB = NB
        NBYTES = 204 * 1024
        big = es.enter_context(nc.sbuf_tensor("arena", [128, NBYTES // 4], F32))
        K.arena = Arena(big, NBYTES)
        ar = K.arena
        K.ps = [es.enter_context(nc.psum_tensor(f"ps{i}", [128, 512], F32)) for i in range(8)]
        K.X = ar.alloc([NCH, L], F32)
        cst = ar.alloc([1, K_CONST_COLS], F32)[:, 0, :]
        P.dma("sp", cst, d["consts"][:, :], writes=["consts"])
        o = 0

        def take(n, shape=None):
            nonlocal o
            v = cst[:, o:o + n]
            o += n
            return v
        K.ones = take(128)
        K.epsc = take(1)
        K.adab = take(DEPTH * 48).rearrange("p (l j) -> p l j", l=DEPTH)
        K.n1g = take(DEPTH * NCH).rearrange("p (l c) -> p l c", l=DEPTH)
        K.n2g = take(DEPTH * NCH).rearrange("p (l c) -> p l c", l=DEPTH)
        K.fg = take(NCH)
        K.cw = take(DEPTH * 44 * 3).rearrange("p (l c k) -> p l c k", l=DEPTH, c=44)
        K.onec = take(1)
        K.ident = take(128)
        K.tri = take(256).rearrange("p (d t) -> p d t", d=2)
        K.negm = take(256).rearrange("p (d t) -> p d t", d=2)
        K.mlbif = take(16)
        K.mlcw = take(48).rearrange("p (c k) -> p c k", k=3)
        K.mlng = take(NCH)
        K.ropeq = take(128).rearrange("p (t c) -> p t c", t=2)
        K.ropek = take(128).rearrange("p (t c) -> p t c", t=2)
        K.halfpi = take(1)
        K.mk8 = take(8)
        K.s5d = take(2 * NCH).rearrange("p (j c) -> p j c", j=2)
        K.s5gb = take(2 * 16).rearrange("p (j c) -> p j c", j=2)
        assert o == K_CONST_COLS, (o, K_CONST_COLS)
        K.MOD = ar.alloc([DEPTH, 48, 5], F32)
        K.A1 = ar.alloc([DEPTH, NCH, 5], F32)
        K.A2 = ar.alloc([DEPTH, NCH, 5], F32)
        K.tmpB = [ar.alloc([1, 512], F32)[:, 0, :] for _ in range(3)]
        K.sq = [ar.alloc([1, 512], F32)[:, 0, :] for _ in range(2)]

        K.S5R = ar.alloc([2, 3, 64], F32)
        emit_mods(P, K)
        for js in s5_js:
            emit_s5_gen(P, K, js)
        for b in range(NB):
            for c in range(NCH):
                P.dma("sp", K.X[:, c, :], d["xin"][b, c], writes=[("X", c)])
            for l in layers:
                if mixers:
                    if l % 3 == 2:
                        emit_na(P, K, l, b)
                    if l % 3 == 1:
                        emit_mlstm(P, K, l, b)
                    if l % 3 == 0:
                        emit_s5(P, K, l, b)
                emit_ffn(P, K, l, b)
            if final:
                emit_final(P, K, b)
            else:
                for c in range(NCH):
                    P.dma("sp", d["out"][b, c], K.X[:, c, LCTX:L], reads=[("X", c)], writes=[("out", b, c)])
                P.barrier()
        P.emit()
    nc._used_inputs = set(d.keys()) - {"out", "ml_sq", "ml_sv", "ml_so", "s5t", "s5wb", "s5wc"}
    return nc


K_CONST_COLS = 128 + 1 + DEPTH * 48 + DEPTH * NCH * 2 + NCH + DEPTH * 44 * 3 + 1 + 128 + 256 + 256 + 16 + 48 + NCH + 128 + 128 + 1 + 8 + 16 + 32


def make_consts(inp):
    cols = []
    cols.append(np.full((128, 128), 1.0 / D, np.float32))
    cols.append(np.full((128, 1), EPS, np.float32))
    ab = np.asarray(inp["ada_b"], np.float32).reshape(DEPTH, 48, 128).transpose(2, 0, 1).reshape(128, -1)
    cols.append(ab)
    for nm in ("norm1_g", "norm2_g"):
        g = np.asarray(inp[nm], np.float32).reshape(DEPTH, NCH, 128).transpose(2, 0, 1).reshape(128, -1)
        cols.append(g)
    cols.append(np.asarray(inp["final_g"], np.float32).reshape(NCH, 128).T)
    cw = np.asarray(inp["ffn_conv"], np.float32).reshape(DEPTH, 3, 44, 128).transpose(3, 0, 2, 1).reshape(128, -1)
    cols.append(cw)
    cols.append(np.ones((128, 1), np.float32))
    cols.append(np.eye(128, dtype=np.float32))
    si = np.arange(128)[:, None]
    ti = np.arange(128)[None, :]
    trif = (si <= ti).astype(np.float32)
    trib = (si >= ti).astype(np.float32)
    cols.append(np.concatenate([trif, trib], axis=1))
    cols.append(np.concatenate([(1 - trif) * NEGM, (1 - trib) * NEGM], axis=1).astype(np.float32))
    bif = np.asarray(inp["ml_b_if"], np.float32)[0].reshape(1, 16)
    cols.append(np.repeat(bif, 128, axis=0))
    mc_ = np.asarray(inp["ml_conv"], np.float32)[0]
    cwm = np.empty((128, 16, 3), np.float32)
    for qk in range(2):
        for hd in range(ML_H):
            base = qk * D + hd * ML_DH
            for ab in range(2):
                idx = np.concatenate([base + 64 * ab + np.arange(64), base + 128 + 64 * ab + np.arange(64)])
                cwm[:, (qk * ML_H + hd) * 2 + ab, :] = mc_[:, idx].T
    cols.append(cwm.reshape(128, 48))
    cols.append(np.asarray(inp["ml_norm_g"], np.float32)[0].reshape(NCH, 128).T)
    inv = (10000.0 ** (-np.arange(64, dtype=np.float32) / 64.0)).astype(np.float32)
    pos = np.arange(64, dtype=np.float32)
    ang = (pos[None, :] * inv[:, None]).astype(np.float32)
    tab = np.stack([np.cos(ang), np.sin(ang)], axis=1).astype(np.float32)
    tab = np.concatenate([tab, tab], axis=0)
    cols.append((tab / np.float32(16.0)).reshape(128, 128).astype(np.float32))
    cols.append(tab.reshape(128, 128))
    cols.append(np.full((128, 1), np.pi / 2, np.float32))
    cols.append((np.arange(128)[:, None] // 16 == np.arange(8)[None, :]).astype(np.float32))
    cols.append(np.asarray(inp["s5_d"], np.float32).reshape(2, NCH, 128).transpose(2, 0, 1).reshape(128, 16))
    cols.append(np.asarray(inp["s5_glu_b"], np.float32).reshape(2, 16, 128).transpose(2, 0, 1).reshape(128, 32))
    out = np.ascontiguousarray(np.concatenate(cols, axis=1), dtype=np.float32)
    assert out.shape[1] == K_CONST_COLS
    return out


def make_s5_params(inp):
    f = lambda k: np.asarray(inp[k], np.float32)
    lre, lim, ldt = f("s5_lam_re"), f("s5_lam_im"), f("s5_log_dt")
    bre, bim, cre, cim = f("s5_b_re"), f("s5_b_im"), f("s5_c_re"), f("s5_c_im")

    def lay1(a):
        a = a.reshape(2, 2, 32, 2, 64).transpose(0, 3, 4, 1, 2)
        return a.reshape(2, 128, 64)

    def lay2(a):
        a = a.reshape(2, 2, 8, 8, 64).transpose(0, 3, 1, 2, 4)
        a = np.broadcast_to(a[:, :, None], (2, 8, 16, 2, 8, 64))
        return a.reshape(2, 128, 1024)
    ldt4 = np.broadcast_to(ldt[..., None], (2, 2, 64, 64))
    p1 = np.stack([lay1(lre), lay1(lim), lay1(ldt4)], axis=2)
    b2 = []
    for bb in (bre, bim):
        a = bb.reshape(2, 2, 8, 8, 64, 16).transpose(0, 3, 5, 1, 2, 4)
        b2.append(a.reshape(2, 128, 1024))
    p2 = np.stack([lay2(lre), lay2(lim), lay2(ldt4), b2[0], b2[1]], axis=2)
    cc = []
    for c_ in (cre, cim):
        a = c_.reshape(2, 2, 32, 2, 16, 64).transpose(0, 3, 5, 1, 2, 4)
        cc.append(a.reshape(2, 128, 1024))
    s5c = np.stack(cc, axis=2)
    return (np.ascontiguousarray(p1, dtype=np.float32), np.ascontiguousarray(p2, dtype=np.float32),
            np.ascontiguousarray(s5c, dtype=np.float32))


def make_in_maps(inp, NB, ncores):
    x = np.asarray(inp["x"], np.float32)
    ctx = np.asarray(inp["ctx"], np.float32)
    c = np.asarray(inp["c"], np.float32)
    c_ctx = np.asarray(inp["c_ctx"], np.float32)
    consts = make_consts(inp)
    s5p1, s5p2, s5c = make_s5_params(inp)
    shared = {
        "s5p1": s5p1, "s5p2": s5p2, "s5c": s5c,
        "s5_glu_w": np.ascontiguousarray(inp["s5_glu_w"], dtype=np.float32),
        "ada_w": np.ascontiguousarray(inp["ada_w"], dtype=np.float32),
        "consts": consts,
        "ffn_w_in": np.ascontiguousarray(inp["ffn_w_in"], dtype=np.float32),
        "ffn_w_out": np.ascontiguousarray(inp["ffn_w_out"], dtype=np.float32),
        "ml_w_in": np.ascontiguousarray(inp["ml_w_in"], dtype=np.float32),
        "ml_w_out": np.ascontiguousarray(inp["ml_w_out"], dtype=np.float32),
        "ml_wif": np.ascontiguousarray(np.asarray(inp["ml_w_if"], np.float32)[0].transpose(1, 0, 2).reshape(NCH, 128, 16).transpose(1, 0, 2)),
        "na_w_qkv": np.ascontiguousarray(inp["na_w_qkv"], dtype=np.float32),
        "na_w_out": np.ascontiguousarray(inp["na_w_out"], dtype=np.float32),
        "na_bias": make_na_bias(np.asarray(inp["na_rpb"])[0])[None],
    }
    maps = []
    for core in range(ncores):
        b0 = core * NB
        xin = np.empty((NB, D, L), np.float32)
        for i in range(NB):
            xin[i, :, :LCTX] = ctx[b0 + i].T
            xin[i, :, LCTX:] = x[b0 + i].T
        cc = np.concatenate([c[b0:b0 + NB], np.zeros((4 - NB, D), np.float32), c_ctx[None]], axis=0)
        cT = np.ascontiguousarray(cc.reshape(5, NCH, 128).transpose(2, 1, 0))
        m = dict(shared)
        m["xin"] = xin.reshape(NB, NCH, 128, L)
        m["cT"] = cT
        maps.append(m)
    return maps


def gather_out(res, NB, ncores):
    outs = []
    for core in range(ncores):
        o = np.asarray(res.results[core]["out"]).reshape(NB, D, LLAT)
        outs.append(np.ascontiguousarray(o.transpose(0, 2, 1)))
    return np.concatenate(outs, axis=0)


def nc_inputs(nc):
    return getattr(nc, "_used_inputs")


def kernel(**inputs):
    NB = 4
    nc = build_program(NB=NB)
    maps = make_in_maps(inputs, NB, NCORES)
    maps = [{k: v for k, v in m.items() if k in nc_inputs(nc)} for m in maps]
    res = run_bass_kernel_spmd(nc, maps, core_ids=list(range(NCORES)))
    return gather_out(res, NB, NCORES).astype(np.float32)
```
